# Optimizing a Trainium2 kernel written in Bass

```python
import jax
import jax.numpy as jnp
from jax import lax
import numpy as np


D_MODEL = 2048
BATCH = 1
SEQ = 8192
DEPTH = 1
DEC_BATCH = 32
DEC_SEQ = 16
PAST_LEN = 1024

CHUNK = 64
Q_BLOCK = 128
N_HEADS = 8
QK_NOPE = 128
QK_ROPE = 64
V_HEAD = 128
Q_LORA = 512
KV_LORA = 512
ROPE_THETA = 10000.0
ATTN_SCALE = (QK_NOPE + QK_ROPE) ** -0.5
CONV_WIDTH = 1024
CONV_K = 3
PEER_HEADS = 8
PEER_QDIM = 256
PEER_HALF = PEER_QDIM // 2
N_KEYS = 128
N_EXPERTS = N_KEYS * N_KEYS
PEER_TOPK = 16
PEER_BLOCK = 128
N_BRANCH = 2
EPS = 1e-6
IN_COLS = Q_LORA + KV_LORA + QK_ROPE + 3 * CONV_WIDTH + N_BRANCH * D_MODEL
IN_SPLITS = [Q_LORA, Q_LORA + KV_LORA, Q_LORA + KV_LORA + QK_ROPE,
             Q_LORA + KV_LORA + QK_ROPE + CONV_WIDTH,
             Q_LORA + KV_LORA + QK_ROPE + 2 * CONV_WIDTH,
             Q_LORA + KV_LORA + QK_ROPE + 3 * CONV_WIDTH]

kernel_name = 'streaming_mla_shortconv_peer'


def rms_norm(x, g):
    xf = x.astype(jnp.float32)
    y = xf * lax.rsqrt(jnp.mean(xf * xf, axis=-1, keepdims=True) + EPS)
    return y.astype(x.dtype) * g


def modulate(h, shift, scale):
    return h * (1 + scale[:, None, :]) + shift[:, None, :]


def rope(x, pos):
    half = x.shape[-1] // 2
    inv = 1.0 / (ROPE_THETA ** (jnp.arange(half, dtype=jnp.float32) / half))
    ang = pos.astype(jnp.float32)[:, None] * inv[None, :]
    ang = ang.reshape((ang.shape[0],) + (1,) * (x.ndim - 3) + (half,))
    cos = jnp.cos(ang).astype(x.dtype)
    sin = jnp.sin(ang).astype(x.dtype)
    x1, x2 = x[..., :half], x[..., half:]
    return jnp.concatenate([x1 * cos - x2 * sin, x1 * sin + x2 * cos], axis=-1)


def mla_attend(q_nope, q_rope, k_nope, k_rope, v, q_pos, k_pos):
    s = (jnp.einsum('bqhd,bkhd->bhqk', q_nope, k_nope)
         + jnp.einsum('bqhd,bkd->bhqk', q_rope, k_rope))
    s = s.astype(jnp.float32) * ATTN_SCALE
    mask = (k_pos // CHUNK)[None, :] <= (q_pos // CHUNK)[:, None]
    s = jnp.where(mask[None, None], s, jnp.finfo(jnp.float32).min)
    p = jax.nn.softmax(s, axis=-1).astype(v.dtype)
    return jnp.einsum('bhqk,bkhd->bqhd', p, v)


def peer_tokens(t, w_pq, sub_k1, sub_k2, w_u, w_v):
    n = t.shape[0]
    q = (t @ w_pq).reshape(n, PEER_HEADS, PEER_QDIM)
    s1 = jnp.einsum('thd,hnd->thn', q[..., :PEER_HALF], sub_k1)
    s2 = jnp.einsum('thd,hnd->thn', q[..., PEER_HALF:], sub_k2)
    v1, i1 = lax.top_k(s1, PEER_TOPK)
    v2, i2 = lax.top_k(s2, PEER_TOPK)
    cand = (v1[..., :, None] + v2[..., None, :]).reshape(n, PEER_HEADS, PEER_TOPK * PEER_TOPK)
    cidx = (i1[..., :, None] * N_KEYS + i2[..., None, :]).reshape(n, PEER_HEADS, PEER_TOPK * PEER_TOPK)
    sv, sel = lax.top_k(cand, PEER_TOPK)
    eidx = jnp.take_along_axis(cidx, sel, axis=-1)
    gate = jax.nn.softmax(sv.astype(jnp.float32), axis=-1).astype(t.dtype)
    act = jax.nn.gelu(jnp.einsum('thkd,td->thk', w_u[eidx], t), approximate=False)
    return jnp.einsum('thk,thkd->td', gate * act, w_v[eidx])


def peer(h, w_pq, sub_k1, sub_k2, w_u, w_v):
    b, s, d = h.shape
    n = b * s
    nb = -(-n // PEER_BLOCK)
    flat = jnp.pad(h.reshape(n, d), ((0, nb * PEER_BLOCK - n), (0, 0)))
    out = lax.map(lambda t: peer_tokens(t, w_pq, sub_k1, sub_k2, w_u, w_v),
                  flat.reshape(nb, PEER_BLOCK, d))
    return out.reshape(nb * PEER_BLOCK, d)[:n].reshape(b, s, d)


def trunk_layer(x, c, pos, past_ckv, past_krope, conv_left,
                w_ada, b_ada, g_n1, w_in, g_q, g_kv, w_uq, w_uk, w_uv, w_oa,
                w_conv, b_conv, w_ob, w_o, g_n2, w_pq, sub_k1, sub_k2, w_u, w_v):
    b, s, d = x.shape
    mod = c @ w_ada + b_ada
    sh1, sc1, gt1, sh2, sc2, gt2 = jnp.split(mod, 6, axis=-1)
    h = modulate(rms_norm(x, g_n1), sh1, sc1)
    p = h @ w_in
    p_q, p_kv, p_kr, p_h, p_b, p_c, p_g = jnp.split(p, IN_SPLITS, axis=-1)

    c_q = rms_norm(p_q, g_q)
    q = jnp.einsum('bsr,rhd->bshd', c_q, w_uq)
    q_nope = q[..., :QK_NOPE]
    q_rope = rope(q[..., QK_NOPE:], pos)
    ckv = rms_norm(p_kv, g_kv)
    krope = rope(p_kr, pos)
    if past_ckv is None:
        keys_ckv, keys_kr, k_pos = ckv, krope, pos
    else:
        keys_ckv = jnp.concatenate([past_ckv, ckv], axis=1)
        keys_kr = jnp.concatenate([past_krope, krope], axis=1)
        k_pos = jnp.arange(keys_ckv.shape[1])
    k_nope = jnp.einsum('bkr,rhd->bkhd', keys_ckv, w_uk)
    v = jnp.einsum('bkr,rhd->bkhd', keys_ckv, w_uv)
    if s > Q_BLOCK and s % Q_BLOCK == 0:
        nb = s // Q_BLOCK
        qn = q_nope.reshape(b, nb, Q_BLOCK, N_HEADS, QK_NOPE).swapaxes(0, 1)
        qr = q_rope.reshape(b, nb, Q_BLOCK, N_HEADS, QK_ROPE).swapaxes(0, 1)
        qp = pos.reshape(nb, Q_BLOCK)
        o = lax.map(lambda a: mla_attend(a[0], a[1], k_nope, keys_kr, v, a[2], k_pos), (qn, qr, qp))
        o = o.swapaxes(0, 1).reshape(b, s, N_HEADS * V_HEAD)
    else:
        o = mla_attend(q_nope, q_rope, k_nope, keys_kr, v, pos, k_pos).reshape(b, s, N_HEADS * V_HEAD)
    branch_a = o @ w_oa

    z = p_c * p_h
    zpad = jnp.concatenate([conv_left, z], axis=1)
    yc = (w_conv[0] * zpad[:, 0:s] + w_conv[1] * zpad[:, 1:s + 1]
          + w_conv[2] * zpad[:, 2:s + 2] + b_conv)
    branch_b = (p_b * yc) @ w_ob
    new_conv = zpad[:, zpad.shape[1] - (CONV_K - 1):]

    g = jax.nn.sigmoid(p_g).reshape(b, s, N_BRANCH, d)
    merged = g[:, :, 0] * branch_a + g[:, :, 1] * branch_b
    x = x + gt1[:, None, :] * (merged @ w_o)

    h2 = modulate(rms_norm(x, g_n2), sh2, sc2)
    x = x + gt2[:, None, :] * peer(h2, w_pq, sub_k1, sub_k2, w_u, w_v)
    return x, ckv, krope, new_conv


def setup_inputs(seed: int = 0):
    key = jax.random.key(seed)
    ks = jax.random.split(key, 32)
    f32 = jnp.float32

    def nrm(k, shape, scale):
        return jax.random.normal(k, shape, f32) * scale

    def gain(k, shape):
        return 1.0 + 0.01 * jax.random.normal(k, shape, f32)

    L, D, W = DEPTH, D_MODEL, CONV_WIDTH
    return {
        'x_prompt': nrm(ks[0], (BATCH, SEQ, D), 1.0),
        'x_sample': nrm(ks[1], (DEC_BATCH, DEC_SEQ, D), 1.0),
        'cache_ckv': nrm(ks[2], (L, DEC_BATCH, PAST_LEN, KV_LORA), 1.0),
        'cache_krope': nrm(ks[3], (L, DEC_BATCH, PAST_LEN, QK_ROPE), 1.0),
        'state_conv': nrm(ks[4], (L, DEC_BATCH, CONV_K - 1, W), 0.5),
        'c_prompt': nrm(ks[5], (BATCH, D), 1.0),
        'c_sample': nrm(ks[6], (DEC_BATCH, D), 1.0),
        'w_ada': nrm(ks[7], (L, D, 6 * D), 0.2 * D ** -0.5),
        'b_ada': nrm(ks[8], (L, 6 * D), 0.01),
        'g_n1': gain(ks[9], (L, D)),
        'w_in': nrm(ks[10], (L, D, IN_COLS), D ** -0.5),
        'g_q': gain(ks[11], (L, Q_LORA)),
        'g_kv': gain(ks[12], (L, KV_LORA)),
        'w_uq': nrm(ks[13], (L, Q_LORA, N_HEADS, QK_NOPE + QK_ROPE), Q_LORA ** -0.5),
        'w_uk': nrm(ks[14], (L, KV_LORA, N_HEADS, QK_NOPE), KV_LORA ** -0.5),
        'w_uv': nrm(ks[15], (L, KV_LORA, N_HEADS, V_HEAD), KV_LORA ** -0.5),
        'w_oa': nrm(ks[16], (L, N_HEADS * V_HEAD, D), (N_HEADS * V_HEAD) ** -0.5),
        'w_conv': nrm(ks[17], (L, CONV_K, W), CONV_K ** -0.5),
        'b_conv': nrm(ks[18], (L, W), 0.01),
        'w_ob': nrm(ks[19], (L, W, D), W ** -0.5),
        'w_o': nrm(ks[20], (L, D, D), D ** -0.5),
        'g_n2': gain(ks[21], (L, D)),
        'w_pq': nrm(ks[22], (L, D, PEER_HEADS * PEER_QDIM), D ** -0.5),
        'sub_k1': nrm(ks[23], (L, PEER_HEADS, N_KEYS, PEER_HALF), PEER_HALF ** -0.5),
        'sub_k2': nrm(ks[24], (L, PEER_HEADS, N_KEYS, PEER_HALF), PEER_HALF ** -0.5),
        'w_u': nrm(ks[25], (L, N_EXPERTS, D), D ** -0.5),
        'w_v': nrm(ks[26], (L, N_EXPERTS, D), PEER_HEADS ** -0.5),
        'g_f': gain(ks[27], (D,)),
    }


def reference(x_prompt, x_sample, cache_ckv, cache_krope, state_conv, c_prompt, c_sample,
              w_ada, b_ada, g_n1, w_in, g_q, g_kv, w_uq, w_uk, w_uv, w_oa,
              w_conv, b_conv, w_ob, w_o, g_n2, w_pq, sub_k1, sub_k2, w_u, w_v, g_f):
    past = cache_ckv.shape[2]
    pos_p = jnp.arange(x_prompt.shape[1])
    pos_s = past + jnp.arange(x_sample.shape[1])
    xp, xs = x_prompt, x_sample
    left_p = jnp.zeros((xp.shape[0], CONV_K - 1, CONV_WIDTH), xp.dtype)
    ckv_p, kr_p, conv_p, ckv_s, kr_s, conv_s = [], [], [], [], [], []
    for l in range(DEPTH):
        wl = (w_ada[l], b_ada[l], g_n1[l], w_in[l], g_q[l], g_kv[l], w_uq[l], w_uk[l], w_uv[l],
              w_oa[l], w_conv[l], b_conv[l], w_ob[l], w_o[l], g_n2[l], w_pq[l], sub_k1[l],
              sub_k2[l], w_u[l], w_v[l])
        xp, a1, a2, a3 = trunk_layer(xp, c_prompt, pos_p, None, None, left_p, *wl)
        xs, b1, b2, b3 = trunk_layer(xs, c_sample, pos_s, cache_ckv[l], cache_krope[l], state_conv[l], *wl)
        ckv_p.append(a1); kr_p.append(a2); conv_p.append(a3)
        ckv_s.append(b1); kr_s.append(b2); conv_s.append(b3)
    y_prompt = rms_norm(xp, g_f)
    y_sample = rms_norm(xs, g_f)
    new_ckv_p = jnp.stack(ckv_p)
    new_kr_p = jnp.stack(kr_p)
    new_conv_p = jnp.stack(conv_p)
    new_ckv_s = jnp.stack(ckv_s)
    new_kr_s = jnp.stack(kr_s)
    new_conv_s = jnp.stack(conv_s)
    return (y_prompt, y_sample, new_ckv_p, new_kr_p, new_conv_p, new_ckv_s, new_kr_s, new_conv_s)
```

```python
import numpy as np
import concourse.bass as bass
import concourse.mybir as mybir
from concourse.bass_utils import run_bass_kernel_spmd
from contextlib import ExitStack

F32 = mybir.dt.float32
BF16 = mybir.dt.bfloat16
U32 = mybir.dt.uint32
I32 = mybir.dt.int32
AF = mybir.ActivationFunctionType
ALU = mybir.AluOpType
AX = mybir.AxisListType

NCORES = 8
D = 2048
NT = 1120
NO = 1088
EPS = 1e-6
SCALE = 192.0 ** -0.5
NEG = -30000.0
BLK_T = [(0, 512), (512, 512), (1024, 96)]
BLK_O = [(0, 512), (512, 512), (1024, 64)]
TILES_O = [(i * 128, 128) for i in range(8)] + [(1024, 64)]


class Tl:
    __slots__ = ("t", "w", "r", "dsem", "ssem", "name")

    def __init__(self, t, name):
        self.t = t
        self.name = name
        self.w = None
        self.r = []
        self.dsem = None
        self.ssem = None

    def __getitem__(self, k):
        return self.t[k]


class DSem:
    def __init__(self, sem):
        self.sem = sem
        self.issued = 0


class Eng:
    def __init__(self, name, h, sem):
        self.name = name
        self.h = h
        self.sem = sem
        self.count = 0
        self.waited = {}


class Ring:
    def __init__(self, tiles):
        self.tiles = tiles
        self.i = 0

    def next(self):
        t = self.tiles[self.i % len(self.tiles)]
        self.i += 1
        return t


class K:
    def __init__(self, nc, es):
        self.nc = nc
        self.es = es
        self.eng = {}
        for name, h in (("pe", nc.tensor), ("act", nc.scalar), ("dve", nc.vector),
                        ("pool", nc.gpsimd), ("sp", nc.sync)):
            sem = es.enter_context(nc.semaphore("prog_" + name))
            self.eng[name] = Eng(name, h, sem)
        self.dsems = []
        self.store_sems = []
        self.ninstr = 0
        self.uid = 0
        import os
        self.limit = int(os.environ.get("KLIMIT", "100000000"))

    def sb(self, es, name, shape, dt):
        self.uid += 1
        nm = "%s_%d" % (name, self.uid)
        return Tl(es.enter_context(self.nc.sbuf_tensor(nm, shape, dt)), nm)

    def ps(self, es, name, shape, dt):
        self.uid += 1
        nm = "%s_%d" % (name, self.uid)
        return Tl(es.enter_context(self.nc.psum_tensor(nm, shape, dt)), nm)

    def ring(self, es, name, shape, dt, n, psum=False):
        f = self.ps if psum else self.sb
        return Ring([f(es, name, shape, dt) for _ in range(n)])

    def dram(self, name, shape, dt):
        t = self.nc.dram_tensor(name, shape, dt, kind="Internal").ap()
        return Tl(t, name)

    def newsem(self, name):
        self.uid += 1
        ds = DSem(self.es.enter_context(self.nc.semaphore("%s_%d" % (name[:20], self.uid))))
        self.dsems.append(ds)
        return ds

    def getsem(self, name):
        if not hasattr(self, "sem_pool"):
            self.sem_pool = []
            self.sem_rr = 0
        if len(self.sem_pool) < 72:
            self.sem_pool.append(self.newsem(name))
            return self.sem_pool[-1]
        self.sem_rr += 1
        return self.sem_pool[self.sem_rr % len(self.sem_pool)]

    def _wait(self, E, tok):
        if tok is None:
            return
        if tok[0] == "e":
            _, P, val = tok
            if P is E and E.name in ("pe", "sp"):
                return
            sem = P.sem
        else:
            _, ds, val = tok
            sem = ds.sem
            val = max(val, ds.issued)
        key = id(sem)
        if E.waited.get(key, 0) >= val:
            return
        E.waited[key] = val
        E.h.wait_ge(sem, val)

    def _deps(self, E, r, w):
        for t in r:
            self._wait(E, t.w)
        for t in w:
            self._wait(E, t.w)
            for tok in t.r:
                self._wait(E, tok)

    def do(self, en, fn, r=(), w=(), inc=True, nowaw=False):
        if self.ninstr >= self.limit:
            return None
        E = self.eng[en]
        if nowaw:
            for t in w:
                assert t.w is None or t.w[0] != "e" or t.w[1] is E or not t.r or True
            self._deps(E, r, ())
            for t in w:
                for tok in t.r:
                    self._wait(E, tok)
                if t.w is not None and not (t.w[0] == "e" and t.w[1] is E):
                    self._wait(E, t.w)
        else:
            self._deps(E, r, w)
        ins = fn(E.h)
        self.ninstr += 1
        if inc:
            E.count += 1
            ins.then_inc(E.sem, 1)
            tok = ("e", E, E.count)
        else:
            tok = ("e", E, E.count + 1)
        for t in r:
            t.r.append(tok)
        for t in w:
            t.w = tok
            t.r = []
        return tok

    def dma(self, q, out, in_, r=(), w=(), store=False, scratch=None, **kw):
        if self.ninstr >= self.limit:
            return None
        E = self.eng[q]
        if scratch is not None:
            self._deps(E, r, ())
            if scratch.dsem is None:
                scratch.dsem = self.newsem("sc_" + scratch.name)
            ds = scratch.dsem
        elif store:
            self._deps(E, r, ())
            src = r[0]
            if src.ssem is None:
                src.ssem = self.newsem("st_" + src.name)
                self.store_sems.append(src.ssem)
            ds = src.ssem
        else:
            self._deps(E, r, w)
            dst = w[0]
            if dst.dsem is None:
                dst.dsem = self.getsem("ld_" + dst.name)
            ds = dst.dsem
        ins = E.h.dma_start(out=out, in_=in_, **kw)
        self.ninstr += 1
        ds.issued += 16
        ins.then_inc(ds.sem, 16)
        tok = ("d", ds, ds.issued)
        for t in r:
            t.r.append(tok)
        for t in w:
            t.w = tok
            t.r = []
        if scratch is not None:
            scratch.w = tok
        return tok

    def mark(self, name):
        import os
        if os.environ.get("KVERBOSE"):
            print("MARK", name, self.ninstr, flush=True)

    def barrier(self):
        for E in self.eng.values():
            for P in self.eng.values():
                if P is E or P.name == "sp" or P.count == 0:
                    continue
                if E.waited.get(id(P.sem), 0) < P.count:
                    E.waited[id(P.sem)] = P.count
                    E.h.wait_ge(P.sem, P.count)
            for ds in self.dsems:
                if ds.issued and E.waited.get(id(ds.sem), 0) < ds.issued:
                    E.waited[id(ds.sem)] = ds.issued
                    E.h.wait_ge(ds.sem, ds.issued)

    def finish(self):
        E = self.eng["sp"]
        for ds in self.store_sems:
            E.h.wait_ge(ds.sem, ds.issued)


def bc(ap, shape):
    return ap.broadcast_to(shape)


STAGES = ["p1b", "p1", "kp", "att", "mrg", "all"]


def build(stop_after="all", dbg=False):
    nc = bass.Bass("TRN2", target_bir_lowering=False)
    in_names = []
    nc_in_names = in_names

    def need(stage):
        return STAGES.index(stop_after) >= STAGES.index(stage)

    BIG = {"xall": "kp", "w_u": "all", "w_v": "all", "w_pq": "all", "w_o": "mrg", "w_oa": "mrg"}

    def din(name, shape, dt=F32):
        if name in BIG and not need(BIG[name]):
            return None
        in_names.append(name)
        return nc.dram_tensor(name, shape, dt, kind="ExternalInput").ap()

    def dout(name, shape, dt=F32):
        return nc.dram_tensor(name, shape, dt, kind="ExternalOutput").ap()

    xown = din("xown", [NT, D])
    xall = din("xall", [8192, D])
    c5T = din("c5T", [128, 16, 5])
    w_ada = din("w_ada", [D, 6 * D])
    b_adaT = din("b_adaT", [128, 96])
    b_ada = din("b_ada", [1, 6 * D])
    g_n1T = din("g_n1T", [128, 16])
    g_n2T = din("g_n2T", [128, 16])
    g_qT = din("g_qT", [128, 4])
    g_kvT = din("g_kvT", [128, 4])
    w_in = din("w_in", [D, 8256])
    w_uq = din("w_uq", [512, 1536])
    w_uk = din("w_uk", [512, 1024])
    w_uv = din("w_uv", [512, 1024])
    w_oa = din("w_oa", [1024, D])
    w_ob = din("w_ob", [1024, D])
    w_o = din("w_o", [D, D])
    w_pq = din("w_pq", [D, D])
    w_convT = din("w_convT", [128, 8, 3])
    b_convT = din("b_convT", [128, 8])
    sub_k1 = din("sub_k1", [1024, 128])
    sub_k2 = din("sub_k2", [1024, 128])
    w_u = din("w_u", [16384, D])
    w_v = din("w_v", [16384, D])
    g_f = din("g_f", [1, D])
    cckv = din("cckv", [4, 1024, 512])
    ckr = din("ckr", [4, 1024, 64])
    sconvT = din("sconvT", [128, 8, 4, 2])
    cosq = din("cosq", [64, NT])
    sinq = din("sinq", [64, NT])
    cosk = din("cosk", [64, 8192])
    sink = din("sink", [64, 8192])
    dmask = din("dmask", [128, 4])
    hvalid = din("hvalid", [128, 32])

    o_y = dout("o_y", [NO, D])
    o_ckv = dout("o_ckv", [NO, 512])
    o_kr = dout("o_kr", [NO, 64])
    o_conv = dout("o_conv", [10, 1024])

    with ExitStack() as es:
        k = K(nc, es)
        modrows = k.dram("modrows", [5, 2 * D], F32)
        g0_d = k.dram("g0_d", [16, 128, NO], BF16)
        gb_d = k.dram("gb_d", [16, 128, NO], BF16)
        KT_d = k.dram("KT_d", [8, 16, 128, 512], BF16)
        V_d = k.dram("V_d", [8, 16, 128, 512], BF16)
        KTs_d = k.dram("KTs_d", [4, 8, 128, 1040], BF16)
        Vs_d = k.dram("Vs_d", [4, 8, 128, 9 * 128], BF16)
        x1_d = k.dram("x1_d", [NO, D], F32)
        G_d = k.dram("G_d", [128, 128, NO], BF16)
        cq_d = k.dram("cq_d", [128, 4, NO], BF16)
        krT_d = k.dram("krT_d", [64, 8192], BF16)
        krc_d = k.dram("krc_d", [64, 4, 1040], BF16)
        oT_d = k.dram("oT_d", [128, 8, NO], BF16)

        identf = k.sb(es, "identf", [128, 128], F32)
        ident = k.sb(es, "ident", [128, 128], BF16)
        ones = k.sb(es, "ones", [128, 128], BF16)
        k.do("pool", lambda e: e.memset(identf[:, :], 0.0), w=[identf])
        k.do("pool", lambda e: e.affine_select(out=identf[:, :], in_=identf[:, :], pattern=[[-1, 128]],
                                               compare_op=ALU.not_equal, fill=1.0, base=0, channel_multiplier=1),
             r=[identf], w=[identf])
        k.do("dve", lambda e: e.tensor_copy(out=ident[:, :], in_=identf[:, :]), r=[identf], w=[ident])
        k.do("dve", lambda e: e.memset(ones[:, :], 1.0), w=[ones])

        def ld(es_, name, shape, src, dt=F32, q="sp"):
            t = k.sb(es_, name, shape, dt)
            k.dma(q, t.t[tuple(slice(None) for _ in shape)], src, w=[t])
            return t

        c5f = ld(es, "c5f", [128, 16, 5], c5T[:, :, :])
        c5b = k.sb(es, "c5b", [128, 16, 5], BF16)
        k.do("dve", lambda e: e.tensor_copy(out=c5b[:, :, :], in_=c5f[:, :, :]), r=[c5f], w=[c5b])
        badT = ld(es, "badT", [128, 96], b_adaT[:, :])
        gn1 = ld(es, "gn1", [128, 16], g_n1T[:, :])
        gn2 = ld(es, "gn2", [128, 16], g_n2T[:, :])
        gq = ld(es, "gq", [128, 4], g_qT[:, :])
        gkv = ld(es, "gkv", [128, 4], g_kvT[:, :])
        modT = k.sb(es, "modT", [128, 64, 5], F32)
        A1 = k.sb(es, "A1", [128, 16, 5], F32)
        A2 = k.sb(es, "A2", [128, 16, 5], F32)

        with ExitStack() as pa:
            wpan = k.ring(pa, "adapan", [128, 16, 1024], BF16, 2)
            import os as _os
            ADA_HW = bool(int(_os.environ.get("KA_HW", "0")))
            if ADA_HW:
                wpan32 = k.ring(pa, "adapan32", [128, 16, 1024], F32, 2)
            psf = k.ring(pa, "psf", [128, 4, 8], F32, 2, psum=True)
            psr = k.ring(pa, "psr", [5, 512], F32, 2, psum=True)
            b5 = k.ring(pa, "b5", [5, 512], F32, 2)
            rowst = k.ring(pa, "rowst", [5, 512], F32, 2)
            fm_panels = {0: 0, 1: 4, 2: 8, 3: 12, 4: 16, 5: 20, 6: 24, 7: 28,
                         12: 32, 13: 36, 14: 40, 15: 44, 16: 48, 17: 52, 18: 56, 19: 60}
            for pi in range(24):
                if pi % 2 == 0:
                    wp_full = wpan.next()
                    if ADA_HW:
                        wf = wpan32.next()
                        srcv = w_ada[:, pi * 512:(pi + 2) * 512].rearrange("(kc p) n -> p kc n", p=128)
                        k.dma("sp", wf[:, 0:8, :], srcv[:, 0:8, :], w=[wf])
                        k.dma("act", wf[:, 8:16, :], srcv[:, 8:16, :], w=[wf])
                        k.do("act", lambda e: e.copy(out=wp_full[:, 0:5, :], in_=wf[:, 0:5, :]), r=[wf], w=[wp_full])
                        k.do("dve", lambda e: e.tensor_copy(out=wp_full[:, 5:11, :], in_=wf[:, 5:11, :]), r=[wf], w=[wp_full], nowaw=True)
                        k.do("pool", lambda e: e.tensor_copy(out=wp_full[:, 11:16, :], in_=wf[:, 11:16, :]), r=[wf], w=[wp_full], nowaw=True)
                    else:
                        k.dma("pool", wp_full[:, :, :], w_ada[:, pi * 512:(pi + 2) * 512].rearrange("(kc p) n -> p kc n", p=128), w=[wp_full])

                class _V:
                    pass
                wp = wp_full
                wo_ = (pi % 2) * 512
                if pi in fm_panels:
                    base = fm_panels[pi]
                    p = psf.next()
                    for m in range(4):
                        for kc in range(16):
                            k.do("pe", lambda e: e.matmul(out=p[:, m, 0:5], lhsT=wp[:, kc, wo_ + m * 128:wo_ + (m + 1) * 128],
                                                          rhs=c5b[:, kc, :], start=(kc == 0), stop=(kc == 15)),
                                 r=[wp, c5b], w=[p], inc=(kc == 15 and m == 3))
                    for m in range(4):
                        cc = pi * 4 + m
                        k.do("dve", lambda e: e.tensor_scalar(out=modT[:, base + m, :], in0=p[:, m, 0:5],
                                                              scalar1=badT[:, cc:cc + 1], scalar2=None, op0=ALU.add),
                             r=[p, badT], w=[modT])
                else:
                    p = psr.next()
                    for kc in range(16):
                        k.do("pe", lambda e: e.matmul(out=p[:, :], lhsT=c5b[:, kc, :], rhs=wp[:, kc, wo_:wo_ + 512],
                                                      start=(kc == 0), stop=(kc == 15)),
                             r=[wp, c5b], w=[p], inc=(kc == 15))
                    bt = b5.next()
                    k.dma("sp", bt[:, :], bc(b_ada[0:1, pi * 512:(pi + 1) * 512], [5, 512]), w=[bt])
                    rs = rowst.next()
                    k.do("dve", lambda e: e.tensor_tensor(out=rs[:, :], in0=p[:, :], in1=bt[:, :], op=ALU.add),
                         r=[p, bt], w=[rs])
                    co = (pi - 8) * 512 if pi < 12 else D + (pi - 20) * 512
                    k.dma("sp", modrows.t[:, co:co + 512], rs[:, :], r=[rs], scratch=modrows)
            for (A, g, o) in ((A1, gn1, 16), (A2, gn2, 48)):
                k.do("dve", lambda e: e.tensor_scalar(out=A[:, :, :], in0=modT[:, o:o + 16, :], scalar1=1.0, scalar2=None,
                                                      op0=ALU.add), r=[modT], w=[A])
                k.do("dve", lambda e: e.tensor_tensor(out=A[:, :, :], in0=A[:, :, :],
                                                      in1=bc(g[:, :].unsqueeze(2), [128, 16, 5]), op=ALU.mult),
                     r=[A, g], w=[A])
            k.barrier()

        def make_front(fes, nx=3):
            fr = {}
            fr["x"] = k.ring(fes, "xt", [128, D], F32, nx)
            fr["xn"] = k.ring(fes, "xn", [128, D], BF16, 2)
            fr["junk"] = k.sb(fes, "junk", [128, D], BF16)
            fr["ss"] = k.ring(fes, "ss", [128, 1], F32, 4)
            fr["sd"] = k.ring(fes, "sd", [128, 1], F32, 4)
            fr["rs"] = k.ring(fes, "rs", [128, 1], F32, 4)
            fr["pt"] = k.ring(fes, "ptr", [128, 4, 128], BF16, 3, psum=True)
            fr["n"] = 0
            fr["pend"] = None
            return fr

        def front_from_sb(fr, xt, ntok, A, Bt, Bo, groups, hT, col0):
            ss = fr["ss"].next(); sd = fr["sd"].next(); rs = fr["rs"].next()
            junk = fr["junk"]
            k.do("act", lambda e: e.activation(out=junk[:ntok, :], in_=xt[:ntok, :], func=AF.Square,
                                               accum_out=ss[:ntok, 0:1]), r=[xt], w=[junk, ss])
            k.do("act", lambda e: e.activation(out=sd[:ntok, :], in_=ss[:ntok, :], func=AF.Sqrt, scale=1.0 / D, bias=EPS),
                 r=[ss], w=[sd])
            k.do("dve", lambda e: e.reciprocal(out=rs[:ntok, :], in_=sd[:ntok, :]), r=[sd], w=[rs])
            xn = fr["xn"].next()
            k.do("pool", lambda e: e.tensor_scalar(out=xn[:ntok, :], in0=xt[:ntok, :], scalar1=rs[:ntok, 0:1], scalar2=1.0,
                                                   op0=ALU.mult, op1=ALU.mult), r=[xt, rs], w=[xn])
            prevB = fr.get("pend")
            fr["pend"] = lambda: front_B(fr, xn, ntok, A, Bt, Bo, groups, hT, col0)
            if prevB is not None:
                prevB()

        def front_flush(fr):
            if fr.get("pend") is not None:
                fr["pend"]()
                fr["pend"] = None

        def front_B(fr, xn, ntok, A, Bt, Bo, groups, hT, col0):
            for g4 in range(4):
                p = fr["pt"].next()
                for j in range(4):
                    kc = g4 * 4 + j
                    k.do("pe", lambda e: e.transpose(out=p[:, j, :ntok], in_=xn[:ntok, kc * 128:(kc + 1) * 128],
                                                     identity=ident[:ntok, :ntok]),
                         r=[xn, ident], w=[p], inc=(j == 3))
                if groups is None:
                    fr["n"] += 1
                    if fr["n"] % 2 == 0:
                        k.do("dve", lambda e: e.tensor_copy(out=hT[:, g4 * 4:(g4 + 1) * 4, col0:col0 + ntok], in_=p[:, :, :ntok]), r=[p], w=[hT])
                    else:
                        k.do("act", lambda e: e.copy(out=hT[:, g4 * 4:(g4 + 1) * 4, col0:col0 + ntok], in_=p[:, :, :ntok]), r=[p], w=[hT])
                    continue
                for j in range(4):
                    kc = g4 * 4 + j
                    for (c0, n, r) in groups:
                        fr["n"] += 1
                        if fr["n"] % 2 == 0:
                            k.do("dve", lambda e: e.tensor_scalar(out=hT[:, kc, col0 + c0:col0 + c0 + n], in0=p[:, j, c0:c0 + n],
                                                                  scalar1=A[:, kc, r:r + 1], scalar2=Bt[:, Bo + kc, r:r + 1],
                                                                  op0=ALU.mult, op1=ALU.add), r=[p, A, Bt], w=[hT])
                        else:
                            k.do("act", lambda e: e.activation(out=hT[:, kc, col0 + c0:col0 + c0 + n], in_=p[:, j, c0:c0 + n],
                                                               func=AF.Identity, scale=A[:, kc, r:r + 1],
                                                               bias=Bt[:, Bo + kc, r:r + 1]), r=[p, A, Bt], w=[hT])

        def front(fr, src, ntok, A, Bo, groups, hT, col0):
            xt = fr["x"].next()
            k.dma("act", xt[:ntok, :], src, w=[xt])
            front_from_sb(fr, xt, ntok, A, modT, Bo, groups, hT, col0)

        G_P = [(0, 128, 0)]
        G_M = [(0, 16, 1), (16, 16, 2), (32, 16, 3), (48, 16, 4), (64, 32, 0)]

        def gemm_fm(pan, mlist, nk, hT, blocks, pspool, consume):
            for mi, (m0, msz) in enumerate(mlist):
                for bi, (c0, n) in enumerate(blocks):
                    p = pspool.next()
                    for kc in range(nk):
                        k.do("pe", lambda e: e.matmul(out=p[:msz, :n], lhsT=pan[:, kc, m0:m0 + msz], rhs=hT[:, kc, c0:c0 + n],
                                                      start=(kc == 0), stop=(kc == nk - 1)),
                             r=[pan, hT], w=[p], inc=(kc == nk - 1))
                    consume(mi, bi, c0, n, p)

        M4 = [(0, 128), (128, 128), (256, 128), (384, 128)]

        def rms_fm(res, raw, gT, blocks, pspool, out_f32=None, out_bf=None):
            for (c0, n) in blocks:
                sq = res["sq"].next()
                for kc in range(4):
                    k.do("act", lambda e: e.activation(out=sq[:, kc, :n], in_=raw[:, kc, c0:c0 + n], func=AF.Square),
                         r=[raw], w=[sq])
                p = pspool.next()
                for kc in range(4):
                    k.do("pe", lambda e: e.matmul(out=p[:, :n], lhsT=ones[:, :], rhs=sq[:, kc, :n], start=(kc == 0), stop=(kc == 3)),
                         r=[ones, sq], w=[p], inc=(kc == 3))
                sd = res["sd"].next(); rb = res["rb"].next()
                k.do("act", lambda e: e.activation(out=sd[:, :n], in_=p[:, :n], func=AF.Sqrt, scale=1.0 / 512, bias=EPS),
                     r=[p], w=[sd])
                k.do("dve", lambda e: e.reciprocal(out=rb[:, :n], in_=sd[:, :n]), r=[sd], w=[rb])
                for kc in range(4):
                    if out_f32 is not None:
                        k.do("dve", lambda e: e.scalar_tensor_tensor(out=out_f32[:, kc, c0:c0 + n], in0=raw[:, kc, c0:c0 + n],
                                                                     scalar=gT[:, kc:kc + 1], in1=rb[:, :n],
                                                                     op0=ALU.mult, op1=ALU.mult), r=[raw, gT, rb], w=[out_f32])
                    if out_bf is not None:
                        k.do("dve", lambda e: e.scalar_tensor_tensor(out=out_bf[:, kc, c0:c0 + n], in0=raw[:, kc, c0:c0 + n],
                                                                     scalar=gT[:, kc:kc + 1], in1=rb[:, :n],
                                                                     op0=ALU.mult, op1=ALU.mult), r=[raw, gT, rb], w=[out_bf])

        def make_rms(res_es, width):
            return {"sq": k.ring(res_es, "rsq", [128, 4, width], BF16, 2),
                    "sd": k.ring(res_es, "rsd", [128, width], F32, 2),
                    "rb": k.ring(res_es, "rrb", [128, width], F32, 2)}


        def final_phase(fes_, acc):
            x1r = k.ring(fes_, "fx1", [128, D], F32, 2)
            gtr = k.ring(fes_, "fgt", [128, D], F32, 2)
            yr = k.ring(fes_, "fy", [128, D], F32, 2)
            gfb = k.sb(fes_, "gfb", [128, D], F32)
            junk = k.sb(fes_, "fjunk", [128, D], BF16)
            ssr = k.ring(fes_, "fss", [128, 1], F32, 2)
            sdr = k.ring(fes_, "fsd", [128, 1], F32, 2)
            rsr = k.ring(fes_, "frs", [128, 1], F32, 2)
            k.dma("sp", gfb[:, :], bc(g_f[0:1, :], [128, D]), w=[gfb])
            for ti, (t0, nt) in enumerate(TILES_O):
                x1 = x1r.next()
                k.dma("sp", x1[:nt, :], x1_d.t[t0:t0 + nt, :], r=[x1_d], w=[x1])
                if acc is not None:
                    GT = gtr.next()
                    if t0 < 1024:
                        k.dma("sp", GT[:nt, :], bc(modrows.t[0:1, D:2 * D], [nt, D]), r=[modrows], w=[GT])
                    else:
                        for bb in range(4):
                            k.dma("sp", GT[16 * bb:16 * bb + 16, :], bc(modrows.t[1 + bb:2 + bb, D:2 * D], [16, D]), r=[modrows], w=[GT])
                    a = acc[ti]
                    k.do("pool", lambda e: e.tensor_tensor(out=GT[:nt, :], in0=GT[:nt, :], in1=a[:nt, :], op=ALU.mult), r=[GT, a], w=[GT])
                    k.do("dve", lambda e: e.tensor_tensor(out=x1[:nt, :], in0=x1[:nt, :], in1=GT[:nt, :], op=ALU.add), r=[GT, x1], w=[x1])
                ss = ssr.next(); sd = sdr.next(); rs = rsr.next()
                k.do("act", lambda e: e.activation(out=junk[:nt, :], in_=x1[:nt, :], func=AF.Square, accum_out=ss[:nt, 0:1]), r=[x1], w=[junk, ss])
                k.do("act", lambda e: e.activation(out=sd[:nt, :], in_=ss[:nt, :], func=AF.Sqrt, scale=1.0 / D, bias=EPS), r=[ss], w=[sd])
                k.do("dve", lambda e: e.reciprocal(out=rs[:nt, :], in_=sd[:nt, :]), r=[sd], w=[rs])
                y = yr.next()
                k.do("dve", lambda e: e.scalar_tensor_tensor(out=y[:nt, :], in0=x1[:nt, :], scalar=rs[:nt, 0:1], in1=gfb[:nt, :],
                                                             op0=ALU.mult, op1=ALU.mult), r=[x1, rs, gfb], w=[y])
                k.dma("sp", o_y[t0:t0 + nt, :], y[:nt, :], r=[y], store=True)

        ckvS = k.sb(es, "ckvS", [128, 4, 64], BF16)
        krS = k.sb(es, "krS", [64, 64], BF16)
        p1es = es.enter_context(ExitStack())
        cq_t = ld(p1es, "cosq", [64, NT], cosq[:, :])
        sq_t = ld(p1es, "sinq", [64, NT], sinq[:, :])
        hT1 = k.sb(p1es, "hT1", [128, 16, NT], BF16)
        with ExitStack() as fes:
            fr = make_front(fes)
            for tt in range(8):
                front(fr, xown[tt * 128:(tt + 1) * 128, :], 128, A1, 0, G_P, hT1, tt * 128)
            front(fr, xown[1024:1120, :], 96, A1, 0, G_M, hT1, 1024)
            front_flush(fr)
            k.barrier()

        wpool = k.ring(p1es, "winpan", [128, 16, 512], BF16, 2)
        pg = k.ring(p1es, "pg", [128, 512], F32, 4, psum=True)

        def load_pan(c0, ncols=512):
            wp = wpool.next()
            k.dma("pool", wp[:, :, :ncols], w_in[:, c0:c0 + ncols].rearrange("(kc p) n -> p kc n", p=128), w=[wp])
            return wp

        with ExitStack() as pb:
            cqT = k.sb(pb, "cqT", [128, 4, NO], BF16)
            raw4 = k.sb(pb, "raw4", [128, 4, NT], F32)
            nrm4 = k.sb(pb, "nrm4", [128, 4, NT], F32)
            rres = make_rms(pb, 512)
            ptp = k.ring(pb, "ptp", [128, 512], F32, 2, psum=True)
            ost = k.ring(pb, "ost", [128, 512], F32, 2)

            def cons_raw(mi, bi, c0, n, p):
                k.do("act", lambda e: e.copy(out=raw4[:, mi, c0:c0 + n], in_=p[:, :n]), r=[p], w=[raw4])

            wp = load_pan(0)
            gemm_fm(wp, M4, 16, hT1, BLK_T, pg, cons_raw)
            rms_fm(rres, raw4, gq, BLK_O, pg, out_bf=cqT)
            k.dma("sp", cq_d.t[:, :, :], cqT[:, :, :], r=[cqT], scratch=cq_d)
            wp = load_pan(512)
            gemm_fm(wp, M4, 16, hT1, BLK_T, pg, cons_raw)
            rms_fm(rres, raw4, gkv, BLK_O, pg, out_f32=nrm4)
            k.do("dve", lambda e: e.tensor_copy(out=ckvS[:, :, :], in_=nrm4[:, :, 1024:1088]), r=[nrm4], w=[ckvS])
            for (t0, nt) in TILES_O:
                p = ptp.next()
                for kc in range(4):
                    k.do("pe", lambda e: e.transpose(out=p[:nt, kc * 128:(kc + 1) * 128], in_=nrm4[:, kc, t0:t0 + nt],
                                                     identity=identf[:, :]), r=[nrm4, identf], w=[p], inc=(kc == 3))
                o = ost.next()
                k.do("act", lambda e: e.copy(out=o[:nt, :], in_=p[:nt, :]), r=[p], w=[o])
                k.dma("sp", o_ckv[t0:t0 + nt, :], o[:nt, :], r=[o], store=True)
            wp = wpool.next()
            src = w_in[:, 1024:1088].rearrange("(kc p) n -> p kc n", p=128)
            k.dma("pool", wp[:, :, 0:64], src, w=[wp])
            k.dma("pool", wp[:, :, 64:96], w_in[:, 1056:1088].rearrange("(kc p) n -> p kc n", p=128), w=[wp])
            k.dma("pool", wp[:, :, 96:128], w_in[:, 1024:1056].rearrange("(kc p) n -> p kc n", p=128), w=[wp])
            krf = k.sb(pb, "krf", [64, NT], F32)
            t1 = k.sb(pb, "kt1", [64, NT], F32)

            def cons_kr(mi, bi, c0, n, p):
                if mi == 0:
                    k.do("dve", lambda e: e.tensor_tensor(out=t1[:, c0:c0 + n], in0=p[:64, :n], in1=cq_t[:, c0:c0 + n], op=ALU.mult),
                         r=[p, cq_t], w=[t1])
                else:
                    k.do("dve", lambda e: e.tensor_tensor(out=krf[:, c0:c0 + n], in0=p[:64, :n], in1=sq_t[:, c0:c0 + n], op=ALU.mult),
                         r=[p, sq_t], w=[krf])
                    k.do("pool", lambda e: e.tensor_tensor(out=krf[:, c0:c0 + n], in0=krf[:, c0:c0 + n], in1=t1[:, c0:c0 + n], op=ALU.add),
                         r=[krf, t1], w=[krf])

            gemm_fm(wp, [(0, 64), (64, 64)], 16, hT1, BLK_T, pg, cons_kr)
            k.do("dve", lambda e: e.tensor_copy(out=krS[:, :], in_=krf[:, 1024:1088]), r=[krf], w=[krS])
            for (t0, nt) in TILES_O:
                p = ptp.next()
                k.do("pe", lambda e: e.transpose(out=p[:nt, 0:64], in_=krf[:, t0:t0 + nt], identity=identf[:64, :64]),
                     r=[krf, identf], w=[p])
                o = ost.next()
                k.do("act", lambda e: e.copy(out=o[:nt, 0:64], in_=p[:nt, 0:64]), r=[p], w=[o])
                k.dma("sp", o_kr[t0:t0 + nt, :], o[:nt, 0:64], r=[o], store=True)
            k.barrier()

        if stop_after == "p1b":
            k.barrier()
            k.finish()
            nc._in_names = in_names
            return nc

        k.mark("p1b_done")
        mbT = k.sb(p1es, "mbT", [128, 8, NO], BF16)
        wcv = ld(p1es, "wcv", [128, 8, 3], w_convT[:, :, :])
        bcv = ld(p1es, "bcv", [128, 8], b_convT[:, :])
        hv = ld(p1es, "hv", [128, 32], hvalid[:, :])
        scv = ld(p1es, "scv", [128, 8, 4, 2], sconvT[:, :, :, :])
        with ExitStack() as pcs:
            phT = k.sb(pcs, "phT", [128, 8, NT], BF16)
            pbT = k.sb(pcs, "pbT", [128, 8, NO], BF16)
            zpT = k.sb(pcs, "zpT", [128, 8, 1128], BF16)
            zout = k.sb(pcs, "zout", [128, 8, 10], F32)
            tmpz = k.ring(pcs, "tmpz", [128, 32], F32, 2)
            ycr = k.ring(pcs, "yc", [128, NO], F32, 2)
            ptz = k.ring(pcs, "ptz", [128, 512], F32, 2, psum=True)
            ozs = k.sb(pcs, "ozs", [128, 1024], F32)
            k.do("dve", lambda e: e.tensor_copy(
                out=zpT[:, :, 1056:1128].rearrange("p c (b s) -> p c b s", s=18)[:, :, :, 0:2], in_=scv[:, :, :, :]),
                r=[scv], w=[zpT])
            for pi in range(2):
                wp = load_pan(1088 + 512 * pi)

                def cons_ph(mi, bi, c0, n, p, pi=pi):
                    k.do("act", lambda e: e.copy(out=phT[:, 4 * pi + mi, c0:c0 + n], in_=p[:, :n]), r=[p], w=[phT])
                gemm_fm(wp, M4, 16, hT1, BLK_T, pg, cons_ph)
            for pi in range(2):
                wp = load_pan(2112 + 512 * pi)

                def cons_pb(mi, bi, c0, n, p, pi=pi):
                    n = min(n, NO - c0)
                    k.do("act", lambda e: e.copy(out=pbT[:, 4 * pi + mi, c0:c0 + n], in_=p[:, :n]), r=[p], w=[pbT])
                gemm_fm(wp, M4, 16, hT1, BLK_T, pg, cons_pb)
            k.mark("phpb_done")
            for pi in range(2):
                wp = load_pan(3136 + 512 * pi)

                def cons_pc(mi, bi, c0, n, p, pi=pi):
                    mc = 4 * pi + mi
                    if bi < 2:
                        dst = zpT[:, mc, 528 * bi:528 * (bi + 1)].rearrange("p (j s) -> p j s", s=66)[:, :, 2:66]
                        k.do("dve", lambda e: e.tensor_tensor(out=dst, in0=p[:, 0:512].rearrange("p (j s) -> p j s", s=64),
                                                              in1=phT[:, mc, c0:c0 + 512].rearrange("p (j s) -> p j s", s=64),
                                                              op=ALU.mult), r=[p, phT], w=[zpT])
                        if bi == 1:
                            k.do("dve", lambda e: e.tensor_tensor(out=zout[:, mc, 0:2], in0=p[:, 510:512], in1=phT[:, mc, 1022:1024],
                                                                  op=ALU.mult), r=[p, phT], w=[zout])
                    else:
                        dst = zpT[:, mc, 1056:1128].rearrange("p (j s) -> p j s", s=18)[:, :, 2:18]
                        k.do("dve", lambda e: e.tensor_tensor(out=dst, in0=p[:, 0:64].rearrange("p (j s) -> p j s", s=16),
                                                              in1=phT[:, mc, 1024:1088].rearrange("p (j s) -> p j s", s=16),
                                                              op=ALU.mult), r=[p, phT], w=[zpT])
                        k.do("dve", lambda e: e.tensor_tensor(out=zout[:, mc, 2:10].rearrange("p (j s) -> p j s", s=2),
                                                              in0=p[:, 0:64].rearrange("p (j s) -> p j s", s=16)[:, :, 14:16],
                                                              in1=phT[:, mc, 1024:1088].rearrange("p (j s) -> p j s", s=16)[:, :, 14:16],
                                                              op=ALU.mult), r=[p, phT], w=[zout])
                        tz = tmpz.next()
                        k.do("dve", lambda e: e.tensor_tensor(out=tz[:, :], in0=p[:, 64:96], in1=phT[:, mc, 1088:1120], op=ALU.mult),
                             r=[p, phT], w=[tz])
                        dsth = zpT[:, mc, 0:1056].rearrange("p (j s) -> p j s", s=66)[:, :, 0:2]
                        k.do("pool", lambda e: e.tensor_tensor(out=dsth, in0=tz[:, :].rearrange("p (j s) -> p j s", s=2),
                                                               in1=hv[:, :].rearrange("p (j s) -> p j s", s=2), op=ALU.mult),
                             r=[tz, hv], w=[zpT])
                        yc = ycr.next()
                        for (zv, yv) in ((zpT[:, mc, 0:1056].rearrange("p (j s) -> p j s", s=66), yc[:, 0:1024].rearrange("p (j s) -> p j s", s=64)),
                                         (zpT[:, mc, 1056:1128].rearrange("p (j s) -> p j s", s=18), yc[:, 1024:1088].rearrange("p (j s) -> p j s", s=16))):
                            L = 64 if zv.shape[2] == 66 else 16
                            k.do("dve", lambda e: e.tensor_scalar(out=yv, in0=zv[:, :, 0:L], scalar1=wcv[:, mc, 0:1], scalar2=bcv[:, mc:mc + 1],
                                                                  op0=ALU.mult, op1=ALU.add), r=[zpT, wcv, bcv], w=[yc])
                            k.do("dve", lambda e: e.scalar_tensor_tensor(out=yv, in0=zv[:, :, 1:L + 1], scalar=wcv[:, mc, 1:2], in1=yv,
                                                                         op0=ALU.mult, op1=ALU.add), r=[zpT, wcv, yc], w=[yc])
                            k.do("dve", lambda e: e.scalar_tensor_tensor(out=yv, in0=zv[:, :, 2:L + 2], scalar=wcv[:, mc, 2:3], in1=yv,
                                                                         op0=ALU.mult, op1=ALU.add), r=[zpT, wcv, yc], w=[yc])
                        k.do("pool", lambda e: e.tensor_tensor(out=mbT[:, mc, :], in0=yc[:, :], in1=pbT[:, mc, :], op=ALU.mult),
                             r=[yc, pbT], w=[mbT])
                gemm_fm(wp, M4, 16, hT1, BLK_T, pg, cons_pc)
            k.mark("pc_done")
            for half in range(2):
                p = ptz.next()
                for j in range(4):
                    mc = half * 4 + j
                    k.do("pe", lambda e: e.transpose(out=p[:10, j * 128:(j + 1) * 128], in_=zout[:, mc, :], identity=identf[:, :]),
                         r=[zout, identf], w=[p], inc=(j == 3))
                k.do("act", lambda e: e.copy(out=ozs[:10, half * 512:(half + 1) * 512], in_=p[:10, 0:512]), r=[p], w=[ozs])
            k.dma("sp", o_conv[:, :], ozs[:10, :], r=[ozs], store=True)
            k.barrier()

        k.mark("p1c_done")
        with ExitStack() as pds:
            g1T = k.sb(pds, "g1T", [128, 16, NO], BF16)
            gst = k.ring(pds, "gst", [128, NO], BF16, 3)
            cur = {}
            for pi in range(8):
                wp = load_pan(4160 + 512 * pi)

                def cons_g(mi, bi, c0, n, p, pi=pi):
                    n = min(n, NO - c0)
                    dc = (4 * pi + mi) % 16
                    if pi < 4:
                        if bi == 0:
                            cur["g"] = gst.next()
                        g = cur["g"]
                        k.do("act", lambda e: e.activation(out=g[:, c0:c0 + n], in_=p[:, :n], func=AF.Sigmoid), r=[p], w=[g])
                        if bi == 2:
                            k.dma("sp", g0_d.t[dc, :, :], g[:, :], r=[g], scratch=g0_d)
                    else:
                        k.do("act", lambda e: e.activation(out=g1T[:, dc, c0:c0 + n], in_=p[:, :n], func=AF.Sigmoid), r=[p], w=[g1T])
                gemm_fm(wp, M4, 16, hT1, BLK_T, pg, cons_g)
            for pi in range(4):
                wp = wpool.next()
                k.dma("pool", wp[:, 0:8, :], w_ob[:, pi * 512:(pi + 1) * 512].rearrange("(kc p) n -> p kc n", p=128), w=[wp])

                def cons_b(mi, bi, c0, n, p, pi=pi):
                    dc = 4 * pi + mi
                    if bi == 0:
                        cur["g"] = gst.next()
                    g = cur["g"]
                    k.do("dve", lambda e: e.tensor_tensor(out=g[:, c0:c0 + n], in0=p[:, :n], in1=g1T[:, dc, c0:c0 + n], op=ALU.mult),
                         r=[p, g1T], w=[g])
                    if bi == 2:
                        k.dma("sp", gb_d.t[dc, :, :], g[:, :], r=[g], scratch=gb_d)
                gemm_fm(wp, M4, 8, mbT, BLK_O, pg, cons_b)
            k.barrier()
        p1es.close()
        k.mark("p1_done")

        if stop_after == "p1":
            k.barrier()
            k.finish()
            nc._in_names = in_names
            return nc

        with ExitStack() as kes:
            wuk = k.sb(kes, "wuk", [128, 4, 1024], BF16)
            wuv = k.sb(kes, "wuv", [128, 4, 1024], BF16)
            k.dma("pool", wuk[:, :, :], w_uk.rearrange("(kc p) n -> p kc n", p=128), w=[wuk])
            k.dma("pool", wuv[:, :, :], w_uv.rearrange("(kc p) n -> p kc n", p=128), w=[wuv])
            pk = k.ring(kes, "pk", [128, 512], F32, 4, psum=True)

            def kv_gen(ckT, blocks, tiles, KTst, Vst):
                n_ev = [0]

                def ev(out, in_, rr, ww):
                    n_ev[0] += 1
                    if n_ev[0] % 2:
                        k.do("act", lambda e: e.copy(out=out, in_=in_), r=rr, w=ww)
                    else:
                        k.do("dve", lambda e: e.tensor_copy(out=out, in_=in_), r=rr, w=ww)
                for h in range(8):
                    for (c0, n) in blocks:
                        p = pk.next()
                        for kc in range(4):
                            k.do("pe", lambda e: e.matmul(out=p[:, :n], lhsT=wuk[:, kc, h * 128:(h + 1) * 128], rhs=ckT[:, kc, c0:c0 + n],
                                                          start=(kc == 0), stop=(kc == 3)), r=[wuk, ckT], w=[p], inc=(kc == 3))
                        ev(KTst[:, h, c0:c0 + n], p[:, :n], [p], [KTst])
                for ti, (t0, nk) in enumerate(tiles):
                    for hh in range(2):
                        p = pk.next()
                        for kc in range(4):
                            k.do("pe", lambda e: e.matmul(out=p[:nk, :], lhsT=ckT[:, kc, t0:t0 + nk], rhs=wuv[:, kc, hh * 512:(hh + 1) * 512],
                                                          start=(kc == 0), stop=(kc == 3)), r=[wuv, ckT], w=[p], inc=(kc == 3))
                        ev(Vst[:nk, ti, hh * 512:(hh + 1) * 512], p[:nk, :], [p], [Vst])

            with ExitStack() as kss:
                ptb = k.ring(kss, "ptb", [128, 4, 128], BF16, 2, psum=True)
                ckc_r = k.ring(kss, "ckc", [128, 8, 512], BF16, 2)
                ckcT_r = k.ring(kss, "ckcT", [128, 4, 1040], BF16, 2)
                KTs_r = k.ring(kss, "KTs", [128, 8, 1040], BF16, 2)
                Vs_r = k.ring(kss, "Vs", [128, 9, 1024], BF16, 2)
                krc_r = k.ring(kss, "krc", [128, 8, 64], BF16, 2)
                krcT_r = k.ring(kss, "krcT", [64, 1040], BF16, 2)
                for bb in range(4):
                    ckc = ckc_r.next()
                    k.dma("pool", ckc[:, :, :], cckv[bb].rearrange("(t p) r -> p t r", p=128), w=[ckc])
                    ckcT = ckcT_r.next()
                    for t in range(8):
                        p = ptb.next()
                        for kc in range(4):
                            k.do("pe", lambda e: e.transpose(out=p[:, kc, :], in_=ckc[:, t, kc * 128:(kc + 1) * 128], identity=ident[:, :]),
                                 r=[ckc, ident], w=[p], inc=(kc == 3))
                        k.do("act" if t % 2 else "dve",
                             (lambda e: e.copy(out=ckcT[:, :, t * 128:(t + 1) * 128], in_=p[:, :, :])) if t % 2 else
                             (lambda e: e.tensor_copy(out=ckcT[:, :, t * 128:(t + 1) * 128], in_=p[:, :, :])), r=[p], w=[ckcT])
                    k.do("dve", lambda e: e.tensor_copy(out=ckcT[:, :, 1024:1040], in_=ckvS[:, :, bb * 16:(bb + 1) * 16]), r=[ckvS], w=[ckcT])
                    KTs = KTs_r.next(); Vs = Vs_r.next()
                    kv_gen(ckcT, [(0, 512), (512, 512), (1024, 16)], [(t * 128, 128) for t in range(8)] + [(1024, 16)], KTs, Vs)
                    k.dma("sp", KTs_d.t[bb].rearrange("h p n -> p h n"), KTs[:, :, :], r=[KTs], scratch=KTs_d)
                    k.dma("sp", Vs_d.t[bb].rearrange("h p (t d) -> p t h d", d=128),
                          Vs[:, :, :].rearrange("p t (h d) -> p t h d", d=128), r=[Vs], scratch=Vs_d)
                    krc = krc_r.next()
                    k.dma("pool", krc[:, :, :], ckr[bb].rearrange("(t p) r -> p t r", p=128), w=[krc])
                    krcT = krcT_r.next()
                    for half in range(2):
                        p = ptb.next()
                        for j in range(4):
                            t = half * 4 + j
                            k.do("pe", lambda e: e.transpose(out=p[:64, j, :], in_=krc[:, t, :], identity=ident[:, :]),
                                 r=[krc, ident], w=[p], inc=(j == 3))
                        k.do("act", lambda e: e.copy(out=krcT[:, half * 512:(half + 1) * 512], in_=p[:64, :, :].rearrange("p a b -> p (a b)")),
                             r=[p], w=[krcT])
                    k.do("dve", lambda e: e.tensor_copy(out=krcT[:, 1024:1040], in_=krS[:, bb * 16:(bb + 1) * 16]), r=[krS], w=[krcT])
                    k.dma("sp", krc_d.t[:, bb, :], krcT[:, :], r=[krcT], scratch=krc_d)
                k.barrier()
            k.mark("ks_done")

            with ExitStack() as kps:
                wkv = k.sb(kps, "wkv", [128, 16, 640], BF16)
                k.dma("pool", wkv[:, :, 0:512], w_in[:, 512:1024].rearrange("(kc p) n -> p kc n", p=128), w=[wkv])
                k.dma("pool", wkv[:, :, 512:576], w_in[:, 1024:1088].rearrange("(kc p) n -> p kc n", p=128), w=[wkv])
                k.dma("pool", wkv[:, :, 576:608], w_in[:, 1056:1088].rearrange("(kc p) n -> p kc n", p=128), w=[wkv])
                k.dma("pool", wkv[:, :, 608:640], w_in[:, 1024:1056].rearrange("(kc p) n -> p kc n", p=128), w=[wkv])
                fr = make_front(kps)
                Bbf = k.sb(kps, "Bbf", [128, 16, 2], BF16)
                kbias = k.sb(kps, "kbias", [128, 8], F32)
                k.do("dve", lambda e: e.tensor_copy(out=Bbf[:, :, 0:1], in_=modT[:, 0:16, 0:1]), r=[modT], w=[Bbf])
                for mi_, (m0_, msz_) in enumerate(M4 + [(512, 64), (576, 64)]):
                    pb_ = pk.next()
                    for kc in range(16):
                        k.do("pe", lambda e: e.matmul(out=pb_[:msz_, 0:1], lhsT=wkv[:, kc, m0_:m0_ + msz_], rhs=Bbf[:, kc, 0:1],
                                                      start=(kc == 0), stop=(kc == 15)), r=[wkv, Bbf], w=[pb_], inc=(kc == 15))
                    k.do("act", lambda e: e.copy(out=kbias[:msz_, mi_:mi_ + 1], in_=pb_[:msz_, 0:1]), r=[pb_], w=[kbias])
                for kc in range(16):
                    k.do("dve" if kc % 2 else "pool",
                         lambda e: e.tensor_scalar(out=wkv[:, kc, :], in0=wkv[:, kc, :], scalar1=A1[:, kc, 0:1], scalar2=1.0,
                                                   op0=ALU.mult, op1=ALU.mult), r=[wkv, A1], w=[wkv])
                hTk = k.ring(kps, "hTk", [128, 16, 512], BF16, 2)
                rawk_r = k.ring(kps, "rawk", [128, 4, 512], F32, 2)
                ckb_r = k.ring(kps, "ckb", [128, 4, 512], BF16, 2)
                rres = make_rms(kps, 512)
                cos_r = k.ring(kps, "cosb", [64, 512], F32, 2)
                sin_r = k.ring(kps, "sinb", [64, 512], F32, 2)
                kt1_r = k.ring(kps, "kt1b", [64, 512], F32, 2)
                kt2_r = k.ring(kps, "kt2b", [64, 512], F32, 2)
                krst_r = k.ring(kps, "krst", [64, 512], BF16, 2)
                KT_r = k.ring(kps, "KTst", [128, 8, 512], BF16, 2)
                V_r = k.ring(kps, "Vst", [128, 4, 1024], BF16, 2)
                kst = {"hT": hTk.next(), "pre": False}

                def kp_fronts(b):
                    hT = kst["hT"]
                    for i in range(1 if kst["pre"] else 0, 4):
                        r0 = (4 * b + i) * 128
                        front(fr, xall[r0:r0 + 128, :], 128, A1, 0, None, hT, i * 128)
                    if b < 15:
                        hTn = hTk.next()
                        r0 = (4 * (b + 1)) * 128
                        front(fr, xall[r0:r0 + 128, :], 128, A1, 0, None, hTn, 0)
                        kst["hT"] = hTn
                        kst["pre"] = True
                    else:
                        front_flush(fr)
                    return hT

                def kp_gemm(b, hT):
                    rawk = rawk_r.next()

                    def cons_rawk(mi, bi, c0, n, p, rawk=rawk):
                        k.do("act", lambda e: e.activation(out=rawk[:, mi, :], in_=p[:, :], func=AF.Identity, bias=kbias[:, mi:mi + 1]),
                             r=[p, kbias], w=[rawk])
                    gemm_fm(wkv, M4, 16, hT, [(0, 512)], pk, cons_rawk)
                    cb = cos_r.next(); sb_ = sin_r.next()
                    k.dma("sp", cb[:, :], cosk[:, b * 512:(b + 1) * 512], w=[cb])
                    k.dma("sp", sb_[:, :], sink[:, b * 512:(b + 1) * 512], w=[sb_])
                    t1 = kt1_r.next(); t2 = kt2_r.next(); krst = krst_r.next()

                    def cons_krk(mi, bi, c0, n, p, t1=t1, t2=t2, krst=krst, cb=cb, sb_=sb_):
                        if mi == 0:
                            k.do("dve", lambda e: e.scalar_tensor_tensor(out=t1[:, :], in0=p[:64, :], scalar=kbias[:64, 4:5], in1=cb[:, :],
                                                                         op0=ALU.add, op1=ALU.mult), r=[p, cb, kbias], w=[t1])
                        else:
                            k.do("dve", lambda e: e.scalar_tensor_tensor(out=t2[:, :], in0=p[:64, :], scalar=kbias[:64, 5:6], in1=sb_[:, :],
                                                                         op0=ALU.add, op1=ALU.mult), r=[p, sb_, kbias], w=[t2])
                            k.do("pool", lambda e: e.tensor_tensor(out=krst[:, :], in0=t1[:, :], in1=t2[:, :], op=ALU.add), r=[t1, t2], w=[krst])
                    gemm_fm(wkv, [(512, 64), (576, 64)], 16, hT, [(0, 512)], pk, cons_krk)
                    k.dma("sp", krT_d.t[:, b * 512:(b + 1) * 512], krst[:, :], r=[krst], scratch=krT_d)
                    return rawk

                def kp_rms(b, rawk):
                    ckb = ckb_r.next()
                    rms_fm(rres, rawk, gkv, [(0, 512)], pk, out_bf=ckb)
                    return ckb

                def kp_kv(b, ckb):
                    KTst = KT_r.next(); Vst = V_r.next()
                    kv_gen(ckb, [(0, 512)], [(t * 128, 128) for t in range(4)], KTst, Vst)
                    k.dma("sp", KT_d.t[:, b].rearrange("h p n -> p h n"), KTst[:, :, :], r=[KTst], scratch=KT_d)
                    k.dma("sp", V_d.t[:, b].rearrange("h p (t d) -> p t h d", d=128),
                          Vst[:, :, :].rearrange("p t (h d) -> p t h d", d=128), r=[Vst], scratch=V_d)

                prevraw = None
                for b in range(16):
                    hT = kp_fronts(b)
                    ckb_prev = kp_rms(b - 1, prevraw) if prevraw is not None else None
                    prevraw_new = kp_gemm(b, hT)
                    if ckb_prev is not None:
                        kp_kv(b - 1, ckb_prev)
                    prevraw = prevraw_new
                ckb_last = kp_rms(15, prevraw)
                kp_kv(15, ckb_last)
                k.barrier()
            k.mark("kp_done")

        if stop_after == "kp":
            k.barrier()
            k.finish()
            nc._in_names = in_names
            return nc

        h2T = k.sb(es, "h2T", [128, 16, NO], BF16)
        mes = es.enter_context(ExitStack())
        mT = k.sb(mes, "mT", [128, 16, NO], BF16)
        oes = es.enter_context(ExitStack())
        oT = k.sb(oes, "oT", [128, 8, NO], BF16)
        with ExitStack() as at:
            cqT = k.sb(at, "cqTa", [128, 4, NO], BF16)
            k.dma("sp", cqT[:, :, :], cq_d.t[:, :, :], r=[cq_d], w=[cqT])
            krTa = k.sb(at, "krTa", [64, 8192], BF16)
            k.dma("sp", krTa[:, :], krT_d.t[:, :], r=[krT_d], w=[krTa])
            krcT = k.sb(at, "krcTa", [64, 4, 1040], BF16)
            k.dma("sp", krcT[:, :, :], krc_d.t[:, :, :], r=[krc_d], w=[krcT])
            dm = ld(at, "dm", [128, 4], dmask[:, :])
            cq_t = ld(at, "cosqa", [64, NT], cosq[:, :])
            sq_t = ld(at, "sinqa", [64, NT], sinq[:, :])
            wuq = k.sb(at, "wuq", [128, 4, 1536], BF16)
            k.dma("pool", wuq[:, :, :], w_uq.rearrange("(kc p) n -> p kc n", p=128), w=[wuq])
            wuqs = k.sb(at, "wuqs", [128, 4, 8, 64], BF16)
            for kc in range(4):
                srcv = w_uq[kc * 128:(kc + 1) * 128, :].rearrange("p (h c) -> p h c", c=192)
                k.dma("pool", wuqs[:, kc, :, 0:32], srcv[:, :, 160:192], w=[wuqs])
                k.dma("pool", wuqs[:, kc, :, 32:64], srcv[:, :, 128:160], w=[wuqs])
            qn_r = k.ring(at, "qn", [128, NO], BF16, 2)
            qr_r = k.ring(at, "qr", [64, NO], BF16, 2)
            qt1_r = k.ring(at, "qt1", [64, 512], F32, 2)
            qt2_r = k.ring(at, "qt2", [64, 512], F32, 2)
            kb_r = k.ring(at, "kb", [128, 512], BF16, 6)
            vb_r = k.ring(at, "vb", [128, 4, 128], BF16, 6)
            PT_r = k.ring(at, "PT", [128, 512], BF16, 4)
            rd_r = k.ring(at, "rd", [128, 512], F32, 2)
            dacc_r = [k.ring(at, "dacc0", [128, 512], F32, 2), k.ring(at, "dacc1", [128, 512], F32, 2)]
            onesf = k.sb(at, "onesf", [128, 128], F32)
            k.do("dve", lambda e: e.memset(onesf[:, :], 1.0), w=[onesf])
            kts_r = k.ring(at, "kts", [128, 1040], BF16, 3)
            vs_r = k.ring(at, "vss", [128, 9, 128], BF16, 3)
            ps_s = k.ring(at, "ps_s", [128, 512], F32, 3, psum=True)
            ps_o = k.ring(at, "ps_o", [128, 512], F32, 2, psum=True)
            ps_d = k.ring(at, "ps_d", [128, 512], F32, 2, psum=True)
            ps_q = k.ring(at, "ps_q", [128, 512], F32, 1, psum=True)

            for h in range(8):
                qn = qn_r.next(); qr = qr_r.next()
                for (c0, n) in BLK_O:
                    p = ps_q.next()
                    for kc in range(4):
                        k.do("pe", lambda e: e.matmul(out=p[:, :n], lhsT=wuq[:, kc, h * 192:h * 192 + 128], rhs=cqT[:, kc, c0:c0 + n],
                                                      start=(kc == 0), stop=(kc == 3)), r=[wuq, cqT], w=[p], inc=(kc == 3))
                    k.do("act", lambda e: e.copy(out=qn[:, c0:c0 + n], in_=p[:, :n]), r=[p], w=[qn])
                    p = ps_q.next()
                    for kc in range(4):
                        k.do("pe", lambda e: e.matmul(out=p[:64, :n], lhsT=wuq[:, kc, h * 192 + 128:h * 192 + 192], rhs=cqT[:, kc, c0:c0 + n],
                                                      start=(kc == 0), stop=(kc == 3)), r=[wuq, cqT], w=[p], inc=(kc == 3))
                    t1 = qt1_r.next()
                    k.do("dve", lambda e: e.tensor_tensor(out=t1[:, :n], in0=p[:64, :n], in1=cq_t[:, c0:c0 + n], op=ALU.mult), r=[p, cq_t], w=[t1])
                    p = ps_q.next()
                    for kc in range(4):
                        k.do("pe", lambda e: e.matmul(out=p[:64, :n], lhsT=wuqs[:, kc, h, :], rhs=cqT[:, kc, c0:c0 + n],
                                                      start=(kc == 0), stop=(kc == 3)), r=[wuqs, cqT], w=[p], inc=(kc == 3))
                    t2 = qt2_r.next()
                    k.do("dve", lambda e: e.tensor_tensor(out=t2[:, :n], in0=p[:64, :n], in1=sq_t[:, c0:c0 + n], op=ALU.mult), r=[p, sq_t], w=[t2])
                    k.do("pool", lambda e: e.tensor_tensor(out=qr[:, c0:c0 + n], in0=t1[:, :n], in1=t2[:, :n], op=ALU.add), r=[t1, t2], w=[qr])
                for g in range(2):
                    po = ps_o.next(); pd = ps_d.next()
                    dacc = [dacc_r[0].next(), dacc_r[1].next()]
                    k.do("dve", lambda e: e.memset(dacc[0][:, :], 0.0), w=[dacc[0]])
                    k.do("pool", lambda e: e.memset(dacc[1][:, :], 0.0), w=[dacc[1]])
                    ntile = 0
                    nblk = 8 * g + 8
                    c1 = 512 * (g + 1)
                    items = []
                    for b in range(nblk):
                        for kt in range(4):
                            items.append((b, kt))
                    blk = {}

                    def emit_S(b, kt):
                        if kt == 0:
                            Kb = kb_r.next(); Vb = vb_r.next()
                            k.dma("sp", Kb[:, :], KT_d.t[h, b], r=[KT_d], w=[Kb])
                            k.dma("sp", Vb[:, :, :], V_d.t[h, b].rearrange("p (t d) -> p t d", d=128), r=[V_d], w=[Vb])
                            blk[b] = (Kb, Vb)
                        Kb, Vb = blk[b]
                        jlo = max(b, 8 * g)
                        c0 = 64 * jlo
                        N = c1 - c0
                        diag = (b >= 8 * g)
                        ps = ps_s.next()
                        k.do("pe", lambda e: e.matmul(out=ps[:, :N], lhsT=Kb[:, kt * 128:(kt + 1) * 128], rhs=qn[:, c0:c1], start=True, stop=False),
                             r=[Kb, qn], w=[ps], inc=False)
                        k0 = b * 512 + kt * 128
                        k.do("pe", lambda e: e.matmul(out=ps[:, :N], lhsT=krTa[:, k0:k0 + 128], rhs=qr[:, c0:c1], start=False, stop=True),
                             r=[krTa, qr], w=[ps])
                        PT = PT_r.next()
                        if diag:
                            k.do("act", lambda e: e.activation(out=PT[:, 0:64], in_=ps[:, 0:64], func=AF.Exp, scale=SCALE, bias=dm[:, kt:kt + 1]),
                                 r=[ps, dm], w=[PT])
                            if N > 64:
                                k.do("act", lambda e: e.activation(out=PT[:, 64:N], in_=ps[:, 64:N], func=AF.Exp, scale=SCALE), r=[ps], w=[PT])
                        else:
                            k.do("act", lambda e: e.activation(out=PT[:, :N], in_=ps[:, :N], func=AF.Exp, scale=SCALE), r=[ps], w=[PT])
                        return (b, kt, Vb, PT, c0 - 512 * g, N)

                    def emit_PV(it, idx):
                        b, kt, Vb, PT, lc0, N = it
                        first = (idx == 0)
                        last = (idx == len(items) - 1)
                        k.do("pe", lambda e: e.matmul(out=po[:, lc0:lc0 + N], lhsT=Vb[:, kt, :], rhs=PT[:, :N], start=first, stop=last),
                             r=[Vb, PT], w=[po])
                        da = dacc[idx % 2]
                        k.do("dve" if idx % 2 == 0 else "pool",
                             lambda e: e.tensor_tensor(out=da[:, lc0:lc0 + N], in0=da[:, lc0:lc0 + N], in1=PT[:, :N], op=ALU.add),
                             r=[da, PT], w=[da])

                    pend = None
                    for idx, (b, kt) in enumerate(items):
                        it = emit_S(b, kt)
                        if pend is not None:
                            emit_PV(pend, idx - 1)
                        pend = it
                    emit_PV(pend, len(items) - 1)
                    for i_ in range(2):
                        k.do("pe", lambda e: e.matmul(out=pd[:, :], lhsT=onesf[:, :], rhs=dacc[i_][:, :], start=(i_ == 0), stop=(i_ == 1)),
                             r=[onesf, dacc[i_]], w=[pd], inc=(i_ == 1))
                    rd = rd_r.next()
                    k.do("dve", lambda e: e.reciprocal(out=rd[:, :], in_=pd[:, :]), r=[pd], w=[rd])
                    k.do("dve", lambda e: e.tensor_tensor(out=oT[:, h, 512 * g:512 * (g + 1)], in0=po[:, :], in1=rd[:, :], op=ALU.mult),
                         r=[po, rd], w=[oT])
                po = ps_o.next(); pd = ps_d.next()
                for bb in range(4):
                    Ks = kts_r.next(); Vs = vs_r.next()
                    k.dma("sp", Ks[:, :], KTs_d.t[bb, h], r=[KTs_d], w=[Ks])
                    k.dma("sp", Vs[:, :, :], Vs_d.t[bb, h].rearrange("p (t d) -> p t d", d=128), r=[Vs_d], w=[Vs])
                    q0 = 1024 + 16 * bb
                    for t in range(9):
                        nk = 128 if t < 8 else 16
                        ps = ps_s.next()
                        k.do("pe", lambda e: e.matmul(out=ps[:nk, :16], lhsT=Ks[:, t * 128:t * 128 + nk], rhs=qn[:, q0:q0 + 16], start=True, stop=False),
                             r=[Ks, qn], w=[ps], inc=False)
                        k.do("pe", lambda e: e.matmul(out=ps[:nk, :16], lhsT=krcT[:, bb, t * 128:t * 128 + nk], rhs=qr[:, q0:q0 + 16], start=False, stop=True),
                             r=[krcT, qr], w=[ps])
                        PT = PT_r.next()
                        k.do("act", lambda e: e.activation(out=PT[:nk, :16], in_=ps[:nk, :16], func=AF.Exp, scale=SCALE), r=[ps], w=[PT])
                        k.do("pe", lambda e: e.matmul(out=po[:, 16 * bb:16 * bb + 16], lhsT=Vs[:nk, t, :], rhs=PT[:nk, :16], start=(t == 0), stop=(t == 8)),
                             r=[Vs, PT], w=[po], inc=False)
                        k.do("pe", lambda e: e.matmul(out=pd[:, 16 * bb:16 * bb + 16], lhsT=ones[:nk, :], rhs=PT[:nk, :16], start=(t == 0), stop=(t == 8)),
                             r=[ones, PT], w=[pd])
                rd = rd_r.next()
                k.do("dve", lambda e: e.reciprocal(out=rd[:, 0:64], in_=pd[:, 0:64]), r=[pd], w=[rd])
                k.do("dve", lambda e: e.tensor_tensor(out=oT[:, h, 1024:1088], in0=po[:, 0:64], in1=rd[:, 0:64], op=ALU.mult),
                     r=[po, rd], w=[oT])
            k.barrier()
        k.mark("att_done")

        if stop_after == "att":
            k.barrier()
            k.finish()
            nc._in_names = in_names
            return nc

        with ExitStack() as ma:
            woa_r = k.ring(ma, "woa", [128, 8, 512], BF16, 2)
            g0_r = k.ring(ma, "g0t", [128, NO], BF16, 2)
            gb_r = k.ring(ma, "gbt", [128, NO], BF16, 2)
            mtmp_r = k.ring(ma, "mtmp", [128, 512], F32, 2)
            pm = k.ring(ma, "pm", [128, 512], F32, 4, psum=True)
            cur = {}
            for pi in range(4):
                wp = woa_r.next()
                k.dma("pool", wp[:, :, :], w_oa[:, pi * 512:(pi + 1) * 512].rearrange("(kc p) n -> p kc n", p=128), w=[wp])

                def cons_m(mi, bi, c0, n, p, pi=pi):
                    dc = 4 * pi + mi
                    if bi == 0:
                        cur["g0"] = g0_r.next(); cur["gb"] = gb_r.next()
                        k.dma("sp", cur["g0"][:, :], g0_d.t[dc, :, :], r=[g0_d], w=[cur["g0"]])
                        k.dma("sp", cur["gb"][:, :], gb_d.t[dc, :, :], r=[gb_d], w=[cur["gb"]])
                    g0t = cur["g0"]; gbt = cur["gb"]
                    tm = mtmp_r.next()
                    k.do("dve", lambda e: e.tensor_tensor(out=tm[:, :n], in0=p[:, :n], in1=g0t[:, c0:c0 + n], op=ALU.mult), r=[p, g0t], w=[tm])
                    k.do("pool", lambda e: e.tensor_tensor(out=mT[:, dc, c0:c0 + n], in0=tm[:, :n], in1=gbt[:, c0:c0 + n], op=ALU.add),
                         r=[tm, gbt], w=[mT])
                gemm_fm(wp, M4, 8, oT, BLK_O, pm, cons_m)
            k.barrier()
        oes.close()
        k.mark("mrga_done")
        G_S = [(0, 16, 1), (16, 16, 2), (32, 16, 3), (48, 16, 4)]
        with ExitStack() as mb_:
            wos = [k.sb(mb_, "wo%d" % i, [128, 16, 512], BF16) for i in range(4)]
            for pi in range(4):
                k.dma("pool", wos[pi][:, :, :], w_o[:, pi * 512:(pi + 1) * 512].rearrange("(kc p) n -> p kc n", p=128), w=[wos[pi]])
            fr = make_front(mb_, nx=2)
            GT_r = k.ring(mb_, "GT", [128, D], F32, 1)
            x1_r = k.ring(mb_, "x1", [128, D], F32, 2)
            pm = k.ring(mb_, "pm2", [128, 512], F32, 4, psum=True)
            for (t0, nt) in TILES_O:
                xt = fr["x"].next()
                k.dma("sp", xt[:nt, :], xown[t0:t0 + nt, :], w=[xt])
                GT = GT_r.next()
                if t0 < 1024:
                    k.dma("sp", GT[:nt, :], bc(modrows.t[0:1, 0:D], [nt, D]), r=[modrows], w=[GT])
                else:
                    for bb in range(4):
                        k.dma("sp", GT[16 * bb:16 * bb + 16, :], bc(modrows.t[1 + bb:2 + bb, 0:D], [16, D]), r=[modrows], w=[GT])
                x1 = x1_r.next()
                for dq in range(4):
                    p = pm.next()
                    for kc in range(16):
                        k.do("pe", lambda e: e.matmul(out=p[:nt, :], lhsT=mT[:, kc, t0:t0 + nt], rhs=wos[dq][:, kc, :],
                                                      start=(kc == 0), stop=(kc == 15)), r=[mT, wos[dq]], w=[p], inc=(kc == 15))
                    k.do("dve", lambda e: e.tensor_tensor(out=x1[:nt, dq * 512:(dq + 1) * 512], in0=p[:nt, :], in1=GT[:nt, dq * 512:(dq + 1) * 512],
                                                          op=ALU.mult), r=[p, GT], w=[x1])
                k.do("pool", lambda e: e.tensor_tensor(out=x1[:nt, :], in0=x1[:nt, :], in1=xt[:nt, :], op=ALU.add), r=[x1, xt], w=[x1])
                k.dma("sp", x1_d.t[t0:t0 + nt, :], x1[:nt, :], r=[x1], scratch=x1_d)
                front_from_sb(fr, x1, nt, A2, modT, 32, G_P if t0 < 1024 else G_S, h2T, t0)
            front_flush(fr)
            k.barrier()
        mes.close()
        k.mark("mrg_done")

        if stop_after == "mrg":
            with ExitStack() as fz:
                final_phase(fz, None)
            k.barrier()
            k.finish()
            nc._in_names = in_names
            return nc

        with ExitStack() as pes:
            aT = k.sb(pes, "aT", [128, NO], F32)
            bT = k.sb(pes, "bT", [128, NO], F32)
            gT = k.sb(pes, "gT", [128, NO], F32)
            io_i = k.sb(pes, "io_i", [128, 128], I32)
            io128 = k.sb(pes, "io128", [128, 128], F32)
            k.do("pool", lambda e: e.iota(io_i[:, :], pattern=[[1, 128]], base=0, channel_multiplier=0), w=[io_i])
            k.do("dve", lambda e: e.tensor_copy(out=io128[:, :], in_=io_i[:, :]), r=[io_i], w=[io128])
            with ExitStack() as pq:
                qpT = k.sb(pq, "qpT", [128, 16, NO], BF16)
                wq_r = k.ring(pq, "wpqpan", [128, 16, 512], BF16, 2)
                psq = k.ring(pq, "psq", [128, 512], F32, 4, psum=True)
                ptq = k.ring(pq, "ptq", [128, 4, 128], BF16, 2, psum=True)
                for pi in range(4):
                    wp = wq_r.next()
                    k.dma("pool", wp[:, :, :], w_pq[:, pi * 512:(pi + 1) * 512].rearrange("(kc p) n -> p kc n", p=128), w=[wp])

                    def cons_q(mi, bi, c0, n, p, pi=pi):
                        k.do("act", lambda e: e.copy(out=qpT[:, 4 * pi + mi, c0:c0 + n], in_=p[:, :n]), r=[p], w=[qpT])
                    gemm_fm(wp, M4, 16, h2T, BLK_O, psq, cons_q)
                subkT = k.sb(pq, "subkT", [128, 16, 128], BF16)
                skr = k.ring(pq, "skr", [128, 8, 128], BF16, 2)
                for which, sk in enumerate((sub_k1, sub_k2)):
                    s_ = skr.next()
                    k.dma("pool", s_[:, :, :], sk.rearrange("(h n) d -> n h d", n=128), w=[s_])
                    for half in range(2):
                        p = ptq.next()
                        for j in range(4):
                            h = half * 4 + j
                            k.do("pe", lambda e: e.transpose(out=p[:, j, :], in_=s_[:, h, :], identity=ident[:, :]), r=[s_, ident], w=[p], inc=(j == 3))
                        for j in range(4):
                            h = half * 4 + j
                            k.do("act", lambda e: e.copy(out=subkT[:, 2 * h + which, :], in_=p[:, j, :]), r=[p], w=[subkT])
                sc_r = k.ring(pq, "sc", [128, 16, 128], F32, 2)
                sc2a = [k.sb(pq, "sc2a%d" % i, [128, 128], F32) for i in range(16)]
                v16a = [k.sb(pq, "v16a%d" % i, [128, 8], F32) for i in range(16)]
                v16b = [k.sb(pq, "v16b%d" % i, [128, 8], F32) for i in range(16)]
                ixa = [k.sb(pq, "ixa%d" % i, [128, 8], U32) for i in range(16)]
                ixb = [k.sb(pq, "ixb%d" % i, [128, 8], U32) for i in range(16)]
                cand2a = [k.sb(pq, "cand2a%d" % i, [128, 256], F32) for i in range(8)]
                sva = [k.sb(pq, "sva%d" % i, [128, 8], F32) for i in range(8)]
                svb = [k.sb(pq, "svb%d" % i, [128, 8], F32) for i in range(8)]
                cia = [k.sb(pq, "cia%d" % i, [128, 8], U32) for i in range(8)]
                cib = [k.sb(pq, "cib%d" % i, [128, 8], U32) for i in range(8)]
                v16 = k.sb(pq, "v16", [128, 16, 16], F32)
                ix = k.sb(pq, "ix", [128, 16, 16], U32)
                ixf = k.sb(pq, "ixf", [128, 16, 16], F32)
                cand = k.sb(pq, "cand", [128, 8, 256], F32)
                sv = k.sb(pq, "sv", [128, 8, 16], F32)
                ci = k.sb(pq, "ci", [128, 8, 16], U32)
                sl_i = k.sb(pq, "sl_i", [128, 2, 128], U32)
                sl_f = k.sb(pq, "sl_f", [128, 2, 128], F32)
                eqs = [k.sb(pq, "eq%d" % i, [128, 8, 16, 16], F32) for i in range(2)]
                sel = k.sb(pq, "sel", [128, 3, 128], F32)
                ex = k.sb(pq, "ex", [128, 128], F32)
                zz = k.sb(pq, "zz", [128, 8], F32)
                rz = k.sb(pq, "rz", [128, 8], F32)
                pst = k.ring(pq, "pst", [128, 512], F32, 1, psum=True)
                for (t0, nt) in TILES_O:
                    sc = sc_r.next()
                    for q4 in range(4):
                        p = psq.next()
                        for j in range(4):
                            gi_ = q4 * 4 + j
                            k.do("pe", lambda e: e.matmul(out=p[:nt, j * 128:(j + 1) * 128], lhsT=qpT[:, gi_, t0:t0 + nt], rhs=subkT[:, gi_, :],
                                                          start=True, stop=True), r=[qpT, subkT], w=[p], inc=(j == 3))
                        k.do("act", lambda e: e.copy(out=sc[:nt, q4 * 4:(q4 + 1) * 4, :], in_=p[:nt, :].rearrange("p (a b) -> p a b", b=128)),
                             r=[p], w=[sc])
                    for gi_ in range(16):
                        k.do("dve", lambda e: e.max(out=v16[:nt, gi_, 0:8], in_=sc[:nt, gi_, :]), r=[sc], w=[v16], nowaw=True)
                    for gi_ in range(16):
                        k.do("dve", lambda e: e.max_index(out=ix[:nt, gi_, 0:8], in_max=v16[:nt, gi_, 0:8], in_values=sc[:nt, gi_, :]),
                             r=[sc, v16], w=[ix], nowaw=True)
                    for gi_ in range(16):
                        k.do("dve", lambda e: e.match_replace(out=sc2a[gi_][:nt, :], in_to_replace=v16[:nt, gi_, 0:8], in_values=sc[:nt, gi_, :],
                                                              imm_value=-1e30), r=[sc, v16], w=[sc2a[gi_]])
                    for gi_ in range(16):
                        k.do("dve", lambda e: e.max(out=v16[:nt, gi_, 8:16], in_=sc2a[gi_][:nt, :]), r=[sc2a[gi_]], w=[v16], nowaw=True)
                    for gi_ in range(16):
                        k.do("dve", lambda e: e.max_index(out=ix[:nt, gi_, 8:16], in_max=v16[:nt, gi_, 8:16], in_values=sc2a[gi_][:nt, :]),
                             r=[sc2a[gi_], v16], w=[ix], nowaw=True)
                    k.do("pool", lambda e: e.tensor_copy(out=ixf[:nt, :, :], in_=ix[:nt, :, :]), r=[ix], w=[ixf])
                    v4 = v16[:nt, :, :].rearrange("p (h w) a -> p h w a", w=2)
                    i4 = ixf[:nt, :, :].rearrange("p (h w) a -> p h w a", w=2)
                    S4 = [nt, 8, 16, 16]
                    k.do("pool", lambda e: e.tensor_tensor(out=cand[:nt, :, :].rearrange("p h (a b) -> p h a b", b=16),
                                                          in0=bc(v4[:, :, 0, :].unsqueeze(3), S4), in1=bc(v4[:, :, 1, :].unsqueeze(2), S4), op=ALU.add),
                         r=[v16], w=[cand])
                    for h in range(8):
                        k.do("dve", lambda e: e.max(out=sv[:nt, h, 0:8], in_=cand[:nt, h, :]), r=[cand], w=[sv], nowaw=True)
                    for h in range(8):
                        k.do("dve", lambda e: e.max_index(out=ci[:nt, h, 0:8], in_max=sv[:nt, h, 0:8], in_values=cand[:nt, h, :]), r=[cand, sv], w=[ci], nowaw=True)
                    for h in range(8):
                        k.do("dve", lambda e: e.match_replace(out=cand2a[h][:nt, :], in_to_replace=sv[:nt, h, 0:8], in_values=cand[:nt, h, :],
                                                              imm_value=-1e30), r=[cand, sv], w=[cand2a[h]])
                    for h in range(8):
                        k.do("dve", lambda e: e.max(out=sv[:nt, h, 8:16], in_=cand2a[h][:nt, :]), r=[cand2a[h]], w=[sv], nowaw=True)
                    for h in range(8):
                        k.do("dve", lambda e: e.max_index(out=ci[:nt, h, 8:16], in_max=sv[:nt, h, 8:16], in_values=cand2a[h][:nt, :]),
                             r=[cand2a[h], sv], w=[ci], nowaw=True)
                    civ = ci[:nt, :, :].rearrange("p h k -> p (h k)")
                    k.do("dve", lambda e: e.tensor_single_scalar(out=sl_i[:nt, 0, :], in_=civ, scalar=4, op=ALU.logical_shift_right), r=[ci], w=[sl_i])
                    k.do("dve", lambda e: e.tensor_single_scalar(out=sl_i[:nt, 1, :], in_=civ, scalar=15, op=ALU.bitwise_and), r=[ci], w=[sl_i])
                    k.do("dve", lambda e: e.tensor_copy(out=sl_f[:nt, :, :], in_=sl_i[:nt, :, :]), r=[sl_i], w=[sl_f])
                    for w_ in range(2):
                        eq = eqs[w_]
                        slv = sl_f[:nt, w_, :].rearrange("p (h k) -> p h k", k=16)
                        k.do("dve", lambda e: e.tensor_tensor(out=eq[:nt], in0=bc(slv.unsqueeze(3), S4),
                                                              in1=bc(io128[:nt, 0:16].unsqueeze(1).unsqueeze(1), S4), op=ALU.is_equal),
                             r=[sl_f, io128], w=[eq])
                        k.do("pool", lambda e: e.tensor_tensor(out=eq[:nt], in0=eq[:nt], in1=bc(i4[:, :, w_, :].unsqueeze(2), S4), op=ALU.mult),
                             r=[eq, ixf], w=[eq])
                        k.do("dve", lambda e: e.tensor_reduce(out=sel[:nt, w_, :].rearrange("p (h k) -> p h k", k=16), in_=eq[:nt],
                                                              axis=AX.X, op=ALU.add), r=[eq], w=[sel])
                    k.do("dve", lambda e: e.tensor_tensor(out=ex[:nt, :].rearrange("p (h k) -> p h k", k=16), in0=sv[:nt, :, :],
                                                          in1=bc(sv[:nt, :, 0:1], [nt, 8, 16]), op=ALU.subtract), r=[sv], w=[ex])
                    k.do("act", lambda e: e.activation(out=ex[:nt, :], in_=ex[:nt, :], func=AF.Exp), r=[ex], w=[ex])
                    k.do("dve", lambda e: e.tensor_reduce(out=zz[:nt, :], in_=ex[:nt, :].rearrange("p (h k) -> p h k", k=16), axis=AX.X, op=ALU.add),
                         r=[ex], w=[zz])
                    k.do("dve", lambda e: e.reciprocal(out=rz[:nt, :], in_=zz[:nt, :]), r=[zz], w=[rz])
                    k.do("dve", lambda e: e.tensor_tensor(out=sel[:nt, 2, :].rearrange("p (h k) -> p h k", k=16),
                                                          in0=ex[:nt, :].rearrange("p (h k) -> p h k", k=16),
                                                          in1=bc(rz[:nt, :].unsqueeze(2), [nt, 8, 16]), op=ALU.mult), r=[ex, rz], w=[sel])
                    p = pst.next()
                    for w_ in range(3):
                        k.do("pe", lambda e: e.transpose(out=p[:, w_ * 128:w_ * 128 + nt], in_=sel[:nt, w_, :], identity=identf[:nt, :nt]),
                             r=[sel, identf], w=[p], inc=(w_ == 2))
                    for w_, dst in enumerate((aT, bT, gT)):
                        k.do("act", lambda e: e.copy(out=dst[:, t0:t0 + nt], in_=p[:, w_ * 128:w_ * 128 + nt]), r=[p], w=[dst])
                k.barrier()
            k.mark("peer_topk_done")
            with ExitStack() as pg_:
                Gst_r = k.ring(pg_, "Gst", [128, 128, 128], BF16, 2)
                P1_r = k.ring(pg_, "P1h", [128, 16, 128], BF16, 2)
                Qe_r = k.ring(pg_, "Qe", [128, 16, 128], BF16, 2)
                Q2_r = k.ring(pg_, "Q2g", [128, 16, 128], BF16, 2)
                psG = k.ring(pg_, "psG", [128, 4, 128], F32, 4, psum=True)
                S3 = [128, 16, 128]
                io_bf = k.sb(pg_, "io_bf", [128, 128], BF16)
                a_bf = k.sb(pg_, "a_bf", [128, NO], BF16)
                b_bf = k.sb(pg_, "b_bf", [128, NO], BF16)
                g_bf = k.sb(pg_, "g_bf", [128, NO], BF16)
                k.do("dve", lambda e: e.tensor_copy(out=io_bf[:, :], in_=io128[:, :]), r=[io128], w=[io_bf])
                k.do("dve", lambda e: e.tensor_copy(out=a_bf[:, :], in_=aT[:, :]), r=[aT], w=[a_bf])
                k.do("dve", lambda e: e.tensor_copy(out=b_bf[:, :], in_=bT[:, :]), r=[bT], w=[b_bf])
                k.do("dve", lambda e: e.tensor_copy(out=g_bf[:, :], in_=gT[:, :]), r=[gT], w=[g_bf])
                for (t0, nt) in TILES_O:
                    Gs = Gst_r.next()
                    for t16 in range(nt // 16):
                        tb = t0 + 16 * t16
                        P1 = P1_r.next(); Qe = Qe_r.next(); Q2 = Q2_r.next()
                        k.do("dve", lambda e: e.tensor_tensor(out=P1[:, :, :], in0=bc(io_bf[:, :].unsqueeze(1), S3),
                                                              in1=bc(a_bf[:, tb:tb + 16].unsqueeze(2), S3), op=ALU.is_equal), r=[io_bf, a_bf], w=[P1])
                        k.do("dve", lambda e: e.tensor_tensor(out=Qe[:, :, :], in0=bc(io_bf[:, :].unsqueeze(1), S3),
                                                              in1=bc(b_bf[:, tb:tb + 16].unsqueeze(2), S3), op=ALU.is_equal), r=[io_bf, b_bf], w=[Qe])
                        k.do("pool", lambda e: e.tensor_tensor(out=Q2[:, :, :], in0=Qe[:, :, :],
                                                               in1=bc(g_bf[:, tb:tb + 16].unsqueeze(2), S3), op=ALU.mult), r=[Qe, g_bf], w=[Q2])
                        for j4 in range(4):
                            p = psG.next()
                            for j in range(4):
                                jj = j4 * 4 + j
                                k.do("pe", lambda e: e.matmul(out=p[:, j, :], lhsT=Q2[:, jj, :], rhs=P1[:, jj, :], start=True, stop=True),
                                     r=[Q2, P1], w=[p], inc=(j == 3))
                            tl = 16 * t16 + 4 * j4
                            k.do("act", lambda e: e.copy(out=Gs[:, :, tl:tl + 4], in_=p[:, :, :].rearrange("p t i -> p i t")), r=[p], w=[Gs])
                    for q4 in range(4):
                        k.dma("sp", G_d.t[:, q4 * 32:(q4 + 1) * 32, t0:t0 + nt], Gs[:, q4 * 32:(q4 + 1) * 32, 0:nt], r=[Gs], scratch=G_d)
                k.barrier()
            k.mark("peer_G_done")
            acc = [k.sb(pes, "acc%d" % i, [128, D], F32) for i in range(9)]
            with ExitStack() as pm_:
                wur = k.ring(pm_, "wur", [128, D], BF16, 2)
                wuT_r = k.ring(pm_, "wuT", [128, 16, 128], BF16, 3)
                wvr = k.ring(pm_, "wvr", [128, D], BF16, 8)
                AT_r = k.ring(pm_, "AT", [128, 4, NO], BF16, 2)
                gtc = k.ring(pm_, "gtc", [128, NO], BF16, 3)
                gl_r = k.ring(pm_, "gl", [128, NO], BF16, 2)
                psT = k.ring(pm_, "psT", [128, 8, 128], BF16, 2, psum=True)
                psU = k.ring(pm_, "psU", [128, 512], F32, 3, psum=True)
                psD = k.ring(pm_, "psD", [128, 512], F32, 3, psum=True)
                nev = [0]

                def emit_U(gi):
                    AT = AT_r.next()
                    wvs = []
                    for ec in range(4):
                        i1 = 4 * gi + ec
                        raw = wur.next()
                        k.dma("pool", raw[:, :], w_u[i1 * 128:(i1 + 1) * 128, :], w=[raw])
                        gt = gtc.next()
                        k.dma("sp", gt[:, :], G_d.t[:, i1, :], r=[G_d], w=[gt])
                        wT = wuT_r.next()
                        for g4 in range(2):
                            p = psT.next()
                            for j in range(8):
                                dc = g4 * 8 + j
                                k.do("pe", lambda e: e.transpose(out=p[:, j, :], in_=raw[:, dc * 128:(dc + 1) * 128], identity=ident[:, :]),
                                     r=[raw, ident], w=[p], inc=(j == 7))
                            nev[0] += 1
                            if nev[0] % 2:
                                k.do("act", lambda e: e.copy(out=wT[:, g4 * 8:(g4 + 1) * 8, :], in_=p[:, :, :]), r=[p], w=[wT])
                            else:
                                k.do("dve", lambda e: e.tensor_copy(out=wT[:, g4 * 8:(g4 + 1) * 8, :], in_=p[:, :, :]), r=[p], w=[wT])
                        gl = gl_r.next()
                        for (c0, n) in BLK_O:
                            pu = psU.next()
                            for dc in range(16):
                                k.do("pe", lambda e: e.matmul(out=pu[:, :n], lhsT=wT[:, dc, :], rhs=h2T[:, dc, c0:c0 + n], start=(dc == 0), stop=(dc == 15)),
                                     r=[wT, h2T], w=[pu], inc=(dc == 15))
                            k.do("act", lambda e: e.activation(out=gl[:, c0:c0 + n], in_=pu[:, :n], func=AF.Gelu), r=[pu], w=[gl])
                        k.do("dve", lambda e: e.tensor_tensor(out=AT[:, ec, :], in0=gl[:, :], in1=gt[:, :], op=ALU.mult), r=[gl, gt], w=[AT])
                        wv = wvr.next()
                        k.dma("pool", wv[:, :], w_v[i1 * 128:(i1 + 1) * 128, :], w=[wv])
                        wvs.append(wv)
                    return AT, wvs

                def emit_down(gi, AT, wvs):
                    for ti, (t0, nt) in enumerate(TILES_O):
                        for dq in range(4):
                            pd = psD.next()
                            for ec in range(4):
                                k.do("pe", lambda e: e.matmul(out=pd[:nt, :], lhsT=AT[:, ec, t0:t0 + nt], rhs=wvs[ec][:, dq * 512:(dq + 1) * 512],
                                                              start=(ec == 0), stop=(ec == 3)), r=[AT, wvs[ec]], w=[pd], inc=(ec == 3))
                            a = acc[ti]
                            if gi == 0:
                                k.do("act", lambda e: e.copy(out=a[:nt, dq * 512:(dq + 1) * 512], in_=pd[:nt, :]), r=[pd], w=[a])
                            else:
                                k.do("dve", lambda e: e.tensor_tensor(out=a[:nt, dq * 512:(dq + 1) * 512], in0=pd[:nt, :],
                                                                      in1=a[:nt, dq * 512:(dq + 1) * 512], op=ALU.add), r=[pd, a], w=[a])

                prev = None
                for gi in range(32):
                    cur = emit_U(gi)
                    if prev is not None:
                        emit_down(gi - 1, *prev)
                    prev = cur
                emit_down(31, *prev)
                k.barrier()
            k.mark("peer_main_done")
            with ExitStack() as fz:
                final_phase(fz, acc)
            k.barrier()
        k.barrier()
        k.finish()
    nc._in_names = in_names
    nc._ninstr = k.ninstr
    return nc


def _rope_tables(pos):
    half = 32
    inv = 1.0 / (10000.0 ** (np.arange(half, dtype=np.float32) / half))
    ang = pos.astype(np.float32)[:, None] * inv[None, :].astype(np.float32)
    cos = np.cos(ang).astype(np.float32).T
    sin = np.sin(ang).astype(np.float32).T
    cosT = np.concatenate([cos, cos], axis=0)
    sinT = np.concatenate([-sin, sin], axis=0)
    return np.ascontiguousarray(cosT), np.ascontiguousarray(sinT)


def _fm(v, nchunk):
    return np.ascontiguousarray(np.asarray(v, np.float32).reshape(nchunk, 128).T)


_CACHE = {}


def prepare(inputs):
    f = lambda a: np.ascontiguousarray(np.asarray(a, dtype=np.float32))
    xp = f(inputs["x_prompt"])[0]
    xs = f(inputs["x_sample"])
    shared = {
        "xall": xp,
        "w_ada": f(inputs["w_ada"])[0],
        "b_adaT": _fm(f(inputs["b_ada"])[0], 96),
        "b_ada": f(inputs["b_ada"]).reshape(1, -1),
        "g_n1T": _fm(f(inputs["g_n1"])[0], 16),
        "g_n2T": _fm(f(inputs["g_n2"])[0], 16),
        "g_qT": _fm(f(inputs["g_q"])[0], 4),
        "g_kvT": _fm(f(inputs["g_kv"])[0], 4),
        "w_in": f(inputs["w_in"])[0],
        "w_uq": f(inputs["w_uq"])[0].reshape(512, 1536),
        "w_uk": f(inputs["w_uk"])[0].reshape(512, 1024),
        "w_uv": f(inputs["w_uv"])[0].reshape(512, 1024),
        "w_oa": f(inputs["w_oa"])[0],
        "w_ob": f(inputs["w_ob"])[0],
        "w_o": f(inputs["w_o"])[0],
        "w_pq": f(inputs["w_pq"])[0],
        "w_convT": np.ascontiguousarray(f(inputs["w_conv"])[0].reshape(3, 8, 128).transpose(2, 1, 0)),
        "b_convT": _fm(f(inputs["b_conv"])[0], 8),
        "sub_k1": f(inputs["sub_k1"])[0].reshape(1024, 128),
        "sub_k2": f(inputs["sub_k2"])[0].reshape(1024, 128),
        "w_u": f(inputs["w_u"])[0],
        "w_v": f(inputs["w_v"])[0],
        "g_f": f(inputs["g_f"]).reshape(1, D),
    }
    cosk, sink = _rope_tables(np.arange(8192))
    shared["cosk"] = cosk
    shared["sink"] = sink
    cp = f(inputs["c_prompt"])
    cs = f(inputs["c_sample"])
    cache_ckv = f(inputs["cache_ckv"])[0]
    cache_kr = f(inputs["cache_krope"])[0]
    sconv = f(inputs["state_conv"])[0]
    maps = []
    for c in range(NCORES):
        m = dict(shared)
        lt = np.arange(1024)
        pos_own = (8 * (lt // 64) + c) * 64 + lt % 64
        xo = np.zeros((NT, D), np.float32)
        xo[:1024] = xp[pos_own]
        xo[1024:1088] = xs[4 * c:4 * c + 4].reshape(64, D)
        hv = np.zeros((128, 32), np.float32)
        for j in range(16):
            for i in range(2):
                p = (8 * j + c) * 64 - 2 + i
                if p >= 0:
                    xo[1088 + 2 * j + i] = xp[p]
                    hv[:, 2 * j + i] = 1.0
        m["xown"] = xo
        m["hvalid"] = hv
        c5 = np.concatenate([cp, cs[4 * c:4 * c + 4]], axis=0)
        m["c5T"] = np.ascontiguousarray(c5.reshape(5, 16, 128).transpose(2, 1, 0))
        m["cckv"] = np.ascontiguousarray(cache_ckv[4 * c:4 * c + 4])
        m["ckr"] = np.ascontiguousarray(cache_kr[4 * c:4 * c + 4])
        sc = sconv[4 * c:4 * c + 4].reshape(4, 2, 8, 128).transpose(3, 2, 0, 1)
        m["sconvT"] = np.ascontiguousarray(sc)
        posq = np.concatenate([pos_own, np.tile(1024 + np.arange(16), 4), np.zeros(32, np.int64)])
        cq, sq = _rope_tables(posq)
        m["cosq"] = cq
        m["sinq"] = sq
        dm = np.zeros((128, 4), np.float32)
        for kt in range(4):
            for half in range(2):
                if 2 * kt + half > c:
                    dm[64 * half:64 * half + 64, kt] = NEG
        m["dmask"] = dm
        maps.append(m)
    return maps


def assemble(results):
    y_p = np.zeros((1, 8192, D), np.float32)
    y_s = np.zeros((32, 16, D), np.float32)
    ckv_p = np.zeros((1, 1, 8192, 512), np.float32)
    kr_p = np.zeros((1, 1, 8192, 64), np.float32)
    conv_p = np.zeros((1, 1, 2, 1024), np.float32)
    ckv_s = np.zeros((1, 32, 16, 512), np.float32)
    kr_s = np.zeros((1, 32, 16, 64), np.float32)
    conv_s = np.zeros((1, 32, 2, 1024), np.float32)
    for c in range(NCORES):
        r = results[c]
        lt = np.arange(1024)
        pos_own = (8 * (lt // 64) + c) * 64 + lt % 64
        y_p[0, pos_own] = r["o_y"][:1024]
        y_s[4 * c:4 * c + 4] = r["o_y"][1024:1088].reshape(4, 16, D)
        ckv_p[0, 0, pos_own] = r["o_ckv"][:1024]
        ckv_s[0, 4 * c:4 * c + 4] = r["o_ckv"][1024:1088].reshape(4, 16, 512)
        kr_p[0, 0, pos_own] = r["o_kr"][:1024]
        kr_s[0, 4 * c:4 * c + 4] = r["o_kr"][1024:1088].reshape(4, 16, 64)
        if c == 7:
            conv_p[0, 0] = r["o_conv"][0:2]
        conv_s[0, 4 * c:4 * c + 4] = r["o_conv"][2:10].reshape(4, 2, 1024)
    return (y_p, y_s, ckv_p, kr_p, conv_p, ckv_s, kr_s, conv_s)


def kernel(**inputs):
    maps = prepare(inputs)
    if "nc" not in _CACHE:
        _CACHE["nc"] = build()
    nc = _CACHE["nc"]
    maps = [{n: m[n] for n in nc._in_names} for m in maps]
    res = run_bass_kernel_spmd(nc, maps, core_ids=list(range(NCORES)))
    return assemble(res.results)
```

```python
import numpy as np
import concourse.bass as bass
import concourse.mybir as mybir
from concourse.bass_utils import run_bass_kernel_spmd
from contextlib import ExitStack

F32 = mybir.dt.float32
BF16 = mybir.dt.bfloat16
U32 = mybir.dt.uint32
I32 = mybir.dt.int32
AF = mybir.ActivationFunctionType
ALU = mybir.AluOpType
AX = mybir.AxisListType

NCORES = 8
D = 2048
NT = 1120
NO = 1088
EPS = 1e-6
SCALE = 192.0 ** -0.5
NEG = -30000.0
BLK_T = [(0, 512), (512, 512), (1024, 96)]
BLK_O = [(0, 512), (512, 512), (1024, 64)]
TILES_O = [(i * 128, 128) for i in range(8)] + [(1024, 64)]


class Tl:
    __slots__ = ("t", "w", "r", "dsem", "ssem", "name")

    def __init__(self, t, name):
        self.t = t
        self.name = name
        self.w = None
        self.r = []
        self.dsem = None
        self.ssem = None

    def __getitem__(self, k):
        return self.t[k]


class DSem:
    def __init__(self, sem):
        self.sem = sem
        self.issued = 0


class Eng:
    def __init__(self, name, h, sem):
        self.name = name
        self.h = h
        self.sem = sem
        self.count = 0
        self.waited = {}


class Ring:
    def __init__(self, tiles):
        self.tiles = tiles
        self.i = 0

    def next(self):
        t = self.tiles[self.i % len(self.tiles)]
        self.i += 1
        return t


class K:
    def __init__(self, nc, es):
        self.nc = nc
        self.es = es
        self.eng = {}
        for name, h in (("pe", nc.tensor), ("act", nc.scalar), ("dve", nc.vector),
                        ("pool", nc.gpsimd), ("sp", nc.sync)):
            sem = es.enter_context(nc.semaphore("prog_" + name))
            self.eng[name] = Eng(name, h, sem)
        self.dsems = []
        self.store_sems = []
        self.ninstr = 0
        self.uid = 0
        import os
        self.limit = int(os.environ.get("KLIMIT", "100000000"))

    def sb(self, es, name, shape, dt):
        self.uid += 1
        nm = "%s_%d" % (name, self.uid)
        return Tl(es.enter_context(self.nc.sbuf_tensor(nm, shape, dt)), nm)

    def ps(self, es, name, shape, dt):
        self.uid += 1
        nm = "%s_%d" % (name, self.uid)
        return Tl(es.enter_context(self.nc.psum_tensor(nm, shape, dt)), nm)

    def ring(self, es, name, shape, dt, n, psum=False):
        f = self.ps if psum else self.sb
        return Ring([f(es, name, shape, dt) for _ in range(n)])

    def dram(self, name, shape, dt):
        t = self.nc.dram_tensor(name, shape, dt, kind="Internal").ap()
        return Tl(t, name)

    def newsem(self, name):
        self.uid += 1
        ds = DSem(self.es.enter_context(self.nc.semaphore("%s_%d" % (name[:20], self.uid))))
        self.dsems.append(ds)
        return ds

    def getsem(self, name):
        if not hasattr(self, "sem_pool"):
            self.sem_pool = []
            self.sem_rr = 0
        if len(self.sem_pool) < 72:
            self.sem_pool.append(self.newsem(name))
            return self.sem_pool[-1]
        self.sem_rr += 1
        return self.sem_pool[self.sem_rr % len(self.sem_pool)]

    def _wait(self, E, tok):
        if tok is None:
            return
        if tok[0] == "e":
            _, P, val = tok
            if P is E and E.name in ("pe", "sp"):
                return
            sem = P.sem
        else:
            _, ds, val = tok
            sem = ds.sem
            val = max(val, ds.issued)
        key = id(sem)
        if E.waited.get(key, 0) >= val:
            return
        E.waited[key] = val
        E.h.wait_ge(sem, val)

    def _deps(self, E, r, w):
        for t in r:
            self._wait(E, t.w)
        for t in w:
            self._wait(E, t.w)
            for tok in t.r:
                self._wait(E, tok)

    def do(self, en, fn, r=(), w=(), inc=True, nowaw=False):
        if self.ninstr >= self.limit:
            return None
        E = self.eng[en]
        if nowaw:
            for t in w:
                assert t.w is None or t.w[0] != "e" or t.w[1] is E or not t.r or True
            self._deps(E, r, ())
            for t in w:
                for tok in t.r:
                    self._wait(E, tok)
                if t.w is not None and not (t.w[0] == "e" and t.w[1] is E):
                    self._wait(E, t.w)
        else:
            self._deps(E, r, w)
        ins = fn(E.h)
        self.ninstr += 1
        if inc:
            E.count += 1
            ins.then_inc(E.sem, 1)
            tok = ("e", E, E.count)
        else:
            tok = ("e", E, E.count + 1)
        for t in r:
            t.r.append(tok)
        for t in w:
            t.w = tok
            t.r = []
        return tok

    def dma(self, q, out, in_, r=(), w=(), store=False, scratch=None, **kw):
        if self.ninstr >= self.limit:
            return None
        E = self.eng[q]
        if scratch is not None:
            self._deps(E, r, ())
            if scratch.dsem is None:
                scratch.dsem = self.newsem("sc_" + scratch.name)
            ds = scratch.dsem
        elif store:
            self._deps(E, r, ())
            src = r[0]
            if src.ssem is None:
                src.ssem = self.newsem("st_" + src.name)
                self.store_sems.append(src.ssem)
            ds = src.ssem
        else:
            self._deps(E, r, w)
            dst = w[0]
            if dst.dsem is None:
                dst.dsem = self.getsem("ld_" + dst.name)
            ds = dst.dsem
        ins = E.h.dma_start(out=out, in_=in_, **kw)
        self.ninstr += 1
        ds.issued += 16
        ins.then_inc(ds.sem, 16)
        tok = ("d", ds, ds.issued)
        for t in r:
            t.r.append(tok)
        for t in w:
            t.w = tok
            t.r = []
        if scratch is not None:
            scratch.w = tok
        return tok

    def mark(self, name):
        import os
        if os.environ.get("KVERBOSE"):
            print("MARK", name, self.ninstr, flush=True)

    def barrier(self):
        for E in self.eng.values():
            for P in self.eng.values():
                if P is E or P.name == "sp" or P.count == 0:
                    continue
                if E.waited.get(id(P.sem), 0) < P.count:
                    E.waited[id(P.sem)] = P.count
                    E.h.wait_ge(P.sem, P.count)
            for ds in self.dsems:
                if ds.issued and E.waited.get(id(ds.sem), 0) < ds.issued:
                    E.waited[id(ds.sem)] = ds.issued
                    E.h.wait_ge(ds.sem, ds.issued)

    def finish(self):
        E = self.eng["sp"]
        for ds in self.store_sems:
            E.h.wait_ge(ds.sem, ds.issued)


def bc(ap, shape):
    return ap.broadcast_to(shape)


STAGES = ["p1b", "p1", "kp", "att", "mrg", "all"]


def build(stop_after="all", dbg=False):
    nc = bass.Bass("TRN2", target_bir_lowering=False)
    in_names = []
    nc_in_names = in_names

    def need(stage):
        return STAGES.index(stop_after) >= STAGES.index(stage)

    BIG = {"xall": "kp", "w_u": "all", "w_v": "all", "w_pq": "all", "w_o": "mrg", "w_oa": "mrg"}

    def din(name, shape, dt=F32):
        if name in BIG and not need(BIG[name]):
            return None
        in_names.append(name)
        return nc.dram_tensor(name, shape, dt, kind="ExternalInput").ap()

    def dout(name, shape, dt=F32):
        return nc.dram_tensor(name, shape, dt, kind="ExternalOutput").ap()

    xown = din("xown", [NT, D])
    xall = din("xall", [8192, D])
    c5T = din("c5T", [128, 16, 5])
    w_ada = din("w_ada", [D, 6 * D])
    b_adaT = din("b_adaT", [128, 96])
    b_ada = din("b_ada", [1, 6 * D])
    g_n1T = din("g_n1T", [128, 16])
    g_n2T = din("g_n2T", [128, 16])
    g_qT = din("g_qT", [128, 4])
    g_kvT = din("g_kvT", [128, 4])
    w_in = din("w_in", [D, 8256])
    w_uq = din("w_uq", [512, 1536])
    w_uk = din("w_uk", [512, 1024])
    w_uv = din("w_uv", [512, 1024])
    w_oa = din("w_oa", [1024, D])
    w_ob = din("w_ob", [1024, D])
    w_o = din("w_o", [D, D])
    w_pq = din("w_pq", [D, D])
    w_convT = din("w_convT", [128, 8, 3])
    b_convT = din("b_convT", [128, 8])
    sub_k1 = din("sub_k1", [1024, 128])
    sub_k2 = din("sub_k2", [1024, 128])
    w_u = din("w_u", [16384, D])
    w_v = din("w_v", [16384, D])
    g_f = din("g_f", [1, D])
    cckv = din("cckv", [4, 1024, 512])
    ckr = din("ckr", [4, 1024, 64])
    sconvT = din("sconvT", [128, 8, 4, 2])
    cosq = din("cosq", [64, NT])
    sinq = din("sinq", [64, NT])
    cosk = din("cosk", [64, 8192])
    sink = din("sink", [64, 8192])
    dmask = din("dmask", [128, 4])
    hvalid = din("hvalid", [128, 32])

    o_y = dout("o_y", [NO, D])
    o_ckv = dout("o_ckv", [NO, 512])
    o_kr = dout("o_kr", [NO, 64])
    o_conv = dout("o_conv", [10, 1024])

    with ExitStack() as es:
        k = K(nc, es)
        modrows = k.dram("modrows", [5, 2 * D], F32)
        g0_d = k.dram("g0_d", [16, 128, NO], BF16)
        gb_d = k.dram("gb_d", [16, 128, NO], BF16)
        KT_d = k.dram("KT_d", [8, 16, 128, 512], BF16)
        V_d = k.dram("V_d", [8, 16, 128, 512], BF16)
        KTs_d = k.dram("KTs_d", [4, 8, 128, 1040], BF16)
        Vs_d = k.dram("Vs_d", [4, 8, 128, 9 * 128], BF16)
        x1_d = k.dram("x1_d", [NO, D], F32)
        G_d = k.dram("G_d", [128, 128, NO], BF16)
        cq_d = k.dram("cq_d", [128, 4, NO], BF16)
        krT_d = k.dram("krT_d", [64, 8192], BF16)
        krc_d = k.dram("krc_d", [64, 4, 1040], BF16)
        oT_d = k.dram("oT_d", [128, 8, NO], BF16)

        identf = k.sb(es, "identf", [128, 128], F32)
        ident = k.sb(es, "ident", [128, 128], BF16)
        ones = k.sb(es, "ones", [128, 128], BF16)
        k.do("pool", lambda e: e.memset(identf[:, :], 0.0), w=[identf])
        k.do("pool", lambda e: e.affine_select(out=identf[:, :], in_=identf[:, :], pattern=[[-1, 128]],
                                               compare_op=ALU.not_equal, fill=1.0, base=0, channel_multiplier=1),
             r=[identf], w=[identf])
        k.do("dve", lambda e: e.tensor_copy(out=ident[:, :], in_=identf[:, :]), r=[identf], w=[ident])
        k.do("dve", lambda e: e.memset(ones[:, :], 1.0), w=[ones])

        def ld(es_, name, shape, src, dt=F32, q="sp"):
            t = k.sb(es_, name, shape, dt)
            k.dma(q, t.t[tuple(slice(None) for _ in shape)], src, w=[t])
            return t

        c5f = ld(es, "c5f", [128, 16, 5], c5T[:, :, :])
        c5b = k.sb(es, "c5b", [128, 16, 5], BF16)
        k.do("dve", lambda e: e.tensor_copy(out=c5b[:, :, :], in_=c5f[:, :, :]), r=[c5f], w=[c5b])
        badT = ld(es, "badT", [128, 96], b_adaT[:, :])
        gn1 = ld(es, "gn1", [128, 16], g_n1T[:, :])
        gn2 = ld(es, "gn2", [128, 16], g_n2T[:, :])
        gq = ld(es, "gq", [128, 4], g_qT[:, :])
        gkv = ld(es, "gkv", [128, 4], g_kvT[:, :])
        modT = k.sb(es, "modT", [128, 64, 5], F32)
        A1 = k.sb(es, "A1", [128, 16, 5], F32)
        A2 = k.sb(es, "A2", [128, 16, 5], F32)

        fm_panels = {0: 0, 1: 4, 2: 8, 3: 12, 4: 16, 5: 20, 6: 24, 7: 28,
                     12: 32, 13: 36, 14: 40, 15: 44, 16: 48, 17: 52, 18: 56, 19: 60}

        def make_ada(aes):
            return {"wpan": k.ring(aes, "adapan", [128, 16, 1024], BF16, 2),
                    "ps": k.ring(aes, "psada", [128, 512], F32, 2, psum=True),
                    "b5": k.ring(aes, "b5", [5, 512], F32, 2),
                    "rowst": k.ring(aes, "rowst", [5, 512], F32, 2)}

        def ada_big(ad, bj):
            wp = ad["wpan"].next()
            k.dma("pool", wp[:, :, :], w_ada[:, bj * 1024:(bj + 1) * 1024].rearrange("(kc p) n -> p kc n", p=128), w=[wp])
            for sub in range(2):
                pi = 2 * bj + sub
                wo_ = sub * 512
                p = ad["ps"].next()
                if pi in fm_panels:
                    base = fm_panels[pi]
                    for m in range(4):
                        for kc in range(16):
                            k.do("pe", lambda e: e.matmul(out=p[:, m * 8:m * 8 + 5], lhsT=wp[:, kc, wo_ + m * 128:wo_ + (m + 1) * 128],
                                                          rhs=c5b[:, kc, :], start=(kc == 0), stop=(kc == 15)),
                                 r=[wp, c5b], w=[p], inc=(kc == 15 and m == 3))
                    for m in range(4):
                        cc = pi * 4 + m
                        k.do("dve", lambda e: e.tensor_scalar(out=modT[:, base + m, :], in0=p[:, m * 8:m * 8 + 5],
                                                              scalar1=badT[:, cc:cc + 1], scalar2=None, op0=ALU.add),
                             r=[p, badT], w=[modT])
                else:
                    for kc in range(16):
                        k.do("pe", lambda e: e.matmul(out=p[:5, :], lhsT=c5b[:, kc, :], rhs=wp[:, kc, wo_:wo_ + 512],
                                                      start=(kc == 0), stop=(kc == 15)),
                             r=[wp, c5b], w=[p], inc=(kc == 15))
                    bt = ad["b5"].next()
                    k.dma("sp", bt[:, :], bc(b_ada[0:1, pi * 512:(pi + 1) * 512], [5, 512]), w=[bt])
                    rs = ad["rowst"].next()
                    k.do("dve", lambda e: e.tensor_tensor(out=rs[:, :], in0=p[:5, :], in1=bt[:, :], op=ALU.add),
                         r=[p, bt], w=[rs])
                    co = (pi - 8) * 512 if pi < 12 else D + (pi - 20) * 512
                    k.dma("sp", modrows.t[:, co:co + 512], rs[:, :], r=[rs], scratch=modrows)

        def ada_finish(A, g, o):
            k.do("dve", lambda e: e.tensor_scalar(out=A[:, :, :], in0=modT[:, o:o + 16, :], scalar1=1.0, scalar2=None,
                                                  op0=ALU.add), r=[modT], w=[A])
            k.do("dve", lambda e: e.tensor_tensor(out=A[:, :, :], in0=A[:, :, :],
                                                  in1=bc(g[:, :].unsqueeze(2), [128, 16, 5]), op=ALU.mult),
                 r=[A, g], w=[A])

        with ExitStack() as pa:
            ad = make_ada(pa)
            for bj in range(4):
                ada_big(ad, bj)
            ada_finish(A1, gn1, 16)
            k.barrier()

        def make_front(fes, nx=3):
            fr = {}
            fr["x"] = k.ring(fes, "xt", [128, D], F32, nx)
            fr["xn"] = k.ring(fes, "xn", [128, D], BF16, 2)
            fr["junk"] = k.sb(fes, "junk", [128, D], BF16)
            fr["ss"] = k.ring(fes, "ss", [128, 1], F32, 4)
            fr["sd"] = k.ring(fes, "sd", [128, 1], F32, 4)
            fr["rs"] = k.ring(fes, "rs", [128, 1], F32, 4)
            fr["pt"] = k.ring(fes, "ptr", [128, 4, 128], BF16, 3, psum=True)
            fr["n"] = 0
            fr["pend"] = None
            return fr

        def front_from_sb(fr, xt, ntok, A, Bt, Bo, groups, hT, col0):
            ss = fr["ss"].next(); sd = fr["sd"].next(); rs = fr["rs"].next()
            junk = fr["junk"]
            k.do("act", lambda e: e.activation(out=junk[:ntok, :], in_=xt[:ntok, :], func=AF.Square,
                                               accum_out=ss[:ntok, 0:1]), r=[xt], w=[junk, ss])
            k.do("act", lambda e: e.activation(out=sd[:ntok, :], in_=ss[:ntok, :], func=AF.Sqrt, scale=1.0 / D, bias=EPS),
                 r=[ss], w=[sd])
            k.do("dve", lambda e: e.reciprocal(out=rs[:ntok, :], in_=sd[:ntok, :]), r=[sd], w=[rs])
            xn = fr["xn"].next()
            k.do("pool", lambda e: e.tensor_scalar(out=xn[:ntok, :], in0=xt[:ntok, :], scalar1=rs[:ntok, 0:1], scalar2=1.0,
                                                   op0=ALU.mult, op1=ALU.mult), r=[xt, rs], w=[xn])
            prevB = fr.get("pend")
            fr["pend"] = lambda: front_B(fr, xn, ntok, A, Bt, Bo, groups, hT, col0)
            if prevB is not None:
                prevB()

        def front_flush(fr):
            if fr.get("pend") is not None:
                fr["pend"]()
                fr["pend"] = None

        def front_B(fr, xn, ntok, A, Bt, Bo, groups, hT, col0):
            for g4 in range(4):
                p = fr["pt"].next()
                for j in range(4):
                    kc = g4 * 4 + j
                    k.do("pe", lambda e: e.transpose(out=p[:, j, :ntok], in_=xn[:ntok, kc * 128:(kc + 1) * 128],
                                                     identity=ident[:ntok, :ntok]),
                         r=[xn, ident], w=[p], inc=(j == 3))
                if groups is None:
                    fr["n"] += 1
                    if fr["n"] % 2 == 0:
                        k.do("dve", lambda e: e.tensor_copy(out=hT[:, g4 * 4:(g4 + 1) * 4, col0:col0 + ntok], in_=p[:, :, :ntok]), r=[p], w=[hT])
                    else:
                        k.do("act", lambda e: e.copy(out=hT[:, g4 * 4:(g4 + 1) * 4, col0:col0 + ntok], in_=p[:, :, :ntok]), r=[p], w=[hT])
                    continue
                for j in range(4):
                    kc = g4 * 4 + j
                    for (c0, n, r) in groups:
                        fr["n"] += 1
                        if fr["n"] % 2 == 0:
                            k.do("dve", lambda e: e.tensor_scalar(out=hT[:, kc, col0 + c0:col0 + c0 + n], in0=p[:, j, c0:c0 + n],
                                                                  scalar1=A[:, kc, r:r + 1], scalar2=Bt[:, Bo + kc, r:r + 1],
                                                                  op0=ALU.mult, op1=ALU.add), r=[p, A, Bt], w=[hT])
                        else:
                            k.do("act", lambda e: e.activation(out=hT[:, kc, col0 + c0:col0 + c0 + n], in_=p[:, j, c0:c0 + n],
                                                               func=AF.Identity, scale=A[:, kc, r:r + 1],
                                                               bias=Bt[:, Bo + kc, r:r + 1]), r=[p, A, Bt], w=[hT])

        def front(fr, src, ntok, A, Bo, groups, hT, col0):
            xt = fr["x"].next()
            k.dma("act", xt[:ntok, :], src, w=[xt])
            front_from_sb(fr, xt, ntok, A, modT, Bo, groups, hT, col0)

        G_P = [(0, 128, 0)]
        G_M = [(0, 16, 1), (16, 16, 2), (32, 16, 3), (48, 16, 4), (64, 32, 0)]

        def gemm_fm(pan, mlist, nk, hT, blocks, pspool, consume):
            for mi, (m0, msz) in enumerate(mlist):
                for bi, (c0, n) in enumerate(blocks):
                    p = pspool.next()
                    for kc in range(nk):
                        k.do("pe", lambda e: e.matmul(out=p[:msz, :n], lhsT=pan[:, kc, m0:m0 + msz], rhs=hT[:, kc, c0:c0 + n],
                                                      start=(kc == 0), stop=(kc == nk - 1)),
                             r=[pan, hT], w=[p], inc=(kc == nk - 1))
                    consume(mi, bi, c0, n, p)

        M4 = [(0, 128), (128, 128), (256, 128), (384, 128)]

        def rms_fm(res, raw, gT, blocks, pspool, out_f32=None, out_bf=None):
            for (c0, n) in blocks:
                sq = res["sq"].next()
                for kc in range(4):
                    k.do("act", lambda e: e.activation(out=sq[:, kc, :n], in_=raw[:, kc, c0:c0 + n], func=AF.Square),
                         r=[raw], w=[sq])
                p = pspool.next()
                for kc in range(4):
                    k.do("pe", lambda e: e.matmul(out=p[:, :n], lhsT=ones[:, :], rhs=sq[:, kc, :n], start=(kc == 0), stop=(kc == 3)),
                         r=[ones, sq], w=[p], inc=(kc == 3))
                sd = res["sd"].next(); rb = res["rb"].next()
                k.do("act", lambda e: e.activation(out=sd[:, :n], in_=p[:, :n], func=AF.Sqrt, scale=1.0 / 512, bias=EPS),
                     r=[p], w=[sd])
                k.do("dve", lambda e: e.reciprocal(out=rb[:, :n], in_=sd[:, :n]), r=[sd], w=[rb])
                for kc in range(4):
                    if out_f32 is not None:
                        k.do("dve", lambda e: e.scalar_tensor_tensor(out=out_f32[:, kc, c0:c0 + n], in0=raw[:, kc, c0:c0 + n],
                                                                     scalar=gT[:, kc:kc + 1], in1=rb[:, :n],
                                                                     op0=ALU.mult, op1=ALU.mult), r=[raw, gT, rb], w=[out_f32])
                    if out_bf is not None:
                        k.do("dve", lambda e: e.scalar_tensor_tensor(out=out_bf[:, kc, c0:c0 + n], in0=raw[:, kc, c0:c0 + n],
                                                                     scalar=gT[:, kc:kc + 1], in1=rb[:, :n],
                                                                     op0=ALU.mult, op1=ALU.mult), r=[raw, gT, rb], w=[out_bf])

        def make_rms(res_es, width):
            return {"sq": k.ring(res_es, "rsq", [128, 4, width], BF16, 2),
                    "sd": k.ring(res_es, "rsd", [128, width], F32, 2),
                    "rb": k.ring(res_es, "rrb", [128, width], F32, 2)}


        def final_phase(fes_, acc):
            x1r = k.ring(fes_, "fx1", [128, D], F32, 2)
            gtr = k.ring(fes_, "fgt", [128, D], F32, 2)
            yr = k.ring(fes_, "fy", [128, D], F32, 2)
            gfb = k.sb(fes_, "gfb", [128, D], F32)
            junk = k.sb(fes_, "fjunk", [128, D], BF16)
            ssr = k.ring(fes_, "fss", [128, 1], F32, 2)
            sdr = k.ring(fes_, "fsd", [128, 1], F32, 2)
            rsr = k.ring(fes_, "frs", [128, 1], F32, 2)
            k.dma("sp", gfb[:, :], bc(g_f[0:1, :], [128, D]), w=[gfb])
            for ti, (t0, nt) in enumerate(TILES_O):
                x1 = x1r.next()
                k.dma("sp", x1[:nt, :], x1_d.t[t0:t0 + nt, :], r=[x1_d], w=[x1])
                if acc is not None:
                    GT = gtr.next()
                    if t0 < 1024:
                        k.dma("sp", GT[:nt, :], bc(modrows.t[0:1, D:2 * D], [nt, D]), r=[modrows], w=[GT])
                    else:
                        for bb in range(4):
                            k.dma("sp", GT[16 * bb:16 * bb + 16, :], bc(modrows.t[1 + bb:2 + bb, D:2 * D], [16, D]), r=[modrows], w=[GT])
                    a = acc[ti]
                    k.do("pool", lambda e: e.tensor_tensor(out=GT[:nt, :], in0=GT[:nt, :], in1=a[:nt, :], op=ALU.mult), r=[GT, a], w=[GT])
                    k.do("dve", lambda e: e.tensor_tensor(out=x1[:nt, :], in0=x1[:nt, :], in1=GT[:nt, :], op=ALU.add), r=[GT, x1], w=[x1])
                ss = ssr.next(); sd = sdr.next(); rs = rsr.next()
                k.do("act", lambda e: e.activation(out=junk[:nt, :], in_=x1[:nt, :], func=AF.Square, accum_out=ss[:nt, 0:1]), r=[x1], w=[junk, ss])
                k.do("act", lambda e: e.activation(out=sd[:nt, :], in_=ss[:nt, :], func=AF.Sqrt, scale=1.0 / D, bias=EPS), r=[ss], w=[sd])
                k.do("dve", lambda e: e.reciprocal(out=rs[:nt, :], in_=sd[:nt, :]), r=[sd], w=[rs])
                y = yr.next()
                k.do("dve", lambda e: e.scalar_tensor_tensor(out=y[:nt, :], in0=x1[:nt, :], scalar=rs[:nt, 0:1], in1=gfb[:nt, :],
                                                             op0=ALU.mult, op1=ALU.mult), r=[x1, rs, gfb], w=[y])
                k.dma("sp", o_y[t0:t0 + nt, :], y[:nt, :], r=[y], store=True)

        ckvS = k.sb(es, "ckvS", [128, 4, 64], BF16)
        krS = k.sb(es, "krS", [64, 64], BF16)
        p1es = es.enter_context(ExitStack())
        cq_t = ld(p1es, "cosq", [64, NT], cosq[:, :])
        sq_t = ld(p1es, "sinq", [64, NT], sinq[:, :])
        hT1 = k.sb(p1es, "hT1", [128, 16, NT], BF16)
        with ExitStack() as fes:
            fr = make_front(fes)
            for tt in range(8):
                front(fr, xown[tt * 128:(tt + 1) * 128, :], 128, A1, 0, G_P, hT1, tt * 128)
            front(fr, xown[1024:1120, :], 96, A1, 0, G_M, hT1, 1024)
            front_flush(fr)
            k.barrier()

        wpool = k.ring(p1es, "winpan", [128, 16, 512], BF16, 2)
        pg = k.ring(p1es, "pg", [128, 512], F32, 4, psum=True)

        def load_pan(c0, ncols=512):
            wp = wpool.next()
            k.dma("pool", wp[:, :, :ncols], w_in[:, c0:c0 + ncols].rearrange("(kc p) n -> p kc n", p=128), w=[wp])
            return wp

        with ExitStack() as pb:
            cqT = k.sb(pb, "cqT", [128, 4, NO], BF16)
            raw4 = k.sb(pb, "raw4", [128, 4, NT], F32)
            nrm4 = k.sb(pb, "nrm4", [128, 4, NT], F32)
            rres = make_rms(pb, 512)
            ptp = k.ring(pb, "ptp", [128, 512], F32, 2, psum=True)
            ost = k.ring(pb, "ost", [128, 512], F32, 2)

            def cons_raw(mi, bi, c0, n, p):
                k.do("act", lambda e: e.copy(out=raw4[:, mi, c0:c0 + n], in_=p[:, :n]), r=[p], w=[raw4])

            wp = load_pan(0)
            gemm_fm(wp, M4, 16, hT1, BLK_T, pg, cons_raw)
            rms_fm(rres, raw4, gq, BLK_O, pg, out_bf=cqT)
            k.dma("sp", cq_d.t[:, :, :], cqT[:, :, :], r=[cqT], scratch=cq_d)
            wp = load_pan(512)
            gemm_fm(wp, M4, 16, hT1, BLK_T, pg, cons_raw)
            rms_fm(rres, raw4, gkv, BLK_O, pg, out_f32=nrm4)
            k.do("dve", lambda e: e.tensor_copy(out=ckvS[:, :, :], in_=nrm4[:, :, 1024:1088]), r=[nrm4], w=[ckvS])
            for (t0, nt) in TILES_O:
                p = ptp.next()
                for kc in range(4):
                    k.do("pe", lambda e: e.transpose(out=p[:nt, kc * 128:(kc + 1) * 128], in_=nrm4[:, kc, t0:t0 + nt],
                                                     identity=identf[:, :]), r=[nrm4, identf], w=[p], inc=(kc == 3))
                o = ost.next()
                k.do("act", lambda e: e.copy(out=o[:nt, :], in_=p[:nt, :]), r=[p], w=[o])
                k.dma("sp", o_ckv[t0:t0 + nt, :], o[:nt, :], r=[o], store=True)
            wp = wpool.next()
            src = w_in[:, 1024:1088].rearrange("(kc p) n -> p kc n", p=128)
            k.dma("pool", wp[:, :, 0:64], src, w=[wp])
            k.dma("pool", wp[:, :, 64:96], w_in[:, 1056:1088].rearrange("(kc p) n -> p kc n", p=128), w=[wp])
            k.dma("pool", wp[:, :, 96:128], w_in[:, 1024:1056].rearrange("(kc p) n -> p kc n", p=128), w=[wp])
            krf = k.sb(pb, "krf", [64, NT], F32)
            t1 = k.sb(pb, "kt1", [64, NT], F32)

            def cons_kr(mi, bi, c0, n, p):
                if mi == 0:
                    k.do("dve", lambda e: e.tensor_tensor(out=t1[:, c0:c0 + n], in0=p[:64, :n], in1=cq_t[:, c0:c0 + n], op=ALU.mult),
                         r=[p, cq_t], w=[t1])
                else:
                    k.do("dve", lambda e: e.tensor_tensor(out=krf[:, c0:c0 + n], in0=p[:64, :n], in1=sq_t[:, c0:c0 + n], op=ALU.mult),
                         r=[p, sq_t], w=[krf])
                    k.do("pool", lambda e: e.tensor_tensor(out=krf[:, c0:c0 + n], in0=krf[:, c0:c0 + n], in1=t1[:, c0:c0 + n], op=ALU.add),
                         r=[krf, t1], w=[krf])

            gemm_fm(wp, [(0, 64), (64, 64)], 16, hT1, BLK_T, pg, cons_kr)
            k.do("dve", lambda e: e.tensor_copy(out=krS[:, :], in_=krf[:, 1024:1088]), r=[krf], w=[krS])
            for (t0, nt) in TILES_O:
                p = ptp.next()
                k.do("pe", lambda e: e.transpose(out=p[:nt, 0:64], in_=krf[:, t0:t0 + nt], identity=identf[:64, :64]),
                     r=[krf, identf], w=[p])
                o = ost.next()
                k.do("act", lambda e: e.copy(out=o[:nt, 0:64], in_=p[:nt, 0:64]), r=[p], w=[o])
                k.dma("sp", o_kr[t0:t0 + nt, :], o[:nt, 0:64], r=[o], store=True)
            k.barrier()

        if stop_after == "p1b":
            k.barrier()
            k.finish()
            nc._in_names = in_names
            return nc

        k.mark("p1b_done")
        mbT = k.sb(p1es, "mbT", [128, 8, NO], BF16)
        wcv = ld(p1es, "wcv", [128, 8, 3], w_convT[:, :, :])
        bcv = ld(p1es, "bcv", [128, 8], b_convT[:, :])
        hv = ld(p1es, "hv", [128, 32], hvalid[:, :])
        scv = ld(p1es, "scv", [128, 8, 4, 2], sconvT[:, :, :, :])
        with ExitStack() as pcs:
            phT = k.sb(pcs, "phT", [128, 8, NT], BF16)
            pbT = k.sb(pcs, "pbT", [128, 8, NO], BF16)
            zpT = k.sb(pcs, "zpT", [128, 8, 1128], BF16)
            zout = k.sb(pcs, "zout", [128, 8, 10], F32)
            tmpz = k.ring(pcs, "tmpz", [128, 32], F32, 2)
            ycr = k.ring(pcs, "yc", [128, NO], F32, 2)
            ptz = k.ring(pcs, "ptz", [128, 512], F32, 2, psum=True)
            ozs = k.sb(pcs, "ozs", [128, 1024], F32)
            k.do("dve", lambda e: e.tensor_copy(
                out=zpT[:, :, 1056:1128].rearrange("p c (b s) -> p c b s", s=18)[:, :, :, 0:2], in_=scv[:, :, :, :]),
                r=[scv], w=[zpT])
            for pi in range(2):
                wp = load_pan(1088 + 512 * pi)

                def cons_ph(mi, bi, c0, n, p, pi=pi):
                    k.do("act", lambda e: e.copy(out=phT[:, 4 * pi + mi, c0:c0 + n], in_=p[:, :n]), r=[p], w=[phT])
                gemm_fm(wp, M4, 16, hT1, BLK_T, pg, cons_ph)
            for pi in range(2):
                wp = load_pan(2112 + 512 * pi)

                def cons_pb(mi, bi, c0, n, p, pi=pi):
                    n = min(n, NO - c0)
                    k.do("act", lambda e: e.copy(out=pbT[:, 4 * pi + mi, c0:c0 + n], in_=p[:, :n]), r=[p], w=[pbT])
                gemm_fm(wp, M4, 16, hT1, BLK_T, pg, cons_pb)
            k.mark("phpb_done")
            for pi in range(2):
                wp = load_pan(3136 + 512 * pi)

                def cons_pc(mi, bi, c0, n, p, pi=pi):
                    mc = 4 * pi + mi
                    if bi < 2:
                        dst = zpT[:, mc, 528 * bi:528 * (bi + 1)].rearrange("p (j s) -> p j s", s=66)[:, :, 2:66]
                        k.do("dve", lambda e: e.tensor_tensor(out=dst, in0=p[:, 0:512].rearrange("p (j s) -> p j s", s=64),
                                                              in1=phT[:, mc, c0:c0 + 512].rearrange("p (j s) -> p j s", s=64),
                                                              op=ALU.mult), r=[p, phT], w=[zpT])
                        if bi == 1:
                            k.do("dve", lambda e: e.tensor_tensor(out=zout[:, mc, 0:2], in0=p[:, 510:512], in1=phT[:, mc, 1022:1024],
                                                                  op=ALU.mult), r=[p, phT], w=[zout])
                    else:
                        dst = zpT[:, mc, 1056:1128].rearrange("p (j s) -> p j s", s=18)[:, :, 2:18]
                        k.do("dve", lambda e: e.tensor_tensor(out=dst, in0=p[:, 0:64].rearrange("p (j s) -> p j s", s=16),
                                                              in1=phT[:, mc, 1024:1088].rearrange("p (j s) -> p j s", s=16),
                                                              op=ALU.mult), r=[p, phT], w=[zpT])
                        k.do("dve", lambda e: e.tensor_tensor(out=zout[:, mc, 2:10].rearrange("p (j s) -> p j s", s=2),
                                                              in0=p[:, 0:64].rearrange("p (j s) -> p j s", s=16)[:, :, 14:16],
                                                              in1=phT[:, mc, 1024:1088].rearrange("p (j s) -> p j s", s=16)[:, :, 14:16],
                                                              op=ALU.mult), r=[p, phT], w=[zout])
                        tz = tmpz.next()
                        k.do("dve", lambda e: e.tensor_tensor(out=tz[:, :], in0=p[:, 64:96], in1=phT[:, mc, 1088:1120], op=ALU.mult),
                             r=[p, phT], w=[tz])
                        dsth = zpT[:, mc, 0:1056].rearrange("p (j s) -> p j s", s=66)[:, :, 0:2]
                        k.do("pool", lambda e: e.tensor_tensor(out=dsth, in0=tz[:, :].rearrange("p (j s) -> p j s", s=2),
                                                               in1=hv[:, :].rearrange("p (j s) -> p j s", s=2), op=ALU.mult),
                             r=[tz, hv], w=[zpT])
                        yc = ycr.next()
                        for (zv, yv) in ((zpT[:, mc, 0:1056].rearrange("p (j s) -> p j s", s=66), yc[:, 0:1024].rearrange("p (j s) -> p j s", s=64)),
                                         (zpT[:, mc, 1056:1128].rearrange("p (j s) -> p j s", s=18), yc[:, 1024:1088].rearrange("p (j s) -> p j s", s=16))):
                            L = 64 if zv.shape[2] == 66 else 16
                            k.do("dve", lambda e: e.tensor_scalar(out=yv, in0=zv[:, :, 0:L], scalar1=wcv[:, mc, 0:1], scalar2=bcv[:, mc:mc + 1],
                                                                  op0=ALU.mult, op1=ALU.add), r=[zpT, wcv, bcv], w=[yc])
                            k.do("dve", lambda e: e.scalar_tensor_tensor(out=yv, in0=zv[:, :, 1:L + 1], scalar=wcv[:, mc, 1:2], in1=yv,
                                                                         op0=ALU.mult, op1=ALU.add), r=[zpT, wcv, yc], w=[yc])
                            k.do("dve", lambda e: e.scalar_tensor_tensor(out=yv, in0=zv[:, :, 2:L + 2], scalar=wcv[:, mc, 2:3], in1=yv,
                                                                         op0=ALU.mult, op1=ALU.add), r=[zpT, wcv, yc], w=[yc])
                        k.do("pool", lambda e: e.tensor_tensor(out=mbT[:, mc, :], in0=yc[:, :], in1=pbT[:, mc, :], op=ALU.mult),
                             r=[yc, pbT], w=[mbT])
                gemm_fm(wp, M4, 16, hT1, BLK_T, pg, cons_pc)
            k.mark("pc_done")
            for half in range(2):
                p = ptz.next()
                for j in range(4):
                    mc = half * 4 + j
                    k.do("pe", lambda e: e.transpose(out=p[:10, j * 128:(j + 1) * 128], in_=zout[:, mc, :], identity=identf[:, :]),
                         r=[zout, identf], w=[p], inc=(j == 3))
                k.do("act", lambda e: e.copy(out=ozs[:10, half * 512:(half + 1) * 512], in_=p[:10, 0:512]), r=[p], w=[ozs])
            k.dma("sp", o_conv[:, :], ozs[:10, :], r=[ozs], store=True)
            k.barrier()

        k.mark("p1c_done")
        with ExitStack() as pds:
            g1T = k.sb(pds, "g1T", [128, 16, NO], BF16)
            gst = k.ring(pds, "gst", [128, NO], BF16, 3)
            cur = {}
            for pi in range(8):
                wp = load_pan(4160 + 512 * pi)

                def cons_g(mi, bi, c0, n, p, pi=pi):
                    n = min(n, NO - c0)
                    dc = (4 * pi + mi) % 16
                    if pi < 4:
                        if bi == 0:
                            cur["g"] = gst.next()
                        g = cur["g"]
                        k.do("act", lambda e: e.activation(out=g[:, c0:c0 + n], in_=p[:, :n], func=AF.Sigmoid), r=[p], w=[g])
                        if bi == 2:
                            k.dma("sp", g0_d.t[dc, :, :], g[:, :], r=[g], scratch=g0_d)
                    else:
                        k.do("act", lambda e: e.activation(out=g1T[:, dc, c0:c0 + n], in_=p[:, :n], func=AF.Sigmoid), r=[p], w=[g1T])
                gemm_fm(wp, M4, 16, hT1, BLK_T, pg, cons_g)
            for pi in range(4):
                wp = wpool.next()
                k.dma("pool", wp[:, 0:8, :], w_ob[:, pi * 512:(pi + 1) * 512].rearrange("(kc p) n -> p kc n", p=128), w=[wp])

                def cons_b(mi, bi, c0, n, p, pi=pi):
                    dc = 4 * pi + mi
                    if bi == 0:
                        cur["g"] = gst.next()
                    g = cur["g"]
                    k.do("dve", lambda e: e.tensor_tensor(out=g[:, c0:c0 + n], in0=p[:, :n], in1=g1T[:, dc, c0:c0 + n], op=ALU.mult),
                         r=[p, g1T], w=[g])
                    if bi == 2:
                        k.dma("sp", gb_d.t[dc, :, :], g[:, :], r=[g], scratch=gb_d)
                gemm_fm(wp, M4, 8, mbT, BLK_O, pg, cons_b)
            k.barrier()
        p1es.close()
        k.mark("p1_done")

        if stop_after == "p1":
            k.barrier()
            k.finish()
            nc._in_names = in_names
            return nc

        with ExitStack() as kes:
            wuk = k.sb(kes, "wuk", [128, 4, 1024], BF16)
            wuv = k.sb(kes, "wuv", [128, 4, 1024], BF16)
            k.dma("pool", wuk[:, :, :], w_uk.rearrange("(kc p) n -> p kc n", p=128), w=[wuk])
            k.dma("pool", wuv[:, :, :], w_uv.rearrange("(kc p) n -> p kc n", p=128), w=[wuv])
            pk = k.ring(kes, "pk", [128, 512], F32, 4, psum=True)

            def kv_gen(ckT, blocks, tiles, KTst, Vst):
                n_ev = [0]

                def ev(out, in_, rr, ww):
                    n_ev[0] += 1
                    if n_ev[0] % 2:
                        k.do("act", lambda e: e.copy(out=out, in_=in_), r=rr, w=ww)
                    else:
                        k.do("dve", lambda e: e.tensor_copy(out=out, in_=in_), r=rr, w=ww)
                for h in range(8):
                    for (c0, n) in blocks:
                        p = pk.next()
                        for kc in range(4):
                            k.do("pe", lambda e: e.matmul(out=p[:, :n], lhsT=wuk[:, kc, h * 128:(h + 1) * 128], rhs=ckT[:, kc, c0:c0 + n],
                                                          start=(kc == 0), stop=(kc == 3)), r=[wuk, ckT], w=[p], inc=(kc == 3))
                        ev(KTst[:, h, c0:c0 + n], p[:, :n], [p], [KTst])
                for ti, (t0, nk) in enumerate(tiles):
                    for hh in range(2):
                        p = pk.next()
                        for kc in range(4):
                            k.do("pe", lambda e: e.matmul(out=p[:nk, :], lhsT=ckT[:, kc, t0:t0 + nk], rhs=wuv[:, kc, hh * 512:(hh + 1) * 512],
                                                          start=(kc == 0), stop=(kc == 3)), r=[wuv, ckT], w=[p], inc=(kc == 3))
                        ev(Vst[:nk, ti, hh * 512:(hh + 1) * 512], p[:nk, :], [p], [Vst])

            with ExitStack() as kss:
                ptb = k.ring(kss, "ptb", [128, 4, 128], BF16, 2, psum=True)
                ckc_r = k.ring(kss, "ckc", [128, 8, 512], BF16, 2)
                ckcT_r = k.ring(kss, "ckcT", [128, 4, 1040], BF16, 2)
                KTs_r = k.ring(kss, "KTs", [128, 8, 1040], BF16, 2)
                Vs_r = k.ring(kss, "Vs", [128, 9, 1024], BF16, 2)
                krc_r = k.ring(kss, "krc", [128, 8, 64], BF16, 2)
                krcT_r = k.ring(kss, "krcT", [64, 1040], BF16, 2)
                ad2 = make_ada(kss)
                for bb in range(4):
                    ckc = ckc_r.next()
                    k.dma("pool", ckc[:, :, :], cckv[bb].rearrange("(t p) r -> p t r", p=128), w=[ckc])
                    ckcT = ckcT_r.next()
                    for t in range(8):
                        p = ptb.next()
                        for kc in range(4):
                            k.do("pe", lambda e: e.transpose(out=p[:, kc, :], in_=ckc[:, t, kc * 128:(kc + 1) * 128], identity=ident[:, :]),
                                 r=[ckc, ident], w=[p], inc=(kc == 3))
                        k.do("act" if t % 2 else "dve",
                             (lambda e: e.copy(out=ckcT[:, :, t * 128:(t + 1) * 128], in_=p[:, :, :])) if t % 2 else
                             (lambda e: e.tensor_copy(out=ckcT[:, :, t * 128:(t + 1) * 128], in_=p[:, :, :])), r=[p], w=[ckcT])
                    k.do("dve", lambda e: e.tensor_copy(out=ckcT[:, :, 1024:1040], in_=ckvS[:, :, bb * 16:(bb + 1) * 16]), r=[ckvS], w=[ckcT])
                    KTs = KTs_r.next(); Vs = Vs_r.next()
                    kv_gen(ckcT, [(0, 512), (512, 512), (1024, 16)], [(t * 128, 128) for t in range(8)] + [(1024, 16)], KTs, Vs)
                    k.dma("sp", KTs_d.t[bb].rearrange("h p n -> p h n"), KTs[:, :, :], r=[KTs], scratch=KTs_d)
                    k.dma("sp", Vs_d.t[bb].rearrange("h p (t d) -> p t h d", d=128),
                          Vs[:, :, :].rearrange("p t (h d) -> p t h d", d=128), r=[Vs], scratch=Vs_d)
                    krc = krc_r.next()
                    k.dma("pool", krc[:, :, :], ckr[bb].rearrange("(t p) r -> p t r", p=128), w=[krc])
                    krcT = krcT_r.next()
                    for half in range(2):
                        p = ptb.next()
                        for j in range(4):
                            t = half * 4 + j
                            k.do("pe", lambda e: e.transpose(out=p[:64, j, :], in_=krc[:, t, :], identity=ident[:, :]),
                                 r=[krc, ident], w=[p], inc=(j == 3))
                        k.do("act", lambda e: e.copy(out=krcT[:, half * 512:(half + 1) * 512], in_=p[:64, :, :].rearrange("p a b -> p (a b)")),
                             r=[p], w=[krcT])
                    k.do("dve", lambda e: e.tensor_copy(out=krcT[:, 1024:1040], in_=krS[:, bb * 16:(bb + 1) * 16]), r=[krS], w=[krcT])
                    k.dma("sp", krc_d.t[:, bb, :], krcT[:, :], r=[krcT], scratch=krc_d)
                    ada_big(ad2, 4 + 2 * bb)
                    ada_big(ad2, 5 + 2 * bb)
                ada_finish(A2, gn2, 48)
                k.barrier()
            k.mark("ks_done")

            with ExitStack() as kps:
                wkv = k.sb(kps, "wkv", [128, 16, 640], BF16)
                k.dma("pool", wkv[:, :, 0:512], w_in[:, 512:1024].rearrange("(kc p) n -> p kc n", p=128), w=[wkv])
                k.dma("pool", wkv[:, :, 512:576], w_in[:, 1024:1088].rearrange("(kc p) n -> p kc n", p=128), w=[wkv])
                k.dma("pool", wkv[:, :, 576:608], w_in[:, 1056:1088].rearrange("(kc p) n -> p kc n", p=128), w=[wkv])
                k.dma("pool", wkv[:, :, 608:640], w_in[:, 1024:1056].rearrange("(kc p) n -> p kc n", p=128), w=[wkv])
                fr = make_front(kps)
                Bbf = k.sb(kps, "Bbf", [128, 16, 2], BF16)
                kbias = k.sb(kps, "kbias", [128, 8], F32)
                k.do("dve", lambda e: e.tensor_copy(out=Bbf[:, :, 0:1], in_=modT[:, 0:16, 0:1]), r=[modT], w=[Bbf])
                for mi_, (m0_, msz_) in enumerate(M4 + [(512, 64), (576, 64)]):
                    pb_ = pk.next()
                    for kc in range(16):
                        k.do("pe", lambda e: e.matmul(out=pb_[:msz_, 0:1], lhsT=wkv[:, kc, m0_:m0_ + msz_], rhs=Bbf[:, kc, 0:1],
                                                      start=(kc == 0), stop=(kc == 15)), r=[wkv, Bbf], w=[pb_], inc=(kc == 15))
                    k.do("act", lambda e: e.copy(out=kbias[:msz_, mi_:mi_ + 1], in_=pb_[:msz_, 0:1]), r=[pb_], w=[kbias])
                for kc in range(16):
                    k.do("dve" if kc % 2 else "pool",
                         lambda e: e.tensor_scalar(out=wkv[:, kc, :], in0=wkv[:, kc, :], scalar1=A1[:, kc, 0:1], scalar2=1.0,
                                                   op0=ALU.mult, op1=ALU.mult), r=[wkv, A1], w=[wkv])
                hTk = k.ring(kps, "hTk", [128, 16, 512], BF16, 2)
                rawk_r = k.ring(kps, "rawk", [128, 4, 512], F32, 2)
                ckb_r = k.ring(kps, "ckb", [128, 4, 512], BF16, 2)
                rres = make_rms(kps, 512)
                cos_r = k.ring(kps, "cosb", [64, 512], F32, 2)
                sin_r = k.ring(kps, "sinb", [64, 512], F32, 2)
                kt1_r = k.ring(kps, "kt1b", [64, 512], F32, 2)
                kt2_r = k.ring(kps, "kt2b", [64, 512], F32, 2)
                krst_r = k.ring(kps, "krst", [64, 512], BF16, 2)
                KT_r = k.ring(kps, "KTst", [128, 8, 512], BF16, 2)
                V_r = k.ring(kps, "Vst", [128, 4, 1024], BF16, 2)
                kst = {"hT": hTk.next(), "pre": False}

                def kp_fronts(b):
                    hT = kst["hT"]
                    for i in range(1 if kst["pre"] else 0, 4):
                        r0 = (4 * b + i) * 128
                        front(fr, xall[r0:r0 + 128, :], 128, A1, 0, None, hT, i * 128)
                    if b < 15:
                        hTn = hTk.next()
                        r0 = (4 * (b + 1)) * 128
                        front(fr, xall[r0:r0 + 128, :], 128, A1, 0, None, hTn, 0)
                        kst["hT"] = hTn
                        kst["pre"] = True
                    else:
                        front_flush(fr)
                    return hT

                def kp_gemm(b, hT):
                    rawk = rawk_r.next()

                    def cons_rawk(mi, bi, c0, n, p, rawk=rawk):
                        k.do("act", lambda e: e.activation(out=rawk[:, mi, :], in_=p[:, :], func=AF.Identity, bias=kbias[:, mi:mi + 1]),
                             r=[p, kbias], w=[rawk])
                    gemm_fm(wkv, M4, 16, hT, [(0, 512)], pk, cons_rawk)
                    cb = cos_r.next(); sb_ = sin_r.next()
                    k.dma("sp", cb[:, :], cosk[:, b * 512:(b + 1) * 512], w=[cb])
                    k.dma("sp", sb_[:, :], sink[:, b * 512:(b + 1) * 512], w=[sb_])
                    t1 = kt1_r.next(); t2 = kt2_r.next(); krst = krst_r.next()

                    def cons_krk(mi, bi, c0, n, p, t1=t1, t2=t2, krst=krst, cb=cb, sb_=sb_):
                        if mi == 0:
                            k.do("dve", lambda e: e.scalar_tensor_tensor(out=t1[:, :], in0=p[:64, :], scalar=kbias[:64, 4:5], in1=cb[:, :],
                                                                         op0=ALU.add, op1=ALU.mult), r=[p, cb, kbias], w=[t1])
                        else:
                            k.do("dve", lambda e: e.scalar_tensor_tensor(out=t2[:, :], in0=p[:64, :], scalar=kbias[:64, 5:6], in1=sb_[:, :],
                                                                         op0=ALU.add, op1=ALU.mult), r=[p, sb_, kbias], w=[t2])
                            k.do("pool", lambda e: e.tensor_tensor(out=krst[:, :], in0=t1[:, :], in1=t2[:, :], op=ALU.add), r=[t1, t2], w=[krst])
                    gemm_fm(wkv, [(512, 64), (576, 64)], 16, hT, [(0, 512)], pk, cons_krk)
                    k.dma("sp", krT_d.t[:, b * 512:(b + 1) * 512], krst[:, :], r=[krst], scratch=krT_d)
                    return rawk

                def kp_rms(b, rawk):
                    ckb = ckb_r.next()
                    rms_fm(rres, rawk, gkv, [(0, 512)], pk, out_bf=ckb)
                    return ckb

                def kp_kv(b, ckb):
                    KTst = KT_r.next(); Vst = V_r.next()
                    kv_gen(ckb, [(0, 512)], [(t * 128, 128) for t in range(4)], KTst, Vst)
                    k.dma("sp", KT_d.t[:, b].rearrange("h p n -> p h n"), KTst[:, :, :], r=[KTst], scratch=KT_d)
                    k.dma("sp", V_d.t[:, b].rearrange("h p (t d) -> p t h d", d=128),
                          Vst[:, :, :].rearrange("p t (h d) -> p t h d", d=128), r=[Vst], scratch=V_d)

                prevraw = None
                for b in range(16):
                    hT = kp_fronts(b)
                    ckb_prev = kp_rms(b - 1, prevraw) if prevraw is not None else None
                    prevraw_new = kp_gemm(b, hT)
                    if ckb_prev is not None:
                        kp_kv(b - 1, ckb_prev)
                    prevraw = prevraw_new
                ckb_last = kp_rms(15, prevraw)
                kp_kv(15, ckb_last)
                k.barrier()
            k.mark("kp_done")

        if stop_after == "kp":
            k.barrier()
            k.finish()
            nc._in_names = in_names
            return nc

        h2T = k.sb(es, "h2T", [128, 16, NO], BF16)
        mes = es.enter_context(ExitStack())
        mT = k.sb(mes, "mT", [128, 16, NO], BF16)
        oes = es.enter_context(ExitStack())
        oT = k.sb(oes, "oT", [128, 8, NO], BF16)
        with ExitStack() as at:
            cqT = k.sb(at, "cqTa", [128, 4, NO], BF16)
            k.dma("sp", cqT[:, :, :], cq_d.t[:, :, :], r=[cq_d], w=[cqT])
            krTa = k.sb(at, "krTa", [64, 8192], BF16)
            k.dma("sp", krTa[:, :], krT_d.t[:, :], r=[krT_d], w=[krTa])
            krcT = k.sb(at, "krcTa", [64, 4, 1040], BF16)
            k.dma("sp", krcT[:, :, :], krc_d.t[:, :, :], r=[krc_d], w=[krcT])
            dm = ld(at, "dm", [128, 4], dmask[:, :])
            cq_t = ld(at, "cosqa", [64, NT], cosq[:, :])
            sq_t = ld(at, "sinqa", [64, NT], sinq[:, :])
            wuq = k.sb(at, "wuq", [128, 4, 1536], BF16)
            k.dma("pool", wuq[:, :, :], w_uq.rearrange("(kc p) n -> p kc n", p=128), w=[wuq])
            wuqs = k.sb(at, "wuqs", [128, 4, 8, 64], BF16)
            for kc in range(4):
                srcv = w_uq[kc * 128:(kc + 1) * 128, :].rearrange("p (h c) -> p h c", c=192)
                k.dma("pool", wuqs[:, kc, :, 0:32], srcv[:, :, 160:192], w=[wuqs])
                k.dma("pool", wuqs[:, kc, :, 32:64], srcv[:, :, 128:160], w=[wuqs])
            qn_r = k.ring(at, "qn", [128, NO], BF16, 2)
            qr_r = k.ring(at, "qr", [64, NO], BF16, 2)
            qt1_r = k.ring(at, "qt1", [64, 512], F32, 2)
            qt2_r = k.ring(at, "qt2", [64, 512], F32, 2)
            kb_r = k.ring(at, "kb", [128, 512], BF16, 6)
            vb_r = k.ring(at, "vb", [128, 4, 128], BF16, 6)
            PT_r = k.ring(at, "PT", [128, 512], BF16, 4)
            rd_r = k.ring(at, "rd", [128, 512], F32, 2)
            dacc_r = [k.ring(at, "dacc0", [128, 512], F32, 2), k.ring(at, "dacc1", [128, 512], F32, 2)]
            onesf = k.sb(at, "onesf", [128, 128], F32)
            k.do("dve", lambda e: e.memset(onesf[:, :], 1.0), w=[onesf])
            kts_r = k.ring(at, "kts", [128, 1040], BF16, 3)
            vs_r = k.ring(at, "vss", [128, 9, 128], BF16, 3)
            ps_s = k.ring(at, "ps_s", [128, 512], F32, 3, psum=True)
            ps_o = k.ring(at, "ps_o", [128, 512], F32, 2, psum=True)
            ps_d = k.ring(at, "ps_d", [128, 512], F32, 2, psum=True)
            ps_q = k.ring(at, "ps_q", [128, 512], F32, 1, psum=True)

            def emit_qproj(h):
                qn = qn_r.next(); qr = qr_r.next()
                for (c0, n) in BLK_O:
                    p = ps_q.next()
                    for kc in range(4):
                        k.do("pe", lambda e: e.matmul(out=p[:, :n], lhsT=wuq[:, kc, h * 192:h * 192 + 128], rhs=cqT[:, kc, c0:c0 + n],
                                                      start=(kc == 0), stop=(kc == 3)), r=[wuq, cqT], w=[p], inc=(kc == 3))
                    k.do("act", lambda e: e.copy(out=qn[:, c0:c0 + n], in_=p[:, :n]), r=[p], w=[qn])
                    p = ps_q.next()
                    for kc in range(4):
                        k.do("pe", lambda e: e.matmul(out=p[:64, :n], lhsT=wuq[:, kc, h * 192 + 128:h * 192 + 192], rhs=cqT[:, kc, c0:c0 + n],
                                                      start=(kc == 0), stop=(kc == 3)), r=[wuq, cqT], w=[p], inc=(kc == 3))
                    t1 = qt1_r.next()
                    k.do("dve", lambda e: e.tensor_tensor(out=t1[:, :n], in0=p[:64, :n], in1=cq_t[:, c0:c0 + n], op=ALU.mult), r=[p, cq_t], w=[t1])
                    p = ps_q.next()
                    for kc in range(4):
                        k.do("pe", lambda e: e.matmul(out=p[:64, :n], lhsT=wuqs[:, kc, h, :], rhs=cqT[:, kc, c0:c0 + n],
                                                      start=(kc == 0), stop=(kc == 3)), r=[wuqs, cqT], w=[p], inc=(kc == 3))
                    t2 = qt2_r.next()
                    k.do("dve", lambda e: e.tensor_tensor(out=t2[:, :n], in0=p[:64, :n], in1=sq_t[:, c0:c0 + n], op=ALU.mult), r=[p, sq_t], w=[t2])
                    k.do("pool", lambda e: e.tensor_tensor(out=qr[:, c0:c0 + n], in0=t1[:, :n], in1=t2[:, :n], op=ALU.add), r=[t1, t2], w=[qr])
                return qn, qr

            qnext = emit_qproj(0)
            for h in range(8):
                qn, qr = qnext
                for g in range(2):
                    if g == 1 and h < 7:
                        qnext = emit_qproj(h + 1)
                    po = ps_o.next(); pd = ps_d.next()
                    dacc = [dacc_r[0].next(), dacc_r[1].next()]
                    k.do("dve", lambda e: e.memset(dacc[0][:, :], 0.0), w=[dacc[0]])
                    k.do("pool", lambda e: e.memset(dacc[1][:, :], 0.0), w=[dacc[1]])
                    ntile = 0
                    nblk = 8 * g + 8
                    c1 = 512 * (g + 1)
                    items = []
                    for b in range(nblk):
                        for kt in range(4):
                            items.append((b, kt))
                    blk = {}

                    def emit_S(b, kt):
                        if kt == 0:
                            Kb = kb_r.next(); Vb = vb_r.next()
                            k.dma("sp", Kb[:, :], KT_d.t[h, b], r=[KT_d], w=[Kb])
                            k.dma("sp", Vb[:, :, :], V_d.t[h, b].rearrange("p (t d) -> p t d", d=128), r=[V_d], w=[Vb])
                            blk[b] = (Kb, Vb)
                        Kb, Vb = blk[b]
                        jlo = max(b, 8 * g)
                        c0 = 64 * jlo
                        N = c1 - c0
                        diag = (b >= 8 * g)
                        ps = ps_s.next()
                        k.do("pe", lambda e: e.matmul(out=ps[:, :N], lhsT=Kb[:, kt * 128:(kt + 1) * 128], rhs=qn[:, c0:c1], start=True, stop=False),
                             r=[Kb, qn], w=[ps], inc=False)
                        k0 = b * 512 + kt * 128
                        k.do("pe", lambda e: e.matmul(out=ps[:, :N], lhsT=krTa[:, k0:k0 + 128], rhs=qr[:, c0:c1], start=False, stop=True),
                             r=[krTa, qr], w=[ps])
                        PT = PT_r.next()
                        if diag:
                            k.do("act", lambda e: e.activation(out=PT[:, 0:64], in_=ps[:, 0:64], func=AF.Exp, scale=SCALE, bias=dm[:, kt:kt + 1]),
                                 r=[ps, dm], w=[PT])
                            if N > 64:
                                k.do("act", lambda e: e.activation(out=PT[:, 64:N], in_=ps[:, 64:N], func=AF.Exp, scale=SCALE), r=[ps], w=[PT])
                        else:
                            k.do("act", lambda e: e.activation(out=PT[:, :N], in_=ps[:, :N], func=AF.Exp, scale=SCALE), r=[ps], w=[PT])
                        return (b, kt, Vb, PT, c0 - 512 * g, N)

                    def emit_PV(it, idx):
                        b, kt, Vb, PT, lc0, N = it
                        first = (idx == 0)
                        last = (idx == len(items) - 1)
                        k.do("pe", lambda e: e.matmul(out=po[:, lc0:lc0 + N], lhsT=Vb[:, kt, :], rhs=PT[:, :N], start=first, stop=last),
                             r=[Vb, PT], w=[po])
                        da = dacc[idx % 2]
                        k.do("dve" if idx % 2 == 0 else "pool",
                             lambda e: e.tensor_tensor(out=da[:, lc0:lc0 + N], in0=da[:, lc0:lc0 + N], in1=PT[:, :N], op=ALU.add),
                             r=[da, PT], w=[da])

                    pend = None
                    for idx, (b, kt) in enumerate(items):
                        it = emit_S(b, kt)
                        if pend is not None:
                            emit_PV(pend, idx - 1)
                        pend = it
                    emit_PV(pend, len(items) - 1)
                    for i_ in range(2):
                        k.do("pe", lambda e: e.matmul(out=pd[:, :], lhsT=onesf[:, :], rhs=dacc[i_][:, :], start=(i_ == 0), stop=(i_ == 1)),
                             r=[onesf, dacc[i_]], w=[pd], inc=(i_ == 1))
                    rd = rd_r.next()
                    k.do("dve", lambda e: e.reciprocal(out=rd[:, :], in_=pd[:, :]), r=[pd], w=[rd])
                    k.do("dve", lambda e: e.tensor_tensor(out=oT[:, h, 512 * g:512 * (g + 1)], in0=po[:, :], in1=rd[:, :], op=ALU.mult),
                         r=[po, rd], w=[oT])
                po = ps_o.next(); pd = ps_d.next()
                for bb in range(4):
                    Ks = kts_r.next(); Vs = vs_r.next()
                    k.dma("sp", Ks[:, :], KTs_d.t[bb, h], r=[KTs_d], w=[Ks])
                    k.dma("sp", Vs[:, :, :], Vs_d.t[bb, h].rearrange("p (t d) -> p t d", d=128), r=[Vs_d], w=[Vs])
                    q0 = 1024 + 16 * bb
                    for t in range(9):
                        nk = 128 if t < 8 else 16
                        ps = ps_s.next()
                        k.do("pe", lambda e: e.matmul(out=ps[:nk, :16], lhsT=Ks[:, t * 128:t * 128 + nk], rhs=qn[:, q0:q0 + 16], start=True, stop=False),
                             r=[Ks, qn], w=[ps], inc=False)
                        k.do("pe", lambda e: e.matmul(out=ps[:nk, :16], lhsT=krcT[:, bb, t * 128:t * 128 + nk], rhs=qr[:, q0:q0 + 16], start=False, stop=True),
                             r=[krcT, qr], w=[ps])
                        PT = PT_r.next()
                        k.do("act", lambda e: e.activation(out=PT[:nk, :16], in_=ps[:nk, :16], func=AF.Exp, scale=SCALE), r=[ps], w=[PT])
                        k.do("pe", lambda e: e.matmul(out=po[:, 16 * bb:16 * bb + 16], lhsT=Vs[:nk, t, :], rhs=PT[:nk, :16], start=(t == 0), stop=(t == 8)),
                             r=[Vs, PT], w=[po], inc=False)
                        k.do("pe", lambda e: e.matmul(out=pd[:, 16 * bb:16 * bb + 16], lhsT=ones[:nk, :], rhs=PT[:nk, :16], start=(t == 0), stop=(t == 8)),
                             r=[ones, PT], w=[pd])
                rd = rd_r.next()
                k.do("dve", lambda e: e.reciprocal(out=rd[:, 0:64], in_=pd[:, 0:64]), r=[pd], w=[rd])
                k.do("dve", lambda e: e.tensor_tensor(out=oT[:, h, 1024:1088], in0=po[:, 0:64], in1=rd[:, 0:64], op=ALU.mult),
                     r=[po, rd], w=[oT])
            k.barrier()
        k.mark("att_done")

        if stop_after == "att":
            k.barrier()
            k.finish()
            nc._in_names = in_names
            return nc

        with ExitStack() as ma:
            woa_r = k.ring(ma, "woa", [128, 8, 512], BF16, 2)
            g0_r = k.ring(ma, "g0t", [128, NO], BF16, 3)
            gb_r = k.ring(ma, "gbt", [128, NO], BF16, 3)
            gq_ = []

            def g_issue(dc):
                a = g0_r.next(); b_ = gb_r.next()
                k.dma("sp", a[:, :], g0_d.t[dc, :, :], r=[g0_d], w=[a])
                k.dma("sp", b_[:, :], gb_d.t[dc, :, :], r=[gb_d], w=[b_])
                gq_.append((a, b_))
            g_issue(0)
            mtmp_r = k.ring(ma, "mtmp", [128, 512], F32, 2)
            pm = k.ring(ma, "pm", [128, 512], F32, 4, psum=True)
            cur = {}
            for pi in range(4):
                wp = woa_r.next()
                k.dma("pool", wp[:, :, :], w_oa[:, pi * 512:(pi + 1) * 512].rearrange("(kc p) n -> p kc n", p=128), w=[wp])

                def cons_m(mi, bi, c0, n, p, pi=pi):
                    dc = 4 * pi + mi
                    if bi == 0:
                        cur["g0"], cur["gb"] = gq_.pop(0)
                        if dc < 15:
                            g_issue(dc + 1)
                    g0t = cur["g0"]; gbt = cur["gb"]
                    tm = mtmp_r.next()
                    k.do("dve", lambda e: e.tensor_tensor(out=tm[:, :n], in0=p[:, :n], in1=g0t[:, c0:c0 + n], op=ALU.mult), r=[p, g0t], w=[tm])
                    k.do("pool", lambda e: e.tensor_tensor(out=mT[:, dc, c0:c0 + n], in0=tm[:, :n], in1=gbt[:, c0:c0 + n], op=ALU.add),
                         r=[tm, gbt], w=[mT])
                gemm_fm(wp, M4, 8, oT, BLK_O, pm, cons_m)
            k.barrier()
        oes.close()
        k.mark("mrga_done")
        G_S = [(0, 16, 1), (16, 16, 2), (32, 16, 3), (48, 16, 4)]
        with ExitStack() as mb_:
            wos = [k.sb(mb_, "wo%d" % i, [128, 16, 512], BF16) for i in range(4)]
            for pi in range(4):
                k.dma("pool", wos[pi][:, :, :], w_o[:, pi * 512:(pi + 1) * 512].rearrange("(kc p) n -> p kc n", p=128), w=[wos[pi]])
            fr = make_front(mb_, nx=2)
            GT_r = k.ring(mb_, "GT", [128, D], F32, 1)
            x1_r = k.ring(mb_, "x1", [128, D], F32, 2)
            pm = k.ring(mb_, "pm2", [128, 512], F32, 4, psum=True)
            for (t0, nt) in TILES_O:
                xt = fr["x"].next()
                k.dma("sp", xt[:nt, :], xown[t0:t0 + nt, :], w=[xt])
                GT = GT_r.next()
                if t0 < 1024:
                    k.dma("sp", GT[:nt, :], bc(modrows.t[0:1, 0:D], [nt, D]), r=[modrows], w=[GT])
                else:
                    for bb in range(4):
                        k.dma("sp", GT[16 * bb:16 * bb + 16, :], bc(modrows.t[1 + bb:2 + bb, 0:D], [16, D]), r=[modrows], w=[GT])
                x1 = x1_r.next()
                for dq in range(4):
                    p = pm.next()
                    for kc in range(16):
                        k.do("pe", lambda e: e.matmul(out=p[:nt, :], lhsT=mT[:, kc, t0:t0 + nt], rhs=wos[dq][:, kc, :],
                                                      start=(kc == 0), stop=(kc == 15)), r=[mT, wos[dq]], w=[p], inc=(kc == 15))
                    k.do("dve", lambda e: e.tensor_tensor(out=x1[:nt, dq * 512:(dq + 1) * 512], in0=p[:nt, :], in1=GT[:nt, dq * 512:(dq + 1) * 512],
                                                          op=ALU.mult), r=[p, GT], w=[x1])
                k.do("pool", lambda e: e.tensor_tensor(out=x1[:nt, :], in0=x1[:nt, :], in1=xt[:nt, :], op=ALU.add), r=[x1, xt], w=[x1])
                k.dma("sp", x1_d.t[t0:t0 + nt, :], x1[:nt, :], r=[x1], scratch=x1_d)
                front_from_sb(fr, x1, nt, A2, modT, 32, G_P if t0 < 1024 else G_S, h2T, t0)
            front_flush(fr)
            k.barrier()
        mes.close()
        k.mark("mrg_done")

        if stop_after == "mrg":
            with ExitStack() as fz:
                final_phase(fz, None)
            k.barrier()
            k.finish()
            nc._in_names = in_names
            return nc

        with ExitStack() as pes:
            aT = k.sb(pes, "aT", [128, NO], F32)
            bT = k.sb(pes, "bT", [128, NO], F32)
            gT = k.sb(pes, "gT", [128, NO], F32)
            io_i = k.sb(pes, "io_i", [128, 128], I32)
            io128 = k.sb(pes, "io128", [128, 128], F32)
            k.do("pool", lambda e: e.iota(io_i[:, :], pattern=[[1, 128]], base=0, channel_multiplier=0), w=[io_i])
            k.do("dve", lambda e: e.tensor_copy(out=io128[:, :], in_=io_i[:, :]), r=[io_i], w=[io128])
            with ExitStack() as pq:
                qpT = k.sb(pq, "qpT", [128, 16, NO], BF16)
                wq_r = k.ring(pq, "wpqpan", [128, 16, 512], BF16, 2)
                psq = k.ring(pq, "psq", [128, 512], F32, 4, psum=True)
                ptq = k.ring(pq, "ptq", [128, 4, 128], BF16, 2, psum=True)
                for pi in range(4):
                    wp = wq_r.next()
                    k.dma("pool", wp[:, :, :], w_pq[:, pi * 512:(pi + 1) * 512].rearrange("(kc p) n -> p kc n", p=128), w=[wp])

                    def cons_q(mi, bi, c0, n, p, pi=pi):
                        k.do("act", lambda e: e.copy(out=qpT[:, 4 * pi + mi, c0:c0 + n], in_=p[:, :n]), r=[p], w=[qpT])
                    gemm_fm(wp, M4, 16, h2T, BLK_O, psq, cons_q)
                subkT = k.sb(pq, "subkT", [128, 16, 128], BF16)
                skr = k.ring(pq, "skr", [128, 8, 128], BF16, 2)
                for which, sk in enumerate((sub_k1, sub_k2)):
                    s_ = skr.next()
                    k.dma("pool", s_[:, :, :], sk.rearrange("(h n) d -> n h d", n=128), w=[s_])
                    for half in range(2):
                        p = ptq.next()
                        for j in range(4):
                            h = half * 4 + j
                            k.do("pe", lambda e: e.transpose(out=p[:, j, :], in_=s_[:, h, :], identity=ident[:, :]), r=[s_, ident], w=[p], inc=(j == 3))
                        for j in range(4):
                            h = half * 4 + j
                            k.do("act", lambda e: e.copy(out=subkT[:, 2 * h + which, :], in_=p[:, j, :]), r=[p], w=[subkT])
                sc_r = k.ring(pq, "sc", [128, 16, 128], F32, 2)
                sc2a = [k.sb(pq, "sc2a%d" % i, [128, 128], F32) for i in range(16)]
                v16a = [k.sb(pq, "v16a%d" % i, [128, 8], F32) for i in range(16)]
                v16b = [k.sb(pq, "v16b%d" % i, [128, 8], F32) for i in range(16)]
                ixa = [k.sb(pq, "ixa%d" % i, [128, 8], U32) for i in range(16)]
                ixb = [k.sb(pq, "ixb%d" % i, [128, 8], U32) for i in range(16)]
                cand2a = [k.sb(pq, "cand2a%d" % i, [128, 256], F32) for i in range(8)]
                sva = [k.sb(pq, "sva%d" % i, [128, 8], F32) for i in range(8)]
                svb = [k.sb(pq, "svb%d" % i, [128, 8], F32) for i in range(8)]
                cia = [k.sb(pq, "cia%d" % i, [128, 8], U32) for i in range(8)]
                cib = [k.sb(pq, "cib%d" % i, [128, 8], U32) for i in range(8)]
                v16s = [k.sb(pq, "v16_%d" % i, [128, 16, 16], F32) for i in range(2)]
                ixs = [k.sb(pq, "ix_%d" % i, [128, 16, 16], U32) for i in range(2)]
                ixf = k.sb(pq, "ixf", [128, 16, 16], F32)
                cand = k.sb(pq, "cand", [128, 8, 256], F32)
                sv = k.sb(pq, "sv", [128, 8, 16], F32)
                ci = k.sb(pq, "ci", [128, 8, 16], U32)
                sl_i = k.sb(pq, "sl_i", [128, 2, 128], U32)
                sl_f = k.sb(pq, "sl_f", [128, 2, 128], F32)
                eqs = [k.sb(pq, "eq%d" % i, [128, 8, 16, 16], F32) for i in range(2)]
                sel = k.sb(pq, "sel", [128, 3, 128], F32)
                ex = k.sb(pq, "ex", [128, 128], F32)
                zz = k.sb(pq, "zz", [128, 8], F32)
                rz = k.sb(pq, "rz", [128, 8], F32)
                pst = k.ring(pq, "pst", [128, 512], F32, 1, psum=True)
                def tk_s1(t0, nt, v16, ix):
                    sc = sc_r.next()
                    for q4 in range(4):
                        p = psq.next()
                        for j in range(4):
                            gi_ = q4 * 4 + j
                            k.do("pe", lambda e: e.matmul(out=p[:nt, j * 128:(j + 1) * 128], lhsT=qpT[:, gi_, t0:t0 + nt], rhs=subkT[:, gi_, :],
                                                          start=True, stop=True), r=[qpT, subkT], w=[p], inc=(j == 3))
                        k.do("act", lambda e: e.copy(out=sc[:nt, q4 * 4:(q4 + 1) * 4, :], in_=p[:nt, :].rearrange("p (a b) -> p a b", b=128)),
                             r=[p], w=[sc])
                    for gi_ in range(16):
                        k.do("dve", lambda e: e.max(out=v16[:nt, gi_, 0:8], in_=sc[:nt, gi_, :]), r=[sc], w=[v16], nowaw=True)
                    for gi_ in range(16):
                        k.do("dve", lambda e: e.max_index(out=ix[:nt, gi_, 0:8], in_max=v16[:nt, gi_, 0:8], in_values=sc[:nt, gi_, :]),
                             r=[sc, v16], w=[ix], nowaw=True)
                    for gi_ in range(16):
                        k.do("dve", lambda e: e.match_replace(out=sc2a[gi_][:nt, :], in_to_replace=v16[:nt, gi_, 0:8], in_values=sc[:nt, gi_, :],
                                                              imm_value=-1e30), r=[sc, v16], w=[sc2a[gi_]])
                    for gi_ in range(16):
                        k.do("dve", lambda e: e.max(out=v16[:nt, gi_, 8:16], in_=sc2a[gi_][:nt, :]), r=[sc2a[gi_]], w=[v16], nowaw=True)
                    for gi_ in range(16):
                        k.do("dve", lambda e: e.max_index(out=ix[:nt, gi_, 8:16], in_max=v16[:nt, gi_, 8:16], in_values=sc2a[gi_][:nt, :]),
                             r=[sc2a[gi_], v16], w=[ix], nowaw=True)
                def tk_tail(t0, nt, v16, ix):
                    k.do("pool", lambda e: e.tensor_copy(out=ixf[:nt, :, :], in_=ix[:nt, :, :]), r=[ix], w=[ixf])
                    v4 = v16[:nt, :, :].rearrange("p (h w) a -> p h w a", w=2)
                    i4 = ixf[:nt, :, :].rearrange("p (h w) a -> p h w a", w=2)
                    S4 = [nt, 8, 16, 16]
                    k.do("pool", lambda e: e.tensor_tensor(out=cand[:nt, :, :].rearrange("p h (a b) -> p h a b", b=16),
                                                          in0=bc(v4[:, :, 0, :].unsqueeze(3), S4), in1=bc(v4[:, :, 1, :].unsqueeze(2), S4), op=ALU.add),
                         r=[v16], w=[cand])
                    for h in range(8):
                        k.do("dve", lambda e: e.max(out=sv[:nt, h, 0:8], in_=cand[:nt, h, :]), r=[cand], w=[sv], nowaw=True)
                    for h in range(8):
                        k.do("dve", lambda e: e.max_index(out=ci[:nt, h, 0:8], in_max=sv[:nt, h, 0:8], in_values=cand[:nt, h, :]), r=[cand, sv], w=[ci], nowaw=True)
                    for h in range(8):
                        k.do("dve", lambda e: e.match_replace(out=cand2a[h][:nt, :], in_to_replace=sv[:nt, h, 0:8], in_values=cand[:nt, h, :],
                                                              imm_value=-1e30), r=[cand, sv], w=[cand2a[h]])
                    for h in range(8):
                        k.do("dve", lambda e: e.max(out=sv[:nt, h, 8:16], in_=cand2a[h][:nt, :]), r=[cand2a[h]], w=[sv], nowaw=True)
                    for h in range(8):
                        k.do("dve", lambda e: e.max_index(out=ci[:nt, h, 8:16], in_max=sv[:nt, h, 8:16], in_values=cand2a[h][:nt, :]),
                             r=[cand2a[h], sv], w=[ci], nowaw=True)
                    civ = ci[:nt, :, :].rearrange("p h k -> p (h k)")
                    k.do("dve", lambda e: e.tensor_single_scalar(out=sl_i[:nt, 0, :], in_=civ, scalar=4, op=ALU.logical_shift_right), r=[ci], w=[sl_i])
                    k.do("dve", lambda e: e.tensor_single_scalar(out=sl_i[:nt, 1, :], in_=civ, scalar=15, op=ALU.bitwise_and), r=[ci], w=[sl_i])
                    k.do("dve", lambda e: e.tensor_copy(out=sl_f[:nt, :, :], in_=sl_i[:nt, :, :]), r=[sl_i], w=[sl_f])
                    for w_ in range(2):
                        eq = eqs[w_]
                        slv = sl_f[:nt, w_, :].rearrange("p (h k) -> p h k", k=16)
                        k.do("dve", lambda e: e.tensor_tensor(out=eq[:nt], in0=bc(slv.unsqueeze(3), S4),
                                                              in1=bc(io128[:nt, 0:16].unsqueeze(1).unsqueeze(1), S4), op=ALU.is_equal),
                             r=[sl_f, io128], w=[eq])
                        k.do("pool", lambda e: e.tensor_tensor(out=eq[:nt], in0=eq[:nt], in1=bc(i4[:, :, w_, :].unsqueeze(2), S4), op=ALU.mult),
                             r=[eq, ixf], w=[eq])
                        k.do("dve", lambda e: e.tensor_reduce(out=sel[:nt, w_, :].rearrange("p (h k) -> p h k", k=16), in_=eq[:nt],
                                                              axis=AX.X, op=ALU.add), r=[eq], w=[sel])
                    k.do("dve", lambda e: e.tensor_tensor(out=ex[:nt, :].rearrange("p (h k) -> p h k", k=16), in0=sv[:nt, :, :],
                                                          in1=bc(sv[:nt, :, 0:1], [nt, 8, 16]), op=ALU.subtract), r=[sv], w=[ex])
                    k.do("act", lambda e: e.activation(out=ex[:nt, :], in_=ex[:nt, :], func=AF.Exp), r=[ex], w=[ex])
                    k.do("dve", lambda e: e.tensor_reduce(out=zz[:nt, :], in_=ex[:nt, :].rearrange("p (h k) -> p h k", k=16), axis=AX.X, op=ALU.add),
                         r=[ex], w=[zz])
                    k.do("dve", lambda e: e.reciprocal(out=rz[:nt, :], in_=zz[:nt, :]), r=[zz], w=[rz])
                    k.do("dve", lambda e: e.tensor_tensor(out=sel[:nt, 2, :].rearrange("p (h k) -> p h k", k=16),
                                                          in0=ex[:nt, :].rearrange("p (h k) -> p h k", k=16),
                                                          in1=bc(rz[:nt, :].unsqueeze(2), [nt, 8, 16]), op=ALU.mult), r=[ex, rz], w=[sel])
                    p = pst.next()
                    for w_ in range(3):
                        k.do("pe", lambda e: e.transpose(out=p[:, w_ * 128:w_ * 128 + nt], in_=sel[:nt, w_, :], identity=identf[:nt, :nt]),
                             r=[sel, identf], w=[p], inc=(w_ == 2))
                    for w_, dst in enumerate((aT, bT, gT)):
                        k.do("act", lambda e: e.copy(out=dst[:, t0:t0 + nt], in_=p[:, w_ * 128:w_ * 128 + nt]), r=[p], w=[dst])

                prev_t = None
                for ti_, (t0, nt) in enumerate(TILES_O):
                    vb = (v16s[ti_ % 2], ixs[ti_ % 2])
                    tk_s1(t0, nt, *vb)
                    if prev_t is not None:
                        tk_tail(*prev_t)
                    prev_t = (t0, nt) + vb
                tk_tail(*prev_t)
                k.barrier()
            k.mark("peer_topk_done")
            with ExitStack() as pg_:
                Gst_r = k.ring(pg_, "Gst", [128, 128, 128], BF16, 2)
                P1_r = k.ring(pg_, "P1h", [128, 16, 128], BF16, 2)
                Qe_r = k.ring(pg_, "Qe", [128, 16, 128], BF16, 2)
                Q2_r = k.ring(pg_, "Q2g", [128, 16, 128], BF16, 2)
                psG = k.ring(pg_, "psG", [128, 4, 128], F32, 4, psum=True)
                S3 = [128, 16, 128]
                io_bf = k.sb(pg_, "io_bf", [128, 128], BF16)
                a_bf = k.sb(pg_, "a_bf", [128, NO], BF16)
                b_bf = k.sb(pg_, "b_bf", [128, NO], BF16)
                g_bf = k.sb(pg_, "g_bf", [128, NO], BF16)
                k.do("dve", lambda e: e.tensor_copy(out=io_bf[:, :], in_=io128[:, :]), r=[io128], w=[io_bf])
                k.do("dve", lambda e: e.tensor_copy(out=a_bf[:, :], in_=aT[:, :]), r=[aT], w=[a_bf])
                k.do("dve", lambda e: e.tensor_copy(out=b_bf[:, :], in_=bT[:, :]), r=[bT], w=[b_bf])
                k.do("dve", lambda e: e.tensor_copy(out=g_bf[:, :], in_=gT[:, :]), r=[gT], w=[g_bf])
                for (t0, nt) in TILES_O:
                    Gs = Gst_r.next()
                    for t16 in range(nt // 16):
                        tb = t0 + 16 * t16
                        P1 = P1_r.next(); Qe = Qe_r.next(); Q2 = Q2_r.next()
                        k.do("dve", lambda e: e.tensor_tensor(out=P1[:, :, :], in0=bc(io_bf[:, :].unsqueeze(1), S3),
                                                              in1=bc(a_bf[:, tb:tb + 16].unsqueeze(2), S3), op=ALU.is_equal), r=[io_bf, a_bf], w=[P1])
                        k.do("dve", lambda e: e.tensor_tensor(out=Qe[:, :, :], in0=bc(io_bf[:, :].unsqueeze(1), S3),
                                                              in1=bc(b_bf[:, tb:tb + 16].unsqueeze(2), S3), op=ALU.is_equal), r=[io_bf, b_bf], w=[Qe])
                        k.do("pool", lambda e: e.tensor_tensor(out=Q2[:, :, :], in0=Qe[:, :, :],
                                                               in1=bc(g_bf[:, tb:tb + 16].unsqueeze(2), S3), op=ALU.mult), r=[Qe, g_bf], w=[Q2])
                        for j4 in range(4):
                            p = psG.next()
                            for j in range(4):
                                jj = j4 * 4 + j
                                k.do("pe", lambda e: e.matmul(out=p[:, j, :], lhsT=Q2[:, jj, :], rhs=P1[:, jj, :], start=True, stop=True),
                                     r=[Q2, P1], w=[p], inc=(j == 3))
                            tl = 16 * t16 + 4 * j4
                            k.do("act", lambda e: e.copy(out=Gs[:, :, tl:tl + 4], in_=p[:, :, :].rearrange("p t i -> p i t")), r=[p], w=[Gs])
                    for q4 in range(4):
                        k.dma("sp", G_d.t[:, q4 * 32:(q4 + 1) * 32, t0:t0 + nt], Gs[:, q4 * 32:(q4 + 1) * 32, 0:nt], r=[Gs], scratch=G_d)
                k.barrier()
            k.mark("peer_G_done")
            acc = [k.sb(pes, "acc%d" % i, [128, D], F32) for i in range(9)]
            with ExitStack() as pm_:
                wur = k.ring(pm_, "wur", [128, D], BF16, 2)
                wuT_r = k.ring(pm_, "wuT", [128, 16, 128], BF16, 3)
                wvr = k.ring(pm_, "wvr", [128, D], BF16, 8)
                AT_r = k.ring(pm_, "AT", [128, 4, NO], BF16, 2)
                gtc = k.ring(pm_, "gtc", [128, NO], BF16, 3)
                gl_r = k.ring(pm_, "gl", [128, NO], BF16, 2)
                psT = k.ring(pm_, "psT", [128, 8, 128], BF16, 2, psum=True)
                psU = k.ring(pm_, "psU", [128, 512], F32, 3, psum=True)
                psD = k.ring(pm_, "psD", [128, 512], F32, 3, psum=True)
                nev = [0]

                def emit_U(gi):
                    AT = AT_r.next()
                    wvs = []
                    for ec in range(4):
                        i1 = 4 * gi + ec
                        raw = wur.next()
                        k.dma("pool", raw[:, :], w_u[i1 * 128:(i1 + 1) * 128, :], w=[raw])
                        gt = gtc.next()
                        k.dma("sp", gt[:, :], G_d.t[:, i1, :], r=[G_d], w=[gt])
                        wT = wuT_r.next()
                        for g4 in range(2):
                            p = psT.next()
                            for j in range(8):
                                dc = g4 * 8 + j
                                k.do("pe", lambda e: e.transpose(out=p[:, j, :], in_=raw[:, dc * 128:(dc + 1) * 128], identity=ident[:, :]),
                                     r=[raw, ident], w=[p], inc=(j == 7))
                            nev[0] += 1
                            if nev[0] % 2:
                                k.do("act", lambda e: e.copy(out=wT[:, g4 * 8:(g4 + 1) * 8, :], in_=p[:, :, :]), r=[p], w=[wT])
                            else:
                                k.do("dve", lambda e: e.tensor_copy(out=wT[:, g4 * 8:(g4 + 1) * 8, :], in_=p[:, :, :]), r=[p], w=[wT])
                        gl = gl_r.next()
                        for (c0, n) in BLK_O:
                            pu = psU.next()
                            for dc in range(16):
                                k.do("pe", lambda e: e.matmul(out=pu[:, :n], lhsT=wT[:, dc, :], rhs=h2T[:, dc, c0:c0 + n], start=(dc == 0), stop=(dc == 15)),
                                     r=[wT, h2T], w=[pu], inc=(dc == 15))
                            k.do("act", lambda e: e.activation(out=gl[:, c0:c0 + n], in_=pu[:, :n], func=AF.Gelu), r=[pu], w=[gl])
                        k.do("dve", lambda e: e.tensor_tensor(out=AT[:, ec, :], in0=gl[:, :], in1=gt[:, :], op=ALU.mult), r=[gl, gt], w=[AT])
                        wv = wvr.next()
                        k.dma("pool", wv[:, :], w_v[i1 * 128:(i1 + 1) * 128, :], w=[wv])
                        wvs.append(wv)
                    return AT, wvs

                def emit_down(gi, AT, wvs):
                    for ti, (t0, nt) in enumerate(TILES_O):
                        for dq in range(4):
                            pd = psD.next()
                            for ec in range(4):
                                k.do("pe", lambda e: e.matmul(out=pd[:nt, :], lhsT=AT[:, ec, t0:t0 + nt], rhs=wvs[ec][:, dq * 512:(dq + 1) * 512],
                                                              start=(ec == 0), stop=(ec == 3)), r=[AT, wvs[ec]], w=[pd], inc=(ec == 3))
                            a = acc[ti]
                            if gi == 0:
                                k.do("act", lambda e: e.copy(out=a[:nt, dq * 512:(dq + 1) * 512], in_=pd[:nt, :]), r=[pd], w=[a])
                            else:
                                k.do("dve", lambda e: e.tensor_tensor(out=a[:nt, dq * 512:(dq + 1) * 512], in0=pd[:nt, :],
                                                                      in1=a[:nt, dq * 512:(dq + 1) * 512], op=ALU.add), r=[pd, a], w=[a])

                prev = None
                for gi in range(32):
                    cur = emit_U(gi)
                    if prev is not None:
                        emit_down(gi - 1, *prev)
                    prev = cur
                emit_down(31, *prev)
                k.barrier()
            k.mark("peer_main_done")
            with ExitStack() as fz:
                final_phase(fz, acc)
            k.barrier()
        k.barrier()
        k.finish()
    nc._in_names = in_names
    nc._ninstr = k.ninstr
    return nc


def _rope_tables(pos):
    half = 32
    inv = 1.0 / (10000.0 ** (np.arange(half, dtype=np.float32) / half))
    ang = pos.astype(np.float32)[:, None] * inv[None, :].astype(np.float32)
    cos = np.cos(ang).astype(np.float32).T
    sin = np.sin(ang).astype(np.float32).T
    cosT = np.concatenate([cos, cos], axis=0)
    sinT = np.concatenate([-sin, sin], axis=0)
    return np.ascontiguousarray(cosT), np.ascontiguousarray(sinT)


def _fm(v, nchunk):
    return np.ascontiguousarray(np.asarray(v, np.float32).reshape(nchunk, 128).T)


_CACHE = {}


def prepare(inputs):
    f = lambda a: np.ascontiguousarray(np.asarray(a, dtype=np.float32))
    xp = f(inputs["x_prompt"])[0]
    xs = f(inputs["x_sample"])
    shared = {
        "xall": xp,
        "w_ada": f(inputs["w_ada"])[0],
        "b_adaT": _fm(f(inputs["b_ada"])[0], 96),
        "b_ada": f(inputs["b_ada"]).reshape(1, -1),
        "g_n1T": _fm(f(inputs["g_n1"])[0], 16),
        "g_n2T": _fm(f(inputs["g_n2"])[0], 16),
        "g_qT": _fm(f(inputs["g_q"])[0], 4),
        "g_kvT": _fm(f(inputs["g_kv"])[0], 4),
        "w_in": f(inputs["w_in"])[0],
        "w_uq": f(inputs["w_uq"])[0].reshape(512, 1536),
        "w_uk": f(inputs["w_uk"])[0].reshape(512, 1024),
        "w_uv": f(inputs["w_uv"])[0].reshape(512, 1024),
        "w_oa": f(inputs["w_oa"])[0],
        "w_ob": f(inputs["w_ob"])[0],
        "w_o": f(inputs["w_o"])[0],
        "w_pq": f(inputs["w_pq"])[0],
        "w_convT": np.ascontiguousarray(f(inputs["w_conv"])[0].reshape(3, 8, 128).transpose(2, 1, 0)),
        "b_convT": _fm(f(inputs["b_conv"])[0], 8),
        "sub_k1": f(inputs["sub_k1"])[0].reshape(1024, 128),
        "sub_k2": f(inputs["sub_k2"])[0].reshape(1024, 128),
        "w_u": f(inputs["w_u"])[0],
        "w_v": f(inputs["w_v"])[0],
        "g_f": f(inputs["g_f"]).reshape(1, D),
    }
    cosk, sink = _rope_tables(np.arange(8192))
    shared["cosk"] = cosk
    shared["sink"] = sink
    cp = f(inputs["c_prompt"])
    cs = f(inputs["c_sample"])
    cache_ckv = f(inputs["cache_ckv"])[0]
    cache_kr = f(inputs["cache_krope"])[0]
    sconv = f(inputs["state_conv"])[0]
    maps = []
    for c in range(NCORES):
        m = dict(shared)
        lt = np.arange(1024)
        pos_own = (8 * (lt // 64) + c) * 64 + lt % 64
        xo = np.zeros((NT, D), np.float32)
        xo[:1024] = xp[pos_own]
        xo[1024:1088] = xs[4 * c:4 * c + 4].reshape(64, D)
        hv = np.zeros((128, 32), np.float32)
        for j in range(16):
            for i in range(2):
                p = (8 * j + c) * 64 - 2 + i
                if p >= 0:
                    xo[1088 + 2 * j + i] = xp[p]
                    hv[:, 2 * j + i] = 1.0
        m["xown"] = xo
        m["hvalid"] = hv
        c5 = np.concatenate([cp, cs[4 * c:4 * c + 4]], axis=0)
        m["c5T"] = np.ascontiguousarray(c5.reshape(5, 16, 128).transpose(2, 1, 0))
        m["cckv"] = np.ascontiguousarray(cache_ckv[4 * c:4 * c + 4])
        m["ckr"] = np.ascontiguousarray(cache_kr[4 * c:4 * c + 4])
        sc = sconv[4 * c:4 * c + 4].reshape(4, 2, 8, 128).transpose(3, 2, 0, 1)
        m["sconvT"] = np.ascontiguousarray(sc)
        posq = np.concatenate([pos_own, np.tile(1024 + np.arange(16), 4), np.zeros(32, np.int64)])
        cq, sq = _rope_tables(posq)
        m["cosq"] = cq
        m["sinq"] = sq
        dm = np.zeros((128, 4), np.float32)
        for kt in range(4):
            for half in range(2):
                if 2 * kt + half > c:
                    dm[64 * half:64 * half + 64, kt] = NEG
        m["dmask"] = dm
        maps.append(m)
    return maps


def assemble(results):
    y_p = np.zeros((1, 8192, D), np.float32)
    y_s = np.zeros((32, 16, D), np.float32)
    ckv_p = np.zeros((1, 1, 8192, 512), np.float32)
    kr_p = np.zeros((1, 1, 8192, 64), np.float32)
    conv_p = np.zeros((1, 1, 2, 1024), np.float32)
    ckv_s = np.zeros((1, 32, 16, 512), np.float32)
    kr_s = np.zeros((1, 32, 16, 64), np.float32)
    conv_s = np.zeros((1, 32, 2, 1024), np.float32)
    for c in range(NCORES):
        r = results[c]
        lt = np.arange(1024)
        pos_own = (8 * (lt // 64) + c) * 64 + lt % 64
        y_p[0, pos_own] = r["o_y"][:1024]
        y_s[4 * c:4 * c + 4] = r["o_y"][1024:1088].reshape(4, 16, D)
        ckv_p[0, 0, pos_own] = r["o_ckv"][:1024]
        ckv_s[0, 4 * c:4 * c + 4] = r["o_ckv"][1024:1088].reshape(4, 16, 512)
        kr_p[0, 0, pos_own] = r["o_kr"][:1024]
        kr_s[0, 4 * c:4 * c + 4] = r["o_kr"][1024:1088].reshape(4, 16, 64)
        if c == 7:
            conv_p[0, 0] = r["o_conv"][0:2]
        conv_s[0, 4 * c:4 * c + 4] = r["o_conv"][2:10].reshape(4, 2, 1024)
    return (y_p, y_s, ckv_p, kr_p, conv_p, ckv_s, kr_s, conv_s)


def kernel(**inputs):
    maps = prepare(inputs)
    if "nc" not in _CACHE:
        _CACHE["nc"] = build()
    nc = _CACHE["nc"]
    maps = [{n: m[n] for n in nc._in_names} for m in maps]
    res = run_bass_kernel_spmd(nc, maps, core_ids=list(range(NCORES)))
    return assemble(res.results)
```

```python
import numpy as np
import concourse.bass as bass
import concourse.mybir as mybir
from concourse.bass_utils import run_bass_kernel_spmd
from contextlib import ExitStack

F32 = mybir.dt.float32
BF16 = mybir.dt.bfloat16
U32 = mybir.dt.uint32
I32 = mybir.dt.int32
AF = mybir.ActivationFunctionType
ALU = mybir.AluOpType
AX = mybir.AxisListType

NCORES = 8
D = 2048
NT = 1120
NO = 1088
EPS = 1e-6
SCALE = 192.0 ** -0.5
NEG = -30000.0
BLK_T = [(0, 512), (512, 512), (1024, 96)]
BLK_O = [(0, 512), (512, 512), (1024, 64)]
TILES_O = [(i * 128, 128) for i in range(8)] + [(1024, 64)]


class Tl:
    __slots__ = ("t", "w", "r", "dsem", "ssem", "name")

    def __init__(self, t, name):
        self.t = t
        self.name = name
        self.w = None
        self.r = []
        self.dsem = None
        self.ssem = None

    def __getitem__(self, k):
        return self.t[k]


class DSem:
    def __init__(self, sem):
        self.sem = sem
        self.issued = 0


class Eng:
    def __init__(self, name, h, sem):
        self.name = name
        self.h = h
        self.sem = sem
        self.count = 0
        self.waited = {}


class Ring:
    def __init__(self, tiles):
        self.tiles = tiles
        self.i = 0

    def next(self):
        t = self.tiles[self.i % len(self.tiles)]
        self.i += 1
        return t


class K:
    def __init__(self, nc, es):
        self.nc = nc
        self.es = es
        self.eng = {}
        for name, h in (("pe", nc.tensor), ("act", nc.scalar), ("dve", nc.vector),
                        ("pool", nc.gpsimd), ("sp", nc.sync)):
            sem = es.enter_context(nc.semaphore("prog_" + name))
            self.eng[name] = Eng(name, h, sem)
        self.dsems = []
        self.store_sems = []
        self.ninstr = 0
        self.uid = 0
        import os
        self.limit = int(os.environ.get("KLIMIT", "100000000"))

    def sb(self, es, name, shape, dt):
        self.uid += 1
        nm = "%s_%d" % (name, self.uid)
        return Tl(es.enter_context(self.nc.sbuf_tensor(nm, shape, dt)), nm)

    def ps(self, es, name, shape, dt):
        self.uid += 1
        nm = "%s_%d" % (name, self.uid)
        return Tl(es.enter_context(self.nc.psum_tensor(nm, shape, dt)), nm)

    def ring(self, es, name, shape, dt, n, psum=False):
        f = self.ps if psum else self.sb
        return Ring([f(es, name, shape, dt) for _ in range(n)])

    def dram(self, name, shape, dt):
        t = self.nc.dram_tensor(name, shape, dt, kind="Internal").ap()
        return Tl(t, name)

    def newsem(self, name):
        self.uid += 1
        ds = DSem(self.es.enter_context(self.nc.semaphore("%s_%d" % (name[:20], self.uid))))
        self.dsems.append(ds)
        return ds

    def getsem(self, name):
        if not hasattr(self, "sem_pool"):
            self.sem_pool = []
            self.sem_rr = 0
        if len(self.sem_pool) < 72:
            self.sem_pool.append(self.newsem(name))
            return self.sem_pool[-1]
        self.sem_rr += 1
        return self.sem_pool[self.sem_rr % len(self.sem_pool)]

    def _wait(self, E, tok):
        if tok is None:
            return
        if tok[0] == "e":
            _, P, val = tok
            if P is E and E.name in ("pe", "sp"):
                return
            sem = P.sem
        else:
            _, ds, val = tok
            sem = ds.sem
            val = max(val, ds.issued)
        key = id(sem)
        if E.waited.get(key, 0) >= val:
            return
        E.waited[key] = val
        E.h.wait_ge(sem, val)

    def _deps(self, E, r, w):
        for t in r:
            self._wait(E, t.w)
        for t in w:
            self._wait(E, t.w)
            for tok in t.r:
                self._wait(E, tok)

    def do(self, en, fn, r=(), w=(), inc=True, nowaw=False):
        if self.ninstr >= self.limit:
            return None
        E = self.eng[en]
        if nowaw:
            for t in w:
                assert t.w is None or t.w[0] != "e" or t.w[1] is E or not t.r or True
            self._deps(E, r, ())
            for t in w:
                for tok in t.r:
                    self._wait(E, tok)
                if t.w is not None and not (t.w[0] == "e" and t.w[1] is E):
                    self._wait(E, t.w)
        else:
            self._deps(E, r, w)
        ins = fn(E.h)
        self.ninstr += 1
        if inc:
            E.count += 1
            ins.then_inc(E.sem, 1)
            tok = ("e", E, E.count)
        else:
            tok = ("e", E, E.count + 1)
        for t in r:
            t.r.append(tok)
        for t in w:
            t.w = tok
            t.r = []
        return tok

    def dma(self, q, out, in_, r=(), w=(), store=False, scratch=None, **kw):
        if self.ninstr >= self.limit:
            return None
        E = self.eng[q]
        if scratch is not None:
            self._deps(E, r, ())
            if scratch.dsem is None:
                scratch.dsem = self.newsem("sc_" + scratch.name)
            ds = scratch.dsem
        elif store:
            self._deps(E, r, ())
            src = r[0]
            if src.ssem is None:
                src.ssem = self.newsem("st_" + src.name)
                self.store_sems.append(src.ssem)
            ds = src.ssem
        else:
            self._deps(E, r, w)
            dst = w[0]
            if dst.dsem is None:
                dst.dsem = self.getsem("ld_" + dst.name)
            ds = dst.dsem
        ins = E.h.dma_start(out=out, in_=in_, **kw)
        self.ninstr += 1
        ds.issued += 16
        ins.then_inc(ds.sem, 16)
        tok = ("d", ds, ds.issued)
        for t in r:
            t.r.append(tok)
        for t in w:
            t.w = tok
            t.r = []
        if scratch is not None:
            scratch.w = tok
        return tok

    def mark(self, name):
        import os
        if os.environ.get("KVERBOSE"):
            print("MARK", name, self.ninstr, flush=True)

    def barrier(self):
        for E in self.eng.values():
            for P in self.eng.values():
                if P is E or P.name == "sp" or P.count == 0:
                    continue
                if E.waited.get(id(P.sem), 0) < P.count:
                    E.waited[id(P.sem)] = P.count
                    E.h.wait_ge(P.sem, P.count)
            for ds in self.dsems:
                if ds.issued and E.waited.get(id(ds.sem), 0) < ds.issued:
                    E.waited[id(ds.sem)] = ds.issued
                    E.h.wait_ge(ds.sem, ds.issued)

    def finish(self):
        E = self.eng["sp"]
        for ds in self.store_sems:
            E.h.wait_ge(ds.sem, ds.issued)


def bc(ap, shape):
    return ap.broadcast_to(shape)


STAGES = ["p1b", "p1", "kp", "att", "mrg", "all"]


def build(stop_after="all", dbg=False):
    nc = bass.Bass("TRN2", target_bir_lowering=False)
    in_names = []
    nc_in_names = in_names

    def need(stage):
        return STAGES.index(stop_after) >= STAGES.index(stage)

    BIG = {"xall": "kp", "w_u": "all", "w_v": "all", "w_pq": "all", "w_o": "mrg", "w_oa": "mrg"}

    def din(name, shape, dt=F32):
        if name in BIG and not need(BIG[name]):
            return None
        in_names.append(name)
        return nc.dram_tensor(name, shape, dt, kind="ExternalInput").ap()

    def dout(name, shape, dt=F32):
        return nc.dram_tensor(name, shape, dt, kind="ExternalOutput").ap()

    xown = din("xown", [NT, D])
    xall = din("xall", [8192, D])
    c5T = din("c5T", [128, 16, 5])
    w_ada = din("w_ada", [D, 6 * D])
    b_adaT = din("b_adaT", [128, 96])
    b_ada = din("b_ada", [1, 6 * D])
    g_n1T = din("g_n1T", [128, 16])
    g_n2T = din("g_n2T", [128, 16])
    g_qT = din("g_qT", [128, 4])
    g_kvT = din("g_kvT", [128, 4])
    w_in = din("w_in", [D, 8256])
    w_uq = din("w_uq", [512, 1536])
    w_uk = din("w_uk", [512, 1024])
    w_uv = din("w_uv", [512, 1024])
    w_oa = din("w_oa", [1024, D])
    w_ob = din("w_ob", [1024, D])
    w_o = din("w_o", [D, D])
    w_pq = din("w_pq", [D, D])
    w_convT = din("w_convT", [128, 8, 3])
    b_convT = din("b_convT", [128, 8])
    sub_k1 = din("sub_k1", [1024, 128])
    sub_k2 = din("sub_k2", [1024, 128])
    w_u = din("w_u", [16384, D])
    w_v = din("w_v", [16384, D])
    g_f = din("g_f", [1, D])
    cckv = din("cckv", [4, 1024, 512])
    ckr = din("ckr", [4, 1024, 64])
    sconvT = din("sconvT", [128, 8, 4, 2])
    cosq = din("cosq", [64, NT])
    sinq = din("sinq", [64, NT])
    cosk = din("cosk", [64, 8192])
    sink = din("sink", [64, 8192])
    dmask = din("dmask", [128, 4])
    hvalid = din("hvalid", [128, 32])

    o_y = dout("o_y", [NO, D])
    o_ckv = dout("o_ckv", [NO, 512])
    o_kr = dout("o_kr", [NO, 64])
    o_conv = dout("o_conv", [10, 1024])

    with ExitStack() as es:
        k = K(nc, es)
        modrows = k.dram("modrows", [5, 2 * D], F32)
        g0_d = k.dram("g0_d", [16, 128, NO], BF16)
        gb_d = k.dram("gb_d", [16, 128, NO], BF16)
        KT_d = k.dram("KT_d", [8, 16, 128, 512], BF16)
        V_d = k.dram("V_d", [8, 16, 128, 512], BF16)
        KTs_d = k.dram("KTs_d", [4, 8, 128, 1040], BF16)
        Vs_d = k.dram("Vs_d", [4, 8, 128, 9 * 128], BF16)
        x1_d = k.dram("x1_d", [NO, D], F32)
        G_d = k.dram("G_d", [128, 128, NO], BF16)
        cq_d = k.dram("cq_d", [128, 4, NO], BF16)
        krT_d = k.dram("krT_d", [64, 8192], BF16)
        krc_d = k.dram("krc_d", [64, 4, 1040], BF16)
        oT_d = k.dram("oT_d", [128, 8, NO], BF16)

        identf = k.sb(es, "identf", [128, 128], F32)
        ident = k.sb(es, "ident", [128, 128], BF16)
        ones = k.sb(es, "ones", [128, 128], BF16)
        k.do("pool", lambda e: e.memset(identf[:, :], 0.0), w=[identf])
        k.do("pool", lambda e: e.affine_select(out=identf[:, :], in_=identf[:, :], pattern=[[-1, 128]],
                                               compare_op=ALU.not_equal, fill=1.0, base=0, channel_multiplier=1),
             r=[identf], w=[identf])
        k.do("dve", lambda e: e.tensor_copy(out=ident[:, :], in_=identf[:, :]), r=[identf], w=[ident])
        k.do("dve", lambda e: e.memset(ones[:, :], 1.0), w=[ones])

        def ld(es_, name, shape, src, dt=F32, q="sp"):
            t = k.sb(es_, name, shape, dt)
            k.dma(q, t.t[tuple(slice(None) for _ in shape)], src, w=[t])
            return t

        c5f = ld(es, "c5f", [128, 16, 5], c5T[:, :, :])
        c5b = k.sb(es, "c5b", [128, 16, 5], BF16)
        k.do("dve", lambda e: e.tensor_copy(out=c5b[:, :, :], in_=c5f[:, :, :]), r=[c5f], w=[c5b])
        badT = ld(es, "badT", [128, 96], b_adaT[:, :])
        gn1 = ld(es, "gn1", [128, 16], g_n1T[:, :])
        gn2 = ld(es, "gn2", [128, 16], g_n2T[:, :])
        gq = ld(es, "gq", [128, 4], g_qT[:, :])
        gkv = ld(es, "gkv", [128, 4], g_kvT[:, :])
        modT = k.sb(es, "modT", [128, 64, 5], F32)
        A1 = k.sb(es, "A1", [128, 16, 5], F32)
        A2 = k.sb(es, "A2", [128, 16, 5], F32)

        fm_panels = {0: 0, 1: 4, 2: 8, 3: 12, 4: 16, 5: 20, 6: 24, 7: 28,
                     12: 32, 13: 36, 14: 40, 15: 44, 16: 48, 17: 52, 18: 56, 19: 60}

        def make_ada(aes):
            return {"wpan": k.ring(aes, "adapan", [128, 16, 1024], BF16, 2),
                    "ps": k.ring(aes, "psada", [128, 512], F32, 2, psum=True),
                    "b5": k.ring(aes, "b5", [5, 512], F32, 2),
                    "rowst": k.ring(aes, "rowst", [5, 512], F32, 2)}

        def ada_big(ad, bj):
            wp = ad["wpan"].next()
            k.dma("pool", wp[:, :, :], w_ada[:, bj * 1024:(bj + 1) * 1024].rearrange("(kc p) n -> p kc n", p=128), w=[wp])
            for sub in range(2):
                pi = 2 * bj + sub
                wo_ = sub * 512
                p = ad["ps"].next()
                if pi in fm_panels:
                    base = fm_panels[pi]
                    for m in range(4):
                        for kc in range(16):
                            k.do("pe", lambda e: e.matmul(out=p[:, m * 8:m * 8 + 5], lhsT=wp[:, kc, wo_ + m * 128:wo_ + (m + 1) * 128],
                                                          rhs=c5b[:, kc, :], start=(kc == 0), stop=(kc == 15)),
                                 r=[wp, c5b], w=[p], inc=(kc == 15 and m == 3))
                    for m in range(4):
                        cc = pi * 4 + m
                        k.do("dve", lambda e: e.tensor_scalar(out=modT[:, base + m, :], in0=p[:, m * 8:m * 8 + 5],
                                                              scalar1=badT[:, cc:cc + 1], scalar2=None, op0=ALU.add),
                             r=[p, badT], w=[modT])
                else:
                    for kc in range(16):
                        k.do("pe", lambda e: e.matmul(out=p[:5, :], lhsT=c5b[:, kc, :], rhs=wp[:, kc, wo_:wo_ + 512],
                                                      start=(kc == 0), stop=(kc == 15)),
                             r=[wp, c5b], w=[p], inc=(kc == 15))
                    bt = ad["b5"].next()
                    k.dma("sp", bt[:, :], bc(b_ada[0:1, pi * 512:(pi + 1) * 512], [5, 512]), w=[bt])
                    rs = ad["rowst"].next()
                    k.do("dve", lambda e: e.tensor_tensor(out=rs[:, :], in0=p[:5, :], in1=bt[:, :], op=ALU.add),
                         r=[p, bt], w=[rs])
                    co = (pi - 8) * 512 if pi < 12 else D + (pi - 20) * 512
                    k.dma("sp", modrows.t[:, co:co + 512], rs[:, :], r=[rs], scratch=modrows)

        def ada_finish(A, g, o):
            k.do("dve", lambda e: e.tensor_scalar(out=A[:, :, :], in0=modT[:, o:o + 16, :], scalar1=1.0, scalar2=None,
                                                  op0=ALU.add), r=[modT], w=[A])
            k.do("dve", lambda e: e.tensor_tensor(out=A[:, :, :], in0=A[:, :, :],
                                                  in1=bc(g[:, :].unsqueeze(2), [128, 16, 5]), op=ALU.mult),
                 r=[A, g], w=[A])

        with ExitStack() as pa:
            ad = make_ada(pa)
            for bj in range(4):
                ada_big(ad, bj)
            ada_finish(A1, gn1, 16)
            k.barrier()

        def make_front(fes, nx=3):
            fr = {}
            fr["x"] = k.ring(fes, "xt", [128, D], F32, nx)
            fr["xn"] = k.ring(fes, "xn", [128, D], BF16, 2)
            fr["junk"] = k.sb(fes, "junk", [128, D], BF16)
            fr["ss"] = k.ring(fes, "ss", [128, 1], F32, 4)
            fr["sd"] = k.ring(fes, "sd", [128, 1], F32, 4)
            fr["rs"] = k.ring(fes, "rs", [128, 1], F32, 4)
            fr["pt"] = k.ring(fes, "ptr", [128, 4, 128], BF16, 3, psum=True)
            fr["n"] = 0
            fr["pend"] = None
            return fr

        def front_from_sb(fr, xt, ntok, A, Bt, Bo, groups, hT, col0):
            ss = fr["ss"].next(); sd = fr["sd"].next(); rs = fr["rs"].next()
            junk = fr["junk"]
            k.do("act", lambda e: e.activation(out=junk[:ntok, :], in_=xt[:ntok, :], func=AF.Square,
                                               accum_out=ss[:ntok, 0:1]), r=[xt], w=[junk, ss])
            k.do("act", lambda e: e.activation(out=sd[:ntok, :], in_=ss[:ntok, :], func=AF.Sqrt, scale=1.0 / D, bias=EPS),
                 r=[ss], w=[sd])
            k.do("dve", lambda e: e.reciprocal(out=rs[:ntok, :], in_=sd[:ntok, :]), r=[sd], w=[rs])
            xn = fr["xn"].next()
            k.do("pool", lambda e: e.tensor_scalar(out=xn[:ntok, :], in0=xt[:ntok, :], scalar1=rs[:ntok, 0:1], scalar2=1.0,
                                                   op0=ALU.mult, op1=ALU.mult), r=[xt, rs], w=[xn])
            prevB = fr.get("pend")
            fr["pend"] = lambda: front_B(fr, xn, ntok, A, Bt, Bo, groups, hT, col0)
            if prevB is not None:
                prevB()

        def front_flush(fr):
            if fr.get("pend") is not None:
                fr["pend"]()
                fr["pend"] = None

        def front_B(fr, xn, ntok, A, Bt, Bo, groups, hT, col0):
            for g4 in range(4):
                p = fr["pt"].next()
                for j in range(4):
                    kc = g4 * 4 + j
                    k.do("pe", lambda e: e.transpose(out=p[:, j, :ntok], in_=xn[:ntok, kc * 128:(kc + 1) * 128],
                                                     identity=ident[:ntok, :ntok]),
                         r=[xn, ident], w=[p], inc=(j == 3))
                if groups is None:
                    fr["n"] += 1
                    if fr["n"] % 2 == 0:
                        k.do("dve", lambda e: e.tensor_copy(out=hT[:, g4 * 4:(g4 + 1) * 4, col0:col0 + ntok], in_=p[:, :, :ntok]), r=[p], w=[hT])
                    else:
                        k.do("act", lambda e: e.copy(out=hT[:, g4 * 4:(g4 + 1) * 4, col0:col0 + ntok], in_=p[:, :, :ntok]), r=[p], w=[hT])
                    continue
                for j in range(4):
                    kc = g4 * 4 + j
                    for (c0, n, r) in groups:
                        fr["n"] += 1
                        if fr["n"] % 2 == 0:
                            k.do("dve", lambda e: e.tensor_scalar(out=hT[:, kc, col0 + c0:col0 + c0 + n], in0=p[:, j, c0:c0 + n],
                                                                  scalar1=A[:, kc, r:r + 1], scalar2=Bt[:, Bo + kc, r:r + 1],
                                                                  op0=ALU.mult, op1=ALU.add), r=[p, A, Bt], w=[hT])
                        else:
                            k.do("act", lambda e: e.activation(out=hT[:, kc, col0 + c0:col0 + c0 + n], in_=p[:, j, c0:c0 + n],
                                                               func=AF.Identity, scale=A[:, kc, r:r + 1],
                                                               bias=Bt[:, Bo + kc, r:r + 1]), r=[p, A, Bt], w=[hT])

        def front(fr, src, ntok, A, Bo, groups, hT, col0):
            xt = fr["x"].next()
            k.dma("act", xt[:ntok, :], src, w=[xt])
            front_from_sb(fr, xt, ntok, A, modT, Bo, groups, hT, col0)

        G_P = [(0, 128, 0)]
        G_M = [(0, 16, 1), (16, 16, 2), (32, 16, 3), (48, 16, 4), (64, 32, 0)]

        def gemm_fm(pan, mlist, nk, hT, blocks, pspool, consume):
            for mi, (m0, msz) in enumerate(mlist):
                for bi, (c0, n) in enumerate(blocks):
                    p = pspool.next()
                    for kc in range(nk):
                        k.do("pe", lambda e: e.matmul(out=p[:msz, :n], lhsT=pan[:, kc, m0:m0 + msz], rhs=hT[:, kc, c0:c0 + n],
                                                      start=(kc == 0), stop=(kc == nk - 1)),
                             r=[pan, hT], w=[p], inc=(kc == nk - 1))
                    consume(mi, bi, c0, n, p)

        M4 = [(0, 128), (128, 128), (256, 128), (384, 128)]

        def rms_fm(res, raw, gT, blocks, pspool, out_f32=None, out_bf=None):
            for (c0, n) in blocks:
                sq = res["sq"].next()
                for kc in range(4):
                    k.do("act", lambda e: e.activation(out=sq[:, kc, :n], in_=raw[:, kc, c0:c0 + n], func=AF.Square),
                         r=[raw], w=[sq])
                p = pspool.next()
                for kc in range(4):
                    k.do("pe", lambda e: e.matmul(out=p[:, :n], lhsT=ones[:, :], rhs=sq[:, kc, :n], start=(kc == 0), stop=(kc == 3)),
                         r=[ones, sq], w=[p], inc=(kc == 3))
                sd = res["sd"].next(); rb = res["rb"].next()
                k.do("act", lambda e: e.activation(out=sd[:, :n], in_=p[:, :n], func=AF.Sqrt, scale=1.0 / 512, bias=EPS),
                     r=[p], w=[sd])
                k.do("dve", lambda e: e.reciprocal(out=rb[:, :n], in_=sd[:, :n]), r=[sd], w=[rb])
                for kc in range(4):
                    if out_f32 is not None:
                        k.do("dve", lambda e: e.scalar_tensor_tensor(out=out_f32[:, kc, c0:c0 + n], in0=raw[:, kc, c0:c0 + n],
                                                                     scalar=gT[:, kc:kc + 1], in1=rb[:, :n],
                                                                     op0=ALU.mult, op1=ALU.mult), r=[raw, gT, rb], w=[out_f32])
                    if out_bf is not None:
                        k.do("dve", lambda e: e.scalar_tensor_tensor(out=out_bf[:, kc, c0:c0 + n], in0=raw[:, kc, c0:c0 + n],
                                                                     scalar=gT[:, kc:kc + 1], in1=rb[:, :n],
                                                                     op0=ALU.mult, op1=ALU.mult), r=[raw, gT, rb], w=[out_bf])

        def make_rms(res_es, width):
            return {"sq": k.ring(res_es, "rsq", [128, 4, width], BF16, 2),
                    "sd": k.ring(res_es, "rsd", [128, width], F32, 2),
                    "rb": k.ring(res_es, "rrb", [128, width], F32, 2)}


        def final_phase(fes_, acc):
            x1r = k.ring(fes_, "fx1", [128, D], F32, 2)
            gtr = k.ring(fes_, "fgt", [128, D], F32, 2)
            yr = k.ring(fes_, "fy", [128, D], F32, 2)
            gfb = k.sb(fes_, "gfb", [128, D], F32)
            junk = k.sb(fes_, "fjunk", [128, D], BF16)
            ssr = k.ring(fes_, "fss", [128, 1], F32, 2)
            sdr = k.ring(fes_, "fsd", [128, 1], F32, 2)
            rsr = k.ring(fes_, "frs", [128, 1], F32, 2)
            k.dma("sp", gfb[:, :], bc(g_f[0:1, :], [128, D]), w=[gfb])
            def fin_A(ti, t0, nt):
                x1 = x1r.next()
                k.dma("sp", x1[:nt, :], x1_d.t[t0:t0 + nt, :], r=[x1_d], w=[x1])
                if acc is not None:
                    GT = gtr.next()
                    if t0 < 1024:
                        k.dma("sp", GT[:nt, :], bc(modrows.t[0:1, D:2 * D], [nt, D]), r=[modrows], w=[GT])
                    else:
                        for bb in range(4):
                            k.dma("sp", GT[16 * bb:16 * bb + 16, :], bc(modrows.t[1 + bb:2 + bb, D:2 * D], [16, D]), r=[modrows], w=[GT])
                    a = acc[ti]
                    k.do("pool", lambda e: e.tensor_tensor(out=GT[:nt, :], in0=GT[:nt, :], in1=a[:nt, :], op=ALU.mult), r=[GT, a], w=[GT])
                    k.do("dve", lambda e: e.tensor_tensor(out=x1[:nt, :], in0=x1[:nt, :], in1=GT[:nt, :], op=ALU.add), r=[GT, x1], w=[x1])
                ss = ssr.next(); sd = sdr.next()
                k.do("act", lambda e: e.activation(out=junk[:nt, :], in_=x1[:nt, :], func=AF.Square, accum_out=ss[:nt, 0:1]), r=[x1], w=[junk, ss])
                k.do("act", lambda e: e.activation(out=sd[:nt, :], in_=ss[:nt, :], func=AF.Sqrt, scale=1.0 / D, bias=EPS), r=[ss], w=[sd])
                return (x1, sd, t0, nt)

            def fin_B(x1, sd, t0, nt):
                rs = rsr.next()
                k.do("dve", lambda e: e.reciprocal(out=rs[:nt, :], in_=sd[:nt, :]), r=[sd], w=[rs])
                y = yr.next()
                k.do("dve", lambda e: e.scalar_tensor_tensor(out=y[:nt, :], in0=x1[:nt, :], scalar=rs[:nt, 0:1], in1=gfb[:nt, :],
                                                             op0=ALU.mult, op1=ALU.mult), r=[x1, rs, gfb], w=[y])
                k.dma("sp", o_y[t0:t0 + nt, :], y[:nt, :], r=[y], store=True)

            pendf = None
            for ti, (t0, nt) in enumerate(TILES_O):
                cur_ = fin_A(ti, t0, nt)
                if pendf is not None:
                    fin_B(*pendf)
                pendf = cur_
            fin_B(*pendf)

        ckvS = k.sb(es, "ckvS", [128, 4, 64], BF16)
        krS = k.sb(es, "krS", [64, 64], BF16)
        p1es = es.enter_context(ExitStack())
        cq_t = ld(p1es, "cosq", [64, NT], cosq[:, :])
        sq_t = ld(p1es, "sinq", [64, NT], sinq[:, :])
        hT1 = k.sb(p1es, "hT1", [128, 16, NT], BF16)
        with ExitStack() as fes:
            fr = make_front(fes)
            for tt in range(8):
                front(fr, xown[tt * 128:(tt + 1) * 128, :], 128, A1, 0, G_P, hT1, tt * 128)
            front(fr, xown[1024:1120, :], 96, A1, 0, G_M, hT1, 1024)
            front_flush(fr)
            k.barrier()

        wpool = k.ring(p1es, "winpan", [128, 16, 512], BF16, 2)
        pg = k.ring(p1es, "pg", [128, 512], F32, 4, psum=True)

        def load_pan(c0, ncols=512):
            wp = wpool.next()
            k.dma("pool", wp[:, :, :ncols], w_in[:, c0:c0 + ncols].rearrange("(kc p) n -> p kc n", p=128), w=[wp])
            return wp

        with ExitStack() as pb:
            cqT = k.sb(pb, "cqT", [128, 4, NO], BF16)
            raw4 = k.sb(pb, "raw4", [128, 4, NT], F32)
            nrm4 = k.sb(pb, "nrm4", [128, 4, NT], F32)
            rres = make_rms(pb, 512)
            ptp = k.ring(pb, "ptp", [128, 512], F32, 2, psum=True)
            ost = k.ring(pb, "ost", [128, 512], F32, 2)

            def cons_raw(mi, bi, c0, n, p):
                k.do("act", lambda e: e.copy(out=raw4[:, mi, c0:c0 + n], in_=p[:, :n]), r=[p], w=[raw4])

            wp = load_pan(0)
            gemm_fm(wp, M4, 16, hT1, BLK_T, pg, cons_raw)
            rms_fm(rres, raw4, gq, BLK_O, pg, out_bf=cqT)
            k.dma("sp", cq_d.t[:, :, :], cqT[:, :, :], r=[cqT], scratch=cq_d)
            wp = load_pan(512)
            gemm_fm(wp, M4, 16, hT1, BLK_T, pg, cons_raw)
            rms_fm(rres, raw4, gkv, BLK_O, pg, out_f32=nrm4)
            k.do("dve", lambda e: e.tensor_copy(out=ckvS[:, :, :], in_=nrm4[:, :, 1024:1088]), r=[nrm4], w=[ckvS])
            for (t0, nt) in TILES_O:
                p = ptp.next()
                for kc in range(4):
                    k.do("pe", lambda e: e.transpose(out=p[:nt, kc * 128:(kc + 1) * 128], in_=nrm4[:, kc, t0:t0 + nt],
                                                     identity=identf[:, :]), r=[nrm4, identf], w=[p], inc=(kc == 3))
                o = ost.next()
                k.do("act", lambda e: e.copy(out=o[:nt, :], in_=p[:nt, :]), r=[p], w=[o])
                k.dma("sp", o_ckv[t0:t0 + nt, :], o[:nt, :], r=[o], store=True)
            wp = wpool.next()
            src = w_in[:, 1024:1088].rearrange("(kc p) n -> p kc n", p=128)
            k.dma("pool", wp[:, :, 0:64], src, w=[wp])
            k.dma("pool", wp[:, :, 64:96], w_in[:, 1056:1088].rearrange("(kc p) n -> p kc n", p=128), w=[wp])
            k.dma("pool", wp[:, :, 96:128], w_in[:, 1024:1056].rearrange("(kc p) n -> p kc n", p=128), w=[wp])
            krf = k.sb(pb, "krf", [64, NT], F32)
            t1 = k.sb(pb, "kt1", [64, NT], F32)

            def cons_kr(mi, bi, c0, n, p):
                if mi == 0:
                    k.do("dve", lambda e: e.tensor_tensor(out=t1[:, c0:c0 + n], in0=p[:64, :n], in1=cq_t[:, c0:c0 + n], op=ALU.mult),
                         r=[p, cq_t], w=[t1])
                else:
                    k.do("dve", lambda e: e.tensor_tensor(out=krf[:, c0:c0 + n], in0=p[:64, :n], in1=sq_t[:, c0:c0 + n], op=ALU.mult),
                         r=[p, sq_t], w=[krf])
                    k.do("pool", lambda e: e.tensor_tensor(out=krf[:, c0:c0 + n], in0=krf[:, c0:c0 + n], in1=t1[:, c0:c0 + n], op=ALU.add),
                         r=[krf, t1], w=[krf])

            gemm_fm(wp, [(0, 64), (64, 64)], 16, hT1, BLK_T, pg, cons_kr)
            k.do("dve", lambda e: e.tensor_copy(out=krS[:, :], in_=krf[:, 1024:1088]), r=[krf], w=[krS])
            for (t0, nt) in TILES_O:
                p = ptp.next()
                k.do("pe", lambda e: e.transpose(out=p[:nt, 0:64], in_=krf[:, t0:t0 + nt], identity=identf[:64, :64]),
                     r=[krf, identf], w=[p])
                o = ost.next()
                k.do("act", lambda e: e.copy(out=o[:nt, 0:64], in_=p[:nt, 0:64]), r=[p], w=[o])
                k.dma("sp", o_kr[t0:t0 + nt, :], o[:nt, 0:64], r=[o], store=True)
            k.barrier()

        if stop_after == "p1b":
            k.barrier()
            k.finish()
            nc._in_names = in_names
            return nc

        k.mark("p1b_done")
        mbT = k.sb(p1es, "mbT", [128, 8, NO], BF16)
        wcv = ld(p1es, "wcv", [128, 8, 3], w_convT[:, :, :])
        bcv = ld(p1es, "bcv", [128, 8], b_convT[:, :])
        hv = ld(p1es, "hv", [128, 32], hvalid[:, :])
        scv = ld(p1es, "scv", [128, 8, 4, 2], sconvT[:, :, :, :])
        with ExitStack() as pcs:
            phT = k.sb(pcs, "phT", [128, 8, NT], BF16)
            pbT = k.sb(pcs, "pbT", [128, 8, NO], BF16)
            zpT = k.sb(pcs, "zpT", [128, 8, 1128], BF16)
            zout = k.sb(pcs, "zout", [128, 8, 10], F32)
            tmpz = k.ring(pcs, "tmpz", [128, 32], F32, 2)
            ycr = k.ring(pcs, "yc", [128, NO], F32, 2)
            ptz = k.ring(pcs, "ptz", [128, 512], F32, 2, psum=True)
            ozs = k.sb(pcs, "ozs", [128, 1024], F32)
            k.do("dve", lambda e: e.tensor_copy(
                out=zpT[:, :, 1056:1128].rearrange("p c (b s) -> p c b s", s=18)[:, :, :, 0:2], in_=scv[:, :, :, :]),
                r=[scv], w=[zpT])
            for pi in range(2):
                wp = load_pan(1088 + 512 * pi)

                def cons_ph(mi, bi, c0, n, p, pi=pi):
                    k.do("act", lambda e: e.copy(out=phT[:, 4 * pi + mi, c0:c0 + n], in_=p[:, :n]), r=[p], w=[phT])
                gemm_fm(wp, M4, 16, hT1, BLK_T, pg, cons_ph)
            for pi in range(2):
                wp = load_pan(2112 + 512 * pi)

                def cons_pb(mi, bi, c0, n, p, pi=pi):
                    n = min(n, NO - c0)
                    k.do("act", lambda e: e.copy(out=pbT[:, 4 * pi + mi, c0:c0 + n], in_=p[:, :n]), r=[p], w=[pbT])
                gemm_fm(wp, M4, 16, hT1, BLK_T, pg, cons_pb)
            k.mark("phpb_done")
            for pi in range(2):
                wp = load_pan(3136 + 512 * pi)

                def cons_pc(mi, bi, c0, n, p, pi=pi):
                    mc = 4 * pi + mi
                    if bi < 2:
                        dst = zpT[:, mc, 528 * bi:528 * (bi + 1)].rearrange("p (j s) -> p j s", s=66)[:, :, 2:66]
                        k.do("dve", lambda e: e.tensor_tensor(out=dst, in0=p[:, 0:512].rearrange("p (j s) -> p j s", s=64),
                                                              in1=phT[:, mc, c0:c0 + 512].rearrange("p (j s) -> p j s", s=64),
                                                              op=ALU.mult), r=[p, phT], w=[zpT])
                        if bi == 1:
                            k.do("dve", lambda e: e.tensor_tensor(out=zout[:, mc, 0:2], in0=p[:, 510:512], in1=phT[:, mc, 1022:1024],
                                                                  op=ALU.mult), r=[p, phT], w=[zout])
                    else:
                        dst = zpT[:, mc, 1056:1128].rearrange("p (j s) -> p j s", s=18)[:, :, 2:18]
                        k.do("dve", lambda e: e.tensor_tensor(out=dst, in0=p[:, 0:64].rearrange("p (j s) -> p j s", s=16),
                                                              in1=phT[:, mc, 1024:1088].rearrange("p (j s) -> p j s", s=16),
                                                              op=ALU.mult), r=[p, phT], w=[zpT])
                        k.do("dve", lambda e: e.tensor_tensor(out=zout[:, mc, 2:10].rearrange("p (j s) -> p j s", s=2),
                                                              in0=p[:, 0:64].rearrange("p (j s) -> p j s", s=16)[:, :, 14:16],
                                                              in1=phT[:, mc, 1024:1088].rearrange("p (j s) -> p j s", s=16)[:, :, 14:16],
                                                              op=ALU.mult), r=[p, phT], w=[zout])
                        tz = tmpz.next()
                        k.do("dve", lambda e: e.tensor_tensor(out=tz[:, :], in0=p[:, 64:96], in1=phT[:, mc, 1088:1120], op=ALU.mult),
                             r=[p, phT], w=[tz])
                        dsth = zpT[:, mc, 0:1056].rearrange("p (j s) -> p j s", s=66)[:, :, 0:2]
                        k.do("pool", lambda e: e.tensor_tensor(out=dsth, in0=tz[:, :].rearrange("p (j s) -> p j s", s=2),
                                                               in1=hv[:, :].rearrange("p (j s) -> p j s", s=2), op=ALU.mult),
                             r=[tz, hv], w=[zpT])
                        yc = ycr.next()
                        for (zv, yv) in ((zpT[:, mc, 0:1056].rearrange("p (j s) -> p j s", s=66), yc[:, 0:1024].rearrange("p (j s) -> p j s", s=64)),
                                         (zpT[:, mc, 1056:1128].rearrange("p (j s) -> p j s", s=18), yc[:, 1024:1088].rearrange("p (j s) -> p j s", s=16))):
                            L = 64 if zv.shape[2] == 66 else 16
                            k.do("dve", lambda e: e.tensor_scalar(out=yv, in0=zv[:, :, 0:L], scalar1=wcv[:, mc, 0:1], scalar2=bcv[:, mc:mc + 1],
                                                                  op0=ALU.mult, op1=ALU.add), r=[zpT, wcv, bcv], w=[yc])
                            k.do("dve", lambda e: e.scalar_tensor_tensor(out=yv, in0=zv[:, :, 1:L + 1], scalar=wcv[:, mc, 1:2], in1=yv,
                                                                         op0=ALU.mult, op1=ALU.add), r=[zpT, wcv, yc], w=[yc])
                            k.do("dve", lambda e: e.scalar_tensor_tensor(out=yv, in0=zv[:, :, 2:L + 2], scalar=wcv[:, mc, 2:3], in1=yv,
                                                                         op0=ALU.mult, op1=ALU.add), r=[zpT, wcv, yc], w=[yc])
                        k.do("pool", lambda e: e.tensor_tensor(out=mbT[:, mc, :], in0=yc[:, :], in1=pbT[:, mc, :], op=ALU.mult),
                             r=[yc, pbT], w=[mbT])
                gemm_fm(wp, M4, 16, hT1, BLK_T, pg, cons_pc)
            k.mark("pc_done")
            for half in range(2):
                p = ptz.next()
                for j in range(4):
                    mc = half * 4 + j
                    k.do("pe", lambda e: e.transpose(out=p[:10, j * 128:(j + 1) * 128], in_=zout[:, mc, :], identity=identf[:, :]),
                         r=[zout, identf], w=[p], inc=(j == 3))
                k.do("act", lambda e: e.copy(out=ozs[:10, half * 512:(half + 1) * 512], in_=p[:10, 0:512]), r=[p], w=[ozs])
            k.dma("sp", o_conv[:, :], ozs[:10, :], r=[ozs], store=True)
            k.barrier()

        k.mark("p1c_done")
        with ExitStack() as pds:
            g1T = k.sb(pds, "g1T", [128, 16, NO], BF16)
            gst = k.ring(pds, "gst", [128, NO], BF16, 3)
            cur = {}
            for pi in range(8):
                wp = load_pan(4160 + 512 * pi)

                def cons_g(mi, bi, c0, n, p, pi=pi):
                    n = min(n, NO - c0)
                    dc = (4 * pi + mi) % 16
                    if pi < 4:
                        if bi == 0:
                            cur["g"] = gst.next()
                        g = cur["g"]
                        k.do("act", lambda e: e.activation(out=g[:, c0:c0 + n], in_=p[:, :n], func=AF.Sigmoid), r=[p], w=[g])
                        if bi == 2:
                            k.dma("sp", g0_d.t[dc, :, :], g[:, :], r=[g], scratch=g0_d)
                    else:
                        k.do("act", lambda e: e.activation(out=g1T[:, dc, c0:c0 + n], in_=p[:, :n], func=AF.Sigmoid), r=[p], w=[g1T])
                gemm_fm(wp, M4, 16, hT1, BLK_T, pg, cons_g)
            for pi in range(4):
                wp = wpool.next()
                k.dma("pool", wp[:, 0:8, :], w_ob[:, pi * 512:(pi + 1) * 512].rearrange("(kc p) n -> p kc n", p=128), w=[wp])

                def cons_b(mi, bi, c0, n, p, pi=pi):
                    dc = 4 * pi + mi
                    if bi == 0:
                        cur["g"] = gst.next()
                    g = cur["g"]
                    k.do("dve", lambda e: e.tensor_tensor(out=g[:, c0:c0 + n], in0=p[:, :n], in1=g1T[:, dc, c0:c0 + n], op=ALU.mult),
                         r=[p, g1T], w=[g])
                    if bi == 2:
                        k.dma("sp", gb_d.t[dc, :, :], g[:, :], r=[g], scratch=gb_d)
                gemm_fm(wp, M4, 8, mbT, BLK_O, pg, cons_b)
            k.barrier()
        p1es.close()
        k.mark("p1_done")

        if stop_after == "p1":
            k.barrier()
            k.finish()
            nc._in_names = in_names
            return nc

        with ExitStack() as kes:
            wuk = k.sb(kes, "wuk", [128, 4, 1024], BF16)
            wuv = k.sb(kes, "wuv", [128, 4, 1024], BF16)
            k.dma("pool", wuk[:, :, :], w_uk.rearrange("(kc p) n -> p kc n", p=128), w=[wuk])
            k.dma("pool", wuv[:, :, :], w_uv.rearrange("(kc p) n -> p kc n", p=128), w=[wuv])
            pk = k.ring(kes, "pk", [128, 512], F32, 4, psum=True)

            def kv_gen(ckT, blocks, tiles, KTst, Vst):
                n_ev = [0]

                def ev(out, in_, rr, ww):
                    n_ev[0] += 1
                    if n_ev[0] % 2:
                        k.do("act", lambda e: e.copy(out=out, in_=in_), r=rr, w=ww)
                    else:
                        k.do("dve", lambda e: e.tensor_copy(out=out, in_=in_), r=rr, w=ww)
                for h in range(8):
                    for (c0, n) in blocks:
                        p = pk.next()
                        for kc in range(4):
                            k.do("pe", lambda e: e.matmul(out=p[:, :n], lhsT=wuk[:, kc, h * 128:(h + 1) * 128], rhs=ckT[:, kc, c0:c0 + n],
                                                          start=(kc == 0), stop=(kc == 3)), r=[wuk, ckT], w=[p], inc=(kc == 3))
                        ev(KTst[:, h, c0:c0 + n], p[:, :n], [p], [KTst])
                for ti, (t0, nk) in enumerate(tiles):
                    for hh in range(2):
                        p = pk.next()
                        for kc in range(4):
                            k.do("pe", lambda e: e.matmul(out=p[:nk, :], lhsT=ckT[:, kc, t0:t0 + nk], rhs=wuv[:, kc, hh * 512:(hh + 1) * 512],
                                                          start=(kc == 0), stop=(kc == 3)), r=[wuv, ckT], w=[p], inc=(kc == 3))
                        ev(Vst[:nk, ti, hh * 512:(hh + 1) * 512], p[:nk, :], [p], [Vst])

            with ExitStack() as kss:
                ptb = k.ring(kss, "ptb", [128, 4, 128], BF16, 2, psum=True)
                ckc_r = k.ring(kss, "ckc", [128, 8, 512], BF16, 2)
                ckcT_r = k.ring(kss, "ckcT", [128, 4, 1040], BF16, 2)
                KTs_r = k.ring(kss, "KTs", [128, 8, 1040], BF16, 2)
                Vs_r = k.ring(kss, "Vs", [128, 9, 1024], BF16, 2)
                krc_r = k.ring(kss, "krc", [128, 8, 64], BF16, 2)
                krcT_r = k.ring(kss, "krcT", [64, 1040], BF16, 2)
                ad2 = make_ada(kss)
                for bb in range(4):
                    ckc = ckc_r.next()
                    k.dma("pool", ckc[:, :, :], cckv[bb].rearrange("(t p) r -> p t r", p=128), w=[ckc])
                    ckcT = ckcT_r.next()
                    for t in range(8):
                        p = ptb.next()
                        for kc in range(4):
                            k.do("pe", lambda e: e.transpose(out=p[:, kc, :], in_=ckc[:, t, kc * 128:(kc + 1) * 128], identity=ident[:, :]),
                                 r=[ckc, ident], w=[p], inc=(kc == 3))
                        k.do("act" if t % 2 else "dve",
                             (lambda e: e.copy(out=ckcT[:, :, t * 128:(t + 1) * 128], in_=p[:, :, :])) if t % 2 else
                             (lambda e: e.tensor_copy(out=ckcT[:, :, t * 128:(t + 1) * 128], in_=p[:, :, :])), r=[p], w=[ckcT])
                    k.do("dve", lambda e: e.tensor_copy(out=ckcT[:, :, 1024:1040], in_=ckvS[:, :, bb * 16:(bb + 1) * 16]), r=[ckvS], w=[ckcT])
                    KTs = KTs_r.next(); Vs = Vs_r.next()
                    kv_gen(ckcT, [(0, 512), (512, 512), (1024, 16)], [(t * 128, 128) for t in range(8)] + [(1024, 16)], KTs, Vs)
                    k.dma("sp", KTs_d.t[bb].rearrange("h p n -> p h n"), KTs[:, :, :], r=[KTs], scratch=KTs_d)
                    k.dma("sp", Vs_d.t[bb].rearrange("h p (t d) -> p t h d", d=128),
                          Vs[:, :, :].rearrange("p t (h d) -> p t h d", d=128), r=[Vs], scratch=Vs_d)
                    krc = krc_r.next()
                    k.dma("pool", krc[:, :, :], ckr[bb].rearrange("(t p) r -> p t r", p=128), w=[krc])
                    krcT = krcT_r.next()
                    for half in range(2):
                        p = ptb.next()
                        for j in range(4):
                            t = half * 4 + j
                            k.do("pe", lambda e: e.transpose(out=p[:64, j, :], in_=krc[:, t, :], identity=ident[:, :]),
                                 r=[krc, ident], w=[p], inc=(j == 3))
                        k.do("act", lambda e: e.copy(out=krcT[:, half * 512:(half + 1) * 512], in_=p[:64, :, :].rearrange("p a b -> p (a b)")),
                             r=[p], w=[krcT])
                    k.do("dve", lambda e: e.tensor_copy(out=krcT[:, 1024:1040], in_=krS[:, bb * 16:(bb + 1) * 16]), r=[krS], w=[krcT])
                    k.dma("sp", krc_d.t[:, bb, :], krcT[:, :], r=[krcT], scratch=krc_d)
                    ada_big(ad2, 4 + 2 * bb)
                    ada_big(ad2, 5 + 2 * bb)
                ada_finish(A2, gn2, 48)
                k.barrier()
            k.mark("ks_done")

            with ExitStack() as kps:
                wkv = k.sb(kps, "wkv", [128, 16, 640], BF16)
                k.dma("pool", wkv[:, :, 0:512], w_in[:, 512:1024].rearrange("(kc p) n -> p kc n", p=128), w=[wkv])
                k.dma("pool", wkv[:, :, 512:576], w_in[:, 1024:1088].rearrange("(kc p) n -> p kc n", p=128), w=[wkv])
                k.dma("pool", wkv[:, :, 576:608], w_in[:, 1056:1088].rearrange("(kc p) n -> p kc n", p=128), w=[wkv])
                k.dma("pool", wkv[:, :, 608:640], w_in[:, 1024:1056].rearrange("(kc p) n -> p kc n", p=128), w=[wkv])
                fr = make_front(kps)
                Bbf = k.sb(kps, "Bbf", [128, 16, 2], BF16)
                kbias = k.sb(kps, "kbias", [128, 8], F32)
                k.do("dve", lambda e: e.tensor_copy(out=Bbf[:, :, 0:1], in_=modT[:, 0:16, 0:1]), r=[modT], w=[Bbf])
                for mi_, (m0_, msz_) in enumerate(M4 + [(512, 64), (576, 64)]):
                    pb_ = pk.next()
                    for kc in range(16):
                        k.do("pe", lambda e: e.matmul(out=pb_[:msz_, 0:1], lhsT=wkv[:, kc, m0_:m0_ + msz_], rhs=Bbf[:, kc, 0:1],
                                                      start=(kc == 0), stop=(kc == 15)), r=[wkv, Bbf], w=[pb_], inc=(kc == 15))
                    k.do("act", lambda e: e.copy(out=kbias[:msz_, mi_:mi_ + 1], in_=pb_[:msz_, 0:1]), r=[pb_], w=[kbias])
                for kc in range(16):
                    k.do("dve" if kc % 2 else "pool",
                         lambda e: e.tensor_scalar(out=wkv[:, kc, :], in0=wkv[:, kc, :], scalar1=A1[:, kc, 0:1], scalar2=1.0,
                                                   op0=ALU.mult, op1=ALU.mult), r=[wkv, A1], w=[wkv])
                hTk = k.ring(kps, "hTk", [128, 16, 512], BF16, 2)
                rawk_r = k.ring(kps, "rawk", [128, 4, 512], F32, 2)
                ckb_r = k.ring(kps, "ckb", [128, 4, 512], BF16, 2)
                rres = make_rms(kps, 512)
                cos_r = k.ring(kps, "cosb", [64, 512], F32, 2)
                sin_r = k.ring(kps, "sinb", [64, 512], F32, 2)
                kt1_r = k.ring(kps, "kt1b", [64, 512], F32, 2)
                kt2_r = k.ring(kps, "kt2b", [64, 512], F32, 2)
                krst_r = k.ring(kps, "krst", [64, 512], BF16, 2)
                KT_r = k.ring(kps, "KTst", [128, 8, 512], BF16, 2)
                V_r = k.ring(kps, "Vst", [128, 4, 1024], BF16, 2)
                kst = {"hT": hTk.next(), "pre": False}

                def kp_fronts(b):
                    hT = kst["hT"]
                    for i in range(1 if kst["pre"] else 0, 4):
                        r0 = (4 * b + i) * 128
                        front(fr, xall[r0:r0 + 128, :], 128, A1, 0, None, hT, i * 128)
                    if b < 15:
                        hTn = hTk.next()
                        r0 = (4 * (b + 1)) * 128
                        front(fr, xall[r0:r0 + 128, :], 128, A1, 0, None, hTn, 0)
                        kst["hT"] = hTn
                        kst["pre"] = True
                    else:
                        front_flush(fr)
                    return hT

                def kp_gemm(b, hT):
                    rawk = rawk_r.next()

                    def cons_rawk(mi, bi, c0, n, p, rawk=rawk):
                        k.do("act", lambda e: e.activation(out=rawk[:, mi, :], in_=p[:, :], func=AF.Identity, bias=kbias[:, mi:mi + 1]),
                             r=[p, kbias], w=[rawk])
                    gemm_fm(wkv, M4, 16, hT, [(0, 512)], pk, cons_rawk)
                    cb = cos_r.next(); sb_ = sin_r.next()
                    k.dma("sp", cb[:, :], cosk[:, b * 512:(b + 1) * 512], w=[cb])
                    k.dma("sp", sb_[:, :], sink[:, b * 512:(b + 1) * 512], w=[sb_])
                    t1 = kt1_r.next(); t2 = kt2_r.next(); krst = krst_r.next()

                    def cons_krk(mi, bi, c0, n, p, t1=t1, t2=t2, krst=krst, cb=cb, sb_=sb_):
                        if mi == 0:
                            k.do("dve", lambda e: e.scalar_tensor_tensor(out=t1[:, :], in0=p[:64, :], scalar=kbias[:64, 4:5], in1=cb[:, :],
                                                                         op0=ALU.add, op1=ALU.mult), r=[p, cb, kbias], w=[t1])
                        else:
                            k.do("dve", lambda e: e.scalar_tensor_tensor(out=t2[:, :], in0=p[:64, :], scalar=kbias[:64, 5:6], in1=sb_[:, :],
                                                                         op0=ALU.add, op1=ALU.mult), r=[p, sb_, kbias], w=[t2])
                            k.do("pool", lambda e: e.tensor_tensor(out=krst[:, :], in0=t1[:, :], in1=t2[:, :], op=ALU.add), r=[t1, t2], w=[krst])
                    gemm_fm(wkv, [(512, 64), (576, 64)], 16, hT, [(0, 512)], pk, cons_krk)
                    k.dma("sp", krT_d.t[:, b * 512:(b + 1) * 512], krst[:, :], r=[krst], scratch=krT_d)
                    return rawk

                def kp_rms(b, rawk):
                    ckb = ckb_r.next()
                    rms_fm(rres, rawk, gkv, [(0, 512)], pk, out_bf=ckb)
                    return ckb

                def kp_kv(b, ckb):
                    KTst = KT_r.next(); Vst = V_r.next()
                    kv_gen(ckb, [(0, 512)], [(t * 128, 128) for t in range(4)], KTst, Vst)
                    k.dma("sp", KT_d.t[:, b].rearrange("h p n -> p h n"), KTst[:, :, :], r=[KTst], scratch=KT_d)
                    k.dma("sp", V_d.t[:, b].rearrange("h p (t d) -> p t h d", d=128),
                          Vst[:, :, :].rearrange("p t (h d) -> p t h d", d=128), r=[Vst], scratch=V_d)

                prevraw = None
                for b in range(16):
                    hT = kp_fronts(b)
                    ckb_prev = kp_rms(b - 1, prevraw) if prevraw is not None else None
                    prevraw_new = kp_gemm(b, hT)
                    if ckb_prev is not None:
                        kp_kv(b - 1, ckb_prev)
                    prevraw = prevraw_new
                ckb_last = kp_rms(15, prevraw)
                kp_kv(15, ckb_last)
                k.barrier()
            k.mark("kp_done")

        if stop_after == "kp":
            k.barrier()
            k.finish()
            nc._in_names = in_names
            return nc

        h2T = k.sb(es, "h2T", [128, 16, NO], BF16)
        mes = es.enter_context(ExitStack())
        mT = k.sb(mes, "mT", [128, 16, NO], BF16)
        oes = es.enter_context(ExitStack())
        oT = k.sb(oes, "oT", [128, 8, NO], BF16)
        with ExitStack() as at:
            cqT = k.sb(at, "cqTa", [128, 4, NO], BF16)
            k.dma("sp", cqT[:, :, :], cq_d.t[:, :, :], r=[cq_d], w=[cqT])
            krTa = k.sb(at, "krTa", [64, 8192], BF16)
            k.dma("sp", krTa[:, :], krT_d.t[:, :], r=[krT_d], w=[krTa])
            krcT = k.sb(at, "krcTa", [64, 4, 1040], BF16)
            k.dma("sp", krcT[:, :, :], krc_d.t[:, :, :], r=[krc_d], w=[krcT])
            dm = ld(at, "dm", [128, 4], dmask[:, :])
            cq_t = ld(at, "cosqa", [64, NT], cosq[:, :])
            sq_t = ld(at, "sinqa", [64, NT], sinq[:, :])
            wuq = k.sb(at, "wuq", [128, 4, 1536], BF16)
            k.dma("pool", wuq[:, :, :], w_uq.rearrange("(kc p) n -> p kc n", p=128), w=[wuq])
            wuqs = k.sb(at, "wuqs", [128, 4, 8, 64], BF16)
            for kc in range(4):
                srcv = w_uq[kc * 128:(kc + 1) * 128, :].rearrange("p (h c) -> p h c", c=192)
                k.dma("pool", wuqs[:, kc, :, 0:32], srcv[:, :, 160:192], w=[wuqs])
                k.dma("pool", wuqs[:, kc, :, 32:64], srcv[:, :, 128:160], w=[wuqs])
            qn_r = k.ring(at, "qn", [128, NO], BF16, 2)
            qr_r = k.ring(at, "qr", [64, NO], BF16, 2)
            qt1_r = k.ring(at, "qt1", [64, 512], F32, 2)
            qt2_r = k.ring(at, "qt2", [64, 512], F32, 2)
            kb_r = k.ring(at, "kb", [128, 512], BF16, 6)
            vb_r = k.ring(at, "vb", [128, 4, 128], BF16, 6)
            PT_r = k.ring(at, "PT", [128, 512], BF16, 4)
            rd_r = k.ring(at, "rd", [128, 512], F32, 2)
            dacc_r = [k.ring(at, "dacc0", [128, 512], F32, 2), k.ring(at, "dacc1", [128, 512], F32, 2)]
            onesf = k.sb(at, "onesf", [128, 128], F32)
            k.do("dve", lambda e: e.memset(onesf[:, :], 1.0), w=[onesf])
            kts_r = k.ring(at, "kts", [128, 1040], BF16, 3)
            vs_r = k.ring(at, "vss", [128, 9, 128], BF16, 3)
            ps_s = k.ring(at, "ps_s", [128, 512], F32, 3, psum=True)
            ps_o = k.ring(at, "ps_o", [128, 512], F32, 2, psum=True)
            ps_d = k.ring(at, "ps_d", [128, 512], F32, 2, psum=True)
            ps_q = k.ring(at, "ps_q", [128, 512], F32, 1, psum=True)

            def emit_qproj(h):
                qn = qn_r.next(); qr = qr_r.next()
                for (c0, n) in BLK_O:
                    p = ps_q.next()
                    for kc in range(4):
                        k.do("pe", lambda e: e.matmul(out=p[:, :n], lhsT=wuq[:, kc, h * 192:h * 192 + 128], rhs=cqT[:, kc, c0:c0 + n],
                                                      start=(kc == 0), stop=(kc == 3)), r=[wuq, cqT], w=[p], inc=(kc == 3))
                    k.do("act", lambda e: e.copy(out=qn[:, c0:c0 + n], in_=p[:, :n]), r=[p], w=[qn])
                    p = ps_q.next()
                    for kc in range(4):
                        k.do("pe", lambda e: e.matmul(out=p[:64, :n], lhsT=wuq[:, kc, h * 192 + 128:h * 192 + 192], rhs=cqT[:, kc, c0:c0 + n],
                                                      start=(kc == 0), stop=(kc == 3)), r=[wuq, cqT], w=[p], inc=(kc == 3))
                    t1 = qt1_r.next()
                    k.do("dve", lambda e: e.tensor_tensor(out=t1[:, :n], in0=p[:64, :n], in1=cq_t[:, c0:c0 + n], op=ALU.mult), r=[p, cq_t], w=[t1])
                    p = ps_q.next()
                    for kc in range(4):
                        k.do("pe", lambda e: e.matmul(out=p[:64, :n], lhsT=wuqs[:, kc, h, :], rhs=cqT[:, kc, c0:c0 + n],
                                                      start=(kc == 0), stop=(kc == 3)), r=[wuqs, cqT], w=[p], inc=(kc == 3))
                    t2 = qt2_r.next()
                    k.do("dve", lambda e: e.tensor_tensor(out=t2[:, :n], in0=p[:64, :n], in1=sq_t[:, c0:c0 + n], op=ALU.mult), r=[p, sq_t], w=[t2])
                    k.do("pool", lambda e: e.tensor_tensor(out=qr[:, c0:c0 + n], in0=t1[:, :n], in1=t2[:, :n], op=ALU.add), r=[t1, t2], w=[qr])
                return qn, qr

            qnext = emit_qproj(0)
            for h in range(8):
                qn, qr = qnext
                for g in range(2):
                    if g == 1 and h < 7:
                        qnext = emit_qproj(h + 1)
                    po = ps_o.next(); pd = ps_d.next()
                    dacc = [dacc_r[0].next(), dacc_r[1].next()]
                    k.do("dve", lambda e: e.memset(dacc[0][:, :], 0.0), w=[dacc[0]])
                    k.do("pool", lambda e: e.memset(dacc[1][:, :], 0.0), w=[dacc[1]])
                    ntile = 0
                    nblk = 8 * g + 8
                    c1 = 512 * (g + 1)
                    items = []
                    for b in range(nblk):
                        for kt in range(4):
                            items.append((b, kt))
                    blk = {}

                    def emit_S(b, kt):
                        if kt == 0:
                            Kb = kb_r.next(); Vb = vb_r.next()
                            k.dma("sp", Kb[:, :], KT_d.t[h, b], r=[KT_d], w=[Kb])
                            k.dma("sp", Vb[:, :, :], V_d.t[h, b].rearrange("p (t d) -> p t d", d=128), r=[V_d], w=[Vb])
                            blk[b] = (Kb, Vb)
                        Kb, Vb = blk[b]
                        jlo = max(b, 8 * g)
                        c0 = 64 * jlo
                        N = c1 - c0
                        diag = (b >= 8 * g)
                        ps = ps_s.next()
                        k.do("pe", lambda e: e.matmul(out=ps[:, :N], lhsT=Kb[:, kt * 128:(kt + 1) * 128], rhs=qn[:, c0:c1], start=True, stop=False),
                             r=[Kb, qn], w=[ps], inc=False)
                        k0 = b * 512 + kt * 128
                        k.do("pe", lambda e: e.matmul(out=ps[:, :N], lhsT=krTa[:, k0:k0 + 128], rhs=qr[:, c0:c1], start=False, stop=True),
                             r=[krTa, qr], w=[ps])
                        PT = PT_r.next()
                        if diag:
                            k.do("act", lambda e: e.activation(out=PT[:, 0:64], in_=ps[:, 0:64], func=AF.Exp, scale=SCALE, bias=dm[:, kt:kt + 1]),
                                 r=[ps, dm], w=[PT])
                            if N > 64:
                                k.do("act", lambda e: e.activation(out=PT[:, 64:N], in_=ps[:, 64:N], func=AF.Exp, scale=SCALE), r=[ps], w=[PT])
                        else:
                            k.do("act", lambda e: e.activation(out=PT[:, :N], in_=ps[:, :N], func=AF.Exp, scale=SCALE), r=[ps], w=[PT])
                        return (b, kt, Vb, PT, c0 - 512 * g, N)

                    def emit_PV(it, idx):
                        b, kt, Vb, PT, lc0, N = it
                        first = (idx == 0)
                        last = (idx == len(items) - 1)
                        k.do("pe", lambda e: e.matmul(out=po[:, lc0:lc0 + N], lhsT=Vb[:, kt, :], rhs=PT[:, :N], start=first, stop=last),
                             r=[Vb, PT], w=[po])
                        da = dacc[idx % 2]
                        k.do("dve" if idx % 2 == 0 else "pool",
                             lambda e: e.tensor_tensor(out=da[:, lc0:lc0 + N], in0=da[:, lc0:lc0 + N], in1=PT[:, :N], op=ALU.add),
                             r=[da, PT], w=[da])

                    pend = None
                    for idx, (b, kt) in enumerate(items):
                        it = emit_S(b, kt)
                        if pend is not None:
                            emit_PV(pend, idx - 1)
                        pend = it
                    emit_PV(pend, len(items) - 1)
                    for i_ in range(2):
                        k.do("pe", lambda e: e.matmul(out=pd[:, :], lhsT=onesf[:, :], rhs=dacc[i_][:, :], start=(i_ == 0), stop=(i_ == 1)),
                             r=[onesf, dacc[i_]], w=[pd], inc=(i_ == 1))
                    rd = rd_r.next()
                    k.do("dve", lambda e: e.reciprocal(out=rd[:, :], in_=pd[:, :]), r=[pd], w=[rd])
                    k.do("dve", lambda e: e.tensor_tensor(out=oT[:, h, 512 * g:512 * (g + 1)], in0=po[:, :], in1=rd[:, :], op=ALU.mult),
                         r=[po, rd], w=[oT])
                po = ps_o.next(); pd = ps_d.next()
                for bb in range(4):
                    Ks = kts_r.next(); Vs = vs_r.next()
                    k.dma("sp", Ks[:, :], KTs_d.t[bb, h], r=[KTs_d], w=[Ks])
                    k.dma("sp", Vs[:, :, :], Vs_d.t[bb, h].rearrange("p (t d) -> p t d", d=128), r=[Vs_d], w=[Vs])
                    q0 = 1024 + 16 * bb
                    for t in range(9):
                        nk = 128 if t < 8 else 16
                        ps = ps_s.next()
                        k.do("pe", lambda e: e.matmul(out=ps[:nk, :16], lhsT=Ks[:, t * 128:t * 128 + nk], rhs=qn[:, q0:q0 + 16], start=True, stop=False),
                             r=[Ks, qn], w=[ps], inc=False)
                        k.do("pe", lambda e: e.matmul(out=ps[:nk, :16], lhsT=krcT[:, bb, t * 128:t * 128 + nk], rhs=qr[:, q0:q0 + 16], start=False, stop=True),
                             r=[krcT, qr], w=[ps])
                        PT = PT_r.next()
                        k.do("act", lambda e: e.activation(out=PT[:nk, :16], in_=ps[:nk, :16], func=AF.Exp, scale=SCALE), r=[ps], w=[PT])
                        k.do("pe", lambda e: e.matmul(out=po[:, 16 * bb:16 * bb + 16], lhsT=Vs[:nk, t, :], rhs=PT[:nk, :16], start=(t == 0), stop=(t == 8)),
                             r=[Vs, PT], w=[po], inc=False)
                        k.do("pe", lambda e: e.matmul(out=pd[:, 16 * bb:16 * bb + 16], lhsT=ones[:nk, :], rhs=PT[:nk, :16], start=(t == 0), stop=(t == 8)),
                             r=[ones, PT], w=[pd])
                rd = rd_r.next()
                k.do("dve", lambda e: e.reciprocal(out=rd[:, 0:64], in_=pd[:, 0:64]), r=[pd], w=[rd])
                k.do("dve", lambda e: e.tensor_tensor(out=oT[:, h, 1024:1088], in0=po[:, 0:64], in1=rd[:, 0:64], op=ALU.mult),
                     r=[po, rd], w=[oT])
            k.barrier()
        k.mark("att_done")

        if stop_after == "att":
            k.barrier()
            k.finish()
            nc._in_names = in_names
            return nc

        with ExitStack() as ma:
            woa_r = k.ring(ma, "woa", [128, 8, 512], BF16, 2)
            g0_r = k.ring(ma, "g0t", [128, NO], BF16, 3)
            gb_r = k.ring(ma, "gbt", [128, NO], BF16, 3)
            gq_ = []

            def g_issue(dc):
                a = g0_r.next(); b_ = gb_r.next()
                k.dma("sp", a[:, :], g0_d.t[dc, :, :], r=[g0_d], w=[a])
                k.dma("sp", b_[:, :], gb_d.t[dc, :, :], r=[gb_d], w=[b_])
                gq_.append((a, b_))
            g_issue(0)
            mtmp_r = k.ring(ma, "mtmp", [128, 512], F32, 2)
            pm = k.ring(ma, "pm", [128, 512], F32, 4, psum=True)
            cur = {}
            for pi in range(4):
                wp = woa_r.next()
                k.dma("pool", wp[:, :, :], w_oa[:, pi * 512:(pi + 1) * 512].rearrange("(kc p) n -> p kc n", p=128), w=[wp])

                def cons_m(mi, bi, c0, n, p, pi=pi):
                    dc = 4 * pi + mi
                    if bi == 0:
                        cur["g0"], cur["gb"] = gq_.pop(0)
                        if dc < 15:
                            g_issue(dc + 1)
                    g0t = cur["g0"]; gbt = cur["gb"]
                    tm = mtmp_r.next()
                    k.do("dve", lambda e: e.tensor_tensor(out=tm[:, :n], in0=p[:, :n], in1=g0t[:, c0:c0 + n], op=ALU.mult), r=[p, g0t], w=[tm])
                    k.do("pool", lambda e: e.tensor_tensor(out=mT[:, dc, c0:c0 + n], in0=tm[:, :n], in1=gbt[:, c0:c0 + n], op=ALU.add),
                         r=[tm, gbt], w=[mT])
                gemm_fm(wp, M4, 8, oT, BLK_O, pm, cons_m)
            k.barrier()
        oes.close()
        k.mark("mrga_done")
        G_S = [(0, 16, 1), (16, 16, 2), (32, 16, 3), (48, 16, 4)]
        with ExitStack() as mb_:
            wos = [k.sb(mb_, "wo%d" % i, [128, 16, 512], BF16) for i in range(4)]
            for pi in range(4):
                k.dma("pool", wos[pi][:, :, :], w_o[:, pi * 512:(pi + 1) * 512].rearrange("(kc p) n -> p kc n", p=128), w=[wos[pi]])
            fr = make_front(mb_, nx=2)
            GT_r = k.ring(mb_, "GT", [128, D], F32, 1)
            x1_r = k.ring(mb_, "x1", [128, D], F32, 2)
            pm = k.ring(mb_, "pm2", [128, 512], F32, 4, psum=True)
            for (t0, nt) in TILES_O:
                xt = fr["x"].next()
                k.dma("sp", xt[:nt, :], xown[t0:t0 + nt, :], w=[xt])
                GT = GT_r.next()
                if t0 < 1024:
                    k.dma("sp", GT[:nt, :], bc(modrows.t[0:1, 0:D], [nt, D]), r=[modrows], w=[GT])
                else:
                    for bb in range(4):
                        k.dma("sp", GT[16 * bb:16 * bb + 16, :], bc(modrows.t[1 + bb:2 + bb, 0:D], [16, D]), r=[modrows], w=[GT])
                x1 = x1_r.next()
                for dq in range(4):
                    p = pm.next()
                    for kc in range(16):
                        k.do("pe", lambda e: e.matmul(out=p[:nt, :], lhsT=mT[:, kc, t0:t0 + nt], rhs=wos[dq][:, kc, :],
                                                      start=(kc == 0), stop=(kc == 15)), r=[mT, wos[dq]], w=[p], inc=(kc == 15))
                    k.do("dve", lambda e: e.tensor_tensor(out=x1[:nt, dq * 512:(dq + 1) * 512], in0=p[:nt, :], in1=GT[:nt, dq * 512:(dq + 1) * 512],
                                                          op=ALU.mult), r=[p, GT], w=[x1])
                k.do("pool", lambda e: e.tensor_tensor(out=x1[:nt, :], in0=x1[:nt, :], in1=xt[:nt, :], op=ALU.add), r=[x1, xt], w=[x1])
                k.dma("sp", x1_d.t[t0:t0 + nt, :], x1[:nt, :], r=[x1], scratch=x1_d)
                front_from_sb(fr, x1, nt, A2, modT, 32, G_P if t0 < 1024 else G_S, h2T, t0)
            front_flush(fr)
            k.barrier()
        mes.close()
        k.mark("mrg_done")

        if stop_after == "mrg":
            with ExitStack() as fz:
                final_phase(fz, None)
            k.barrier()
            k.finish()
            nc._in_names = in_names
            return nc

        with ExitStack() as pes:
            aT = k.sb(pes, "aT", [128, NO], F32)
            bT = k.sb(pes, "bT", [128, NO], F32)
            gT = k.sb(pes, "gT", [128, NO], F32)
            io_i = k.sb(pes, "io_i", [128, 128], I32)
            io128 = k.sb(pes, "io128", [128, 128], F32)
            k.do("pool", lambda e: e.iota(io_i[:, :], pattern=[[1, 128]], base=0, channel_multiplier=0), w=[io_i])
            k.do("dve", lambda e: e.tensor_copy(out=io128[:, :], in_=io_i[:, :]), r=[io_i], w=[io128])
            with ExitStack() as pq:
                qpT = k.sb(pq, "qpT", [128, 16, NO], BF16)
                wq_r = k.ring(pq, "wpqpan", [128, 16, 512], BF16, 2)
                psq = k.ring(pq, "psq", [128, 512], F32, 4, psum=True)
                ptq = k.ring(pq, "ptq", [128, 4, 128], BF16, 2, psum=True)
                for pi in range(4):
                    wp = wq_r.next()
                    k.dma("pool", wp[:, :, :], w_pq[:, pi * 512:(pi + 1) * 512].rearrange("(kc p) n -> p kc n", p=128), w=[wp])

                    def cons_q(mi, bi, c0, n, p, pi=pi):
                        k.do("act", lambda e: e.copy(out=qpT[:, 4 * pi + mi, c0:c0 + n], in_=p[:, :n]), r=[p], w=[qpT])
                    gemm_fm(wp, M4, 16, h2T, BLK_O, psq, cons_q)
                subkT = k.sb(pq, "subkT", [128, 16, 128], BF16)
                skr = k.ring(pq, "skr", [128, 8, 128], BF16, 2)
                for which, sk in enumerate((sub_k1, sub_k2)):
                    s_ = skr.next()
                    k.dma("pool", s_[:, :, :], sk.rearrange("(h n) d -> n h d", n=128), w=[s_])
                    for half in range(2):
                        p = ptq.next()
                        for j in range(4):
                            h = half * 4 + j
                            k.do("pe", lambda e: e.transpose(out=p[:, j, :], in_=s_[:, h, :], identity=ident[:, :]), r=[s_, ident], w=[p], inc=(j == 3))
                        for j in range(4):
                            h = half * 4 + j
                            k.do("act", lambda e: e.copy(out=subkT[:, 2 * h + which, :], in_=p[:, j, :]), r=[p], w=[subkT])
                sc_r = k.ring(pq, "sc", [128, 16, 128], F32, 2)
                sc2a = [k.sb(pq, "sc2a%d" % i, [128, 128], F32) for i in range(16)]
                v16a = [k.sb(pq, "v16a%d" % i, [128, 8], F32) for i in range(16)]
                v16b = [k.sb(pq, "v16b%d" % i, [128, 8], F32) for i in range(16)]
                ixa = [k.sb(pq, "ixa%d" % i, [128, 8], U32) for i in range(16)]
                ixb = [k.sb(pq, "ixb%d" % i, [128, 8], U32) for i in range(16)]
                cand2a = [k.sb(pq, "cand2a%d" % i, [128, 256], F32) for i in range(8)]
                sva = [k.sb(pq, "sva%d" % i, [128, 8], F32) for i in range(8)]
                svb = [k.sb(pq, "svb%d" % i, [128, 8], F32) for i in range(8)]
                cia = [k.sb(pq, "cia%d" % i, [128, 8], U32) for i in range(8)]
                cib = [k.sb(pq, "cib%d" % i, [128, 8], U32) for i in range(8)]
                v16s = [k.sb(pq, "v16_%d" % i, [128, 16, 16], F32) for i in range(2)]
                ixs = [k.sb(pq, "ix_%d" % i, [128, 16, 16], U32) for i in range(2)]
                ixf = k.sb(pq, "ixf", [128, 16, 16], F32)
                cand = k.sb(pq, "cand", [128, 8, 256], F32)
                sv = k.sb(pq, "sv", [128, 8, 16], F32)
                ci = k.sb(pq, "ci", [128, 8, 16], U32)
                sl_i = k.sb(pq, "sl_i", [128, 2, 128], U32)
                sl_f = k.sb(pq, "sl_f", [128, 2, 128], F32)
                eqs = [k.sb(pq, "eq%d" % i, [128, 8, 16, 16], F32) for i in range(2)]
                sel = k.sb(pq, "sel", [128, 3, 128], F32)
                ex = k.sb(pq, "ex", [128, 128], F32)
                zz = k.sb(pq, "zz", [128, 8], F32)
                rz = k.sb(pq, "rz", [128, 8], F32)
                pst = k.ring(pq, "pst", [128, 512], F32, 1, psum=True)
                def tk_s1(t0, nt, v16, ix):
                    sc = sc_r.next()
                    for q4 in range(4):
                        p = psq.next()
                        for j in range(4):
                            gi_ = q4 * 4 + j
                            k.do("pe", lambda e: e.matmul(out=p[:nt, j * 128:(j + 1) * 128], lhsT=qpT[:, gi_, t0:t0 + nt], rhs=subkT[:, gi_, :],
                                                          start=True, stop=True), r=[qpT, subkT], w=[p], inc=(j == 3))
                        k.do("act", lambda e: e.copy(out=sc[:nt, q4 * 4:(q4 + 1) * 4, :], in_=p[:nt, :].rearrange("p (a b) -> p a b", b=128)),
                             r=[p], w=[sc])
                    for gi_ in range(16):
                        k.do("dve", lambda e: e.max(out=v16[:nt, gi_, 0:8], in_=sc[:nt, gi_, :]), r=[sc], w=[v16], nowaw=True)
                    for gi_ in range(16):
                        k.do("dve", lambda e: e.max_index(out=ix[:nt, gi_, 0:8], in_max=v16[:nt, gi_, 0:8], in_values=sc[:nt, gi_, :]),
                             r=[sc, v16], w=[ix], nowaw=True)
                    for gi_ in range(16):
                        k.do("dve", lambda e: e.match_replace(out=sc2a[gi_][:nt, :], in_to_replace=v16[:nt, gi_, 0:8], in_values=sc[:nt, gi_, :],
                                                              imm_value=-1e30), r=[sc, v16], w=[sc2a[gi_]])
                    for gi_ in range(16):
                        k.do("dve", lambda e: e.max(out=v16[:nt, gi_, 8:16], in_=sc2a[gi_][:nt, :]), r=[sc2a[gi_]], w=[v16], nowaw=True)
                    for gi_ in range(16):
                        k.do("dve", lambda e: e.max_index(out=ix[:nt, gi_, 8:16], in_max=v16[:nt, gi_, 8:16], in_values=sc2a[gi_][:nt, :]),
                             r=[sc2a[gi_], v16], w=[ix], nowaw=True)
                def tk_tail(t0, nt, v16, ix):
                    k.do("pool", lambda e: e.tensor_copy(out=ixf[:nt, :, :], in_=ix[:nt, :, :]), r=[ix], w=[ixf])
                    v4 = v16[:nt, :, :].rearrange("p (h w) a -> p h w a", w=2)
                    i4 = ixf[:nt, :, :].rearrange("p (h w) a -> p h w a", w=2)
                    S4 = [nt, 8, 16, 16]
                    k.do("pool", lambda e: e.tensor_tensor(out=cand[:nt, :, :].rearrange("p h (a b) -> p h a b", b=16),
                                                          in0=bc(v4[:, :, 0, :].unsqueeze(3), S4), in1=bc(v4[:, :, 1, :].unsqueeze(2), S4), op=ALU.add),
                         r=[v16], w=[cand])
                    for h in range(8):
                        k.do("dve", lambda e: e.max(out=sv[:nt, h, 0:8], in_=cand[:nt, h, :]), r=[cand], w=[sv], nowaw=True)
                    for h in range(8):
                        k.do("dve", lambda e: e.max_index(out=ci[:nt, h, 0:8], in_max=sv[:nt, h, 0:8], in_values=cand[:nt, h, :]), r=[cand, sv], w=[ci], nowaw=True)
                    for h in range(8):
                        k.do("dve", lambda e: e.match_replace(out=cand2a[h][:nt, :], in_to_replace=sv[:nt, h, 0:8], in_values=cand[:nt, h, :],
                                                              imm_value=-1e30), r=[cand, sv], w=[cand2a[h]])
                    for h in range(8):
                        k.do("dve", lambda e: e.max(out=sv[:nt, h, 8:16], in_=cand2a[h][:nt, :]), r=[cand2a[h]], w=[sv], nowaw=True)
                    for h in range(8):
                        k.do("dve", lambda e: e.max_index(out=ci[:nt, h, 8:16], in_max=sv[:nt, h, 8:16], in_values=cand2a[h][:nt, :]),
                             r=[cand2a[h], sv], w=[ci], nowaw=True)
                    civ = ci[:nt, :, :].rearrange("p h k -> p (h k)")
                    k.do("dve", lambda e: e.tensor_single_scalar(out=sl_i[:nt, 0, :], in_=civ, scalar=4, op=ALU.logical_shift_right), r=[ci], w=[sl_i])
                    k.do("dve", lambda e: e.tensor_single_scalar(out=sl_i[:nt, 1, :], in_=civ, scalar=15, op=ALU.bitwise_and), r=[ci], w=[sl_i])
                    k.do("dve", lambda e: e.tensor_copy(out=sl_f[:nt, :, :], in_=sl_i[:nt, :, :]), r=[sl_i], w=[sl_f])
                    for w_ in range(2):
                        eq = eqs[w_]
                        slv = sl_f[:nt, w_, :].rearrange("p (h k) -> p h k", k=16)
                        k.do("dve", lambda e: e.tensor_tensor(out=eq[:nt], in0=bc(slv.unsqueeze(3), S4),
                                                              in1=bc(io128[:nt, 0:16].unsqueeze(1).unsqueeze(1), S4), op=ALU.is_equal),
                             r=[sl_f, io128], w=[eq])
                        k.do("pool", lambda e: e.tensor_tensor(out=eq[:nt], in0=eq[:nt], in1=bc(i4[:, :, w_, :].unsqueeze(2), S4), op=ALU.mult),
                             r=[eq, ixf], w=[eq])
                        k.do("dve", lambda e: e.tensor_reduce(out=sel[:nt, w_, :].rearrange("p (h k) -> p h k", k=16), in_=eq[:nt],
                                                              axis=AX.X, op=ALU.add), r=[eq], w=[sel])
                    k.do("dve", lambda e: e.tensor_tensor(out=ex[:nt, :].rearrange("p (h k) -> p h k", k=16), in0=sv[:nt, :, :],
                                                          in1=bc(sv[:nt, :, 0:1], [nt, 8, 16]), op=ALU.subtract), r=[sv], w=[ex])
                    k.do("act", lambda e: e.activation(out=ex[:nt, :], in_=ex[:nt, :], func=AF.Exp), r=[ex], w=[ex])
                    k.do("dve", lambda e: e.tensor_reduce(out=zz[:nt, :], in_=ex[:nt, :].rearrange("p (h k) -> p h k", k=16), axis=AX.X, op=ALU.add),
                         r=[ex], w=[zz])
                    k.do("dve", lambda e: e.reciprocal(out=rz[:nt, :], in_=zz[:nt, :]), r=[zz], w=[rz])
                    k.do("dve", lambda e: e.tensor_tensor(out=sel[:nt, 2, :].rearrange("p (h k) -> p h k", k=16),
                                                          in0=ex[:nt, :].rearrange("p (h k) -> p h k", k=16),
                                                          in1=bc(rz[:nt, :].unsqueeze(2), [nt, 8, 16]), op=ALU.mult), r=[ex, rz], w=[sel])
                    p = pst.next()
                    for w_ in range(3):
                        k.do("pe", lambda e: e.transpose(out=p[:, w_ * 128:w_ * 128 + nt], in_=sel[:nt, w_, :], identity=identf[:nt, :nt]),
                             r=[sel, identf], w=[p], inc=(w_ == 2))
                    for w_, dst in enumerate((aT, bT, gT)):
                        k.do("act", lambda e: e.copy(out=dst[:, t0:t0 + nt], in_=p[:, w_ * 128:w_ * 128 + nt]), r=[p], w=[dst])

                prev_t = None
                for ti_, (t0, nt) in enumerate(TILES_O):
                    vb = (v16s[ti_ % 2], ixs[ti_ % 2])
                    tk_s1(t0, nt, *vb)
                    if prev_t is not None:
                        tk_tail(*prev_t)
                    prev_t = (t0, nt) + vb
                tk_tail(*prev_t)
                k.barrier()
            k.mark("peer_topk_done")
            with ExitStack() as pg_:
                Gst_r = k.ring(pg_, "Gst", [128, 128, 128], BF16, 2)
                P1_r = k.ring(pg_, "P1h", [128, 16, 128], BF16, 2)
                Qe_r = k.ring(pg_, "Qe", [128, 16, 128], BF16, 2)
                Q2_r = k.ring(pg_, "Q2g", [128, 16, 128], BF16, 2)
                psG = k.ring(pg_, "psG", [128, 4, 128], F32, 4, psum=True)
                S3 = [128, 16, 128]
                io_bf = k.sb(pg_, "io_bf", [128, 128], BF16)
                a_bf = k.sb(pg_, "a_bf", [128, NO], BF16)
                b_bf = k.sb(pg_, "b_bf", [128, NO], BF16)
                g_bf = k.sb(pg_, "g_bf", [128, NO], BF16)
                k.do("dve", lambda e: e.tensor_copy(out=io_bf[:, :], in_=io128[:, :]), r=[io128], w=[io_bf])
                k.do("dve", lambda e: e.tensor_copy(out=a_bf[:, :], in_=aT[:, :]), r=[aT], w=[a_bf])
                k.do("dve", lambda e: e.tensor_copy(out=b_bf[:, :], in_=bT[:, :]), r=[bT], w=[b_bf])
                k.do("dve", lambda e: e.tensor_copy(out=g_bf[:, :], in_=gT[:, :]), r=[gT], w=[g_bf])
                for (t0, nt) in TILES_O:
                    Gs = Gst_r.next()
                    for t16 in range(nt // 16):
                        tb = t0 + 16 * t16
                        P1 = P1_r.next(); Qe = Qe_r.next(); Q2 = Q2_r.next()
                        k.do("dve", lambda e: e.tensor_tensor(out=P1[:, :, :], in0=bc(io_bf[:, :].unsqueeze(1), S3),
                                                              in1=bc(a_bf[:, tb:tb + 16].unsqueeze(2), S3), op=ALU.is_equal), r=[io_bf, a_bf], w=[P1])
                        k.do("dve", lambda e: e.tensor_tensor(out=Qe[:, :, :], in0=bc(io_bf[:, :].unsqueeze(1), S3),
                                                              in1=bc(b_bf[:, tb:tb + 16].unsqueeze(2), S3), op=ALU.is_equal), r=[io_bf, b_bf], w=[Qe])
                        k.do("pool", lambda e: e.tensor_tensor(out=Q2[:, :, :], in0=Qe[:, :, :],
                                                               in1=bc(g_bf[:, tb:tb + 16].unsqueeze(2), S3), op=ALU.mult), r=[Qe, g_bf], w=[Q2])
                        for j4 in range(4):
                            p = psG.next()
                            for j in range(4):
                                jj = j4 * 4 + j
                                k.do("pe", lambda e: e.matmul(out=p[:, j, :], lhsT=Q2[:, jj, :], rhs=P1[:, jj, :], start=True, stop=True),
                                     r=[Q2, P1], w=[p], inc=(j == 3))
                            tl = 16 * t16 + 4 * j4
                            k.do("act", lambda e: e.copy(out=Gs[:, :, tl:tl + 4], in_=p[:, :, :].rearrange("p t i -> p i t")), r=[p], w=[Gs])
                    for q4 in range(4):
                        k.dma("sp", G_d.t[:, q4 * 32:(q4 + 1) * 32, t0:t0 + nt], Gs[:, q4 * 32:(q4 + 1) * 32, 0:nt], r=[Gs], scratch=G_d)
                k.barrier()
            k.mark("peer_G_done")
            acc = [k.sb(pes, "acc%d" % i, [128, D], F32) for i in range(9)]
            with ExitStack() as pm_:
                wur = k.ring(pm_, "wur", [128, D], BF16, 2)
                wuT_r = k.ring(pm_, "wuT", [128, 16, 128], BF16, 3)
                wvr = k.ring(pm_, "wvr", [128, D], BF16, 8)
                AT_r = k.ring(pm_, "AT", [128, 4, NO], BF16, 2)
                gtc = k.ring(pm_, "gtc", [128, NO], BF16, 3)
                gl_r = k.ring(pm_, "gl", [128, NO], BF16, 2)
                psT = k.ring(pm_, "psT", [128, 8, 128], BF16, 2, psum=True)
                psU = k.ring(pm_, "psU", [128, 512], F32, 3, psum=True)
                psD = k.ring(pm_, "psD", [128, 512], F32, 3, psum=True)
                nev = [0]

                def emit_U(gi):
                    AT = AT_r.next()
                    wvs = []
                    for ec in range(4):
                        i1 = 4 * gi + ec
                        raw = wur.next()
                        k.dma("pool", raw[:, :], w_u[i1 * 128:(i1 + 1) * 128, :], w=[raw])
                        gt = gtc.next()
                        k.dma("sp", gt[:, :], G_d.t[:, i1, :], r=[G_d], w=[gt])
                        wT = wuT_r.next()
                        for g4 in range(2):
                            p = psT.next()
                            for j in range(8):
                                dc = g4 * 8 + j
                                k.do("pe", lambda e: e.transpose(out=p[:, j, :], in_=raw[:, dc * 128:(dc + 1) * 128], identity=ident[:, :]),
                                     r=[raw, ident], w=[p], inc=(j == 7))
                            nev[0] += 1
                            if nev[0] % 2:
                                k.do("act", lambda e: e.copy(out=wT[:, g4 * 8:(g4 + 1) * 8, :], in_=p[:, :, :]), r=[p], w=[wT])
                            else:
                                k.do("dve", lambda e: e.tensor_copy(out=wT[:, g4 * 8:(g4 + 1) * 8, :], in_=p[:, :, :]), r=[p], w=[wT])
                        gl = gl_r.next()
                        for (c0, n) in BLK_O:
                            pu = psU.next()
                            for dc in range(16):
                                k.do("pe", lambda e: e.matmul(out=pu[:, :n], lhsT=wT[:, dc, :], rhs=h2T[:, dc, c0:c0 + n], start=(dc == 0), stop=(dc == 15)),
                                     r=[wT, h2T], w=[pu], inc=(dc == 15))
                            k.do("act", lambda e: e.activation(out=gl[:, c0:c0 + n], in_=pu[:, :n], func=AF.Gelu), r=[pu], w=[gl])
                        k.do("dve", lambda e: e.tensor_tensor(out=AT[:, ec, :], in0=gl[:, :], in1=gt[:, :], op=ALU.mult), r=[gl, gt], w=[AT])
                        wv = wvr.next()
                        k.dma("pool", wv[:, :], w_v[i1 * 128:(i1 + 1) * 128, :], w=[wv])
                        wvs.append(wv)
                    return AT, wvs

                def emit_down(gi, AT, wvs):
                    for ti, (t0, nt) in enumerate(TILES_O):
                        for dq in range(4):
                            pd = psD.next()
                            for ec in range(4):
                                k.do("pe", lambda e: e.matmul(out=pd[:nt, :], lhsT=AT[:, ec, t0:t0 + nt], rhs=wvs[ec][:, dq * 512:(dq + 1) * 512],
                                                              start=(ec == 0), stop=(ec == 3)), r=[AT, wvs[ec]], w=[pd], inc=(ec == 3))
                            a = acc[ti]
                            if gi == 0:
                                k.do("act", lambda e: e.copy(out=a[:nt, dq * 512:(dq + 1) * 512], in_=pd[:nt, :]), r=[pd], w=[a])
                            else:
                                k.do("dve", lambda e: e.tensor_tensor(out=a[:nt, dq * 512:(dq + 1) * 512], in0=pd[:nt, :],
                                                                      in1=a[:nt, dq * 512:(dq + 1) * 512], op=ALU.add), r=[pd, a], w=[a])

                prev = None
                for gi in range(32):
                    cur = emit_U(gi)
                    if prev is not None:
                        emit_down(gi - 1, *prev)
                    prev = cur
                emit_down(31, *prev)
                k.barrier()
            k.mark("peer_main_done")
            with ExitStack() as fz:
                final_phase(fz, acc)
            k.barrier()
        k.barrier()
        k.finish()
    nc._in_names = in_names
    nc._ninstr = k.ninstr
    return nc


def _rope_tables(pos):
    half = 32
    inv = 1.0 / (10000.0 ** (np.arange(half, dtype=np.float32) / half))
    ang = pos.astype(np.float32)[:, None] * inv[None, :].astype(np.float32)
    cos = np.cos(ang).astype(np.float32).T
    sin = np.sin(ang).astype(np.float32).T
    cosT = np.concatenate([cos, cos], axis=0)
    sinT = np.concatenate([-sin, sin], axis=0)
    return np.ascontiguousarray(cosT), np.ascontiguousarray(sinT)


def _fm(v, nchunk):
    return np.ascontiguousarray(np.asarray(v, np.float32).reshape(nchunk, 128).T)


_CACHE = {}


def prepare(inputs):
    f = lambda a: np.ascontiguousarray(np.asarray(a, dtype=np.float32))
    xp = f(inputs["x_prompt"])[0]
    xs = f(inputs["x_sample"])
    shared = {
        "xall": xp,
        "w_ada": f(inputs["w_ada"])[0],
        "b_adaT": _fm(f(inputs["b_ada"])[0], 96),
        "b_ada": f(inputs["b_ada"]).reshape(1, -1),
        "g_n1T": _fm(f(inputs["g_n1"])[0], 16),
        "g_n2T": _fm(f(inputs["g_n2"])[0], 16),
        "g_qT": _fm(f(inputs["g_q"])[0], 4),
        "g_kvT": _fm(f(inputs["g_kv"])[0], 4),
        "w_in": f(inputs["w_in"])[0],
        "w_uq": f(inputs["w_uq"])[0].reshape(512, 1536),
        "w_uk": f(inputs["w_uk"])[0].reshape(512, 1024),
        "w_uv": f(inputs["w_uv"])[0].reshape(512, 1024),
        "w_oa": f(inputs["w_oa"])[0],
        "w_ob": f(inputs["w_ob"])[0],
        "w_o": f(inputs["w_o"])[0],
        "w_pq": f(inputs["w_pq"])[0],
        "w_convT": np.ascontiguousarray(f(inputs["w_conv"])[0].reshape(3, 8, 128).transpose(2, 1, 0)),
        "b_convT": _fm(f(inputs["b_conv"])[0], 8),
        "sub_k1": f(inputs["sub_k1"])[0].reshape(1024, 128),
        "sub_k2": f(inputs["sub_k2"])[0].reshape(1024, 128),
        "w_u": f(inputs["w_u"])[0],
        "w_v": f(inputs["w_v"])[0],
        "g_f": f(inputs["g_f"]).reshape(1, D),
    }
    cosk, sink = _rope_tables(np.arange(8192))
    shared["cosk"] = cosk
    shared["sink"] = sink
    cp = f(inputs["c_prompt"])
    cs = f(inputs["c_sample"])
    cache_ckv = f(inputs["cache_ckv"])[0]
    cache_kr = f(inputs["cache_krope"])[0]
    sconv = f(inputs["state_conv"])[0]
    maps = []
    for c in range(NCORES):
        m = dict(shared)
        lt = np.arange(1024)
        pos_own = (8 * (lt // 64) + c) * 64 + lt % 64
        xo = np.zeros((NT, D), np.float32)
        xo[:1024] = xp[pos_own]
        xo[1024:1088] = xs[4 * c:4 * c + 4].reshape(64, D)
        hv = np.zeros((128, 32), np.float32)
        for j in range(16):
            for i in range(2):
                p = (8 * j + c) * 64 - 2 + i
                if p >= 0:
                    xo[1088 + 2 * j + i] = xp[p]
                    hv[:, 2 * j + i] = 1.0
        m["xown"] = xo
        m["hvalid"] = hv
        c5 = np.concatenate([cp, cs[4 * c:4 * c + 4]], axis=0)
        m["c5T"] = np.ascontiguousarray(c5.reshape(5, 16, 128).transpose(2, 1, 0))
        m["cckv"] = np.ascontiguousarray(cache_ckv[4 * c:4 * c + 4])
        m["ckr"] = np.ascontiguousarray(cache_kr[4 * c:4 * c + 4])
        sc = sconv[4 * c:4 * c + 4].reshape(4, 2, 8, 128).transpose(3, 2, 0, 1)
        m["sconvT"] = np.ascontiguousarray(sc)
        posq = np.concatenate([pos_own, np.tile(1024 + np.arange(16), 4), np.zeros(32, np.int64)])
        cq, sq = _rope_tables(posq)
        m["cosq"] = cq
        m["sinq"] = sq
        dm = np.zeros((128, 4), np.float32)
        for kt in range(4):
            for half in range(2):
                if 2 * kt + half > c:
                    dm[64 * half:64 * half + 64, kt] = NEG
        m["dmask"] = dm
        maps.append(m)
    return maps


def assemble(results):
    y_p = np.zeros((1, 8192, D), np.float32)
    y_s = np.zeros((32, 16, D), np.float32)
    ckv_p = np.zeros((1, 1, 8192, 512), np.float32)
    kr_p = np.zeros((1, 1, 8192, 64), np.float32)
    conv_p = np.zeros((1, 1, 2, 1024), np.float32)
    ckv_s = np.zeros((1, 32, 16, 512), np.float32)
    kr_s = np.zeros((1, 32, 16, 64), np.float32)
    conv_s = np.zeros((1, 32, 2, 1024), np.float32)
    for c in range(NCORES):
        r = results[c]
        lt = np.arange(1024)
        pos_own = (8 * (lt // 64) + c) * 64 + lt % 64
        y_p[0, pos_own] = r["o_y"][:1024]
        y_s[4 * c:4 * c + 4] = r["o_y"][1024:1088].reshape(4, 16, D)
        ckv_p[0, 0, pos_own] = r["o_ckv"][:1024]
        ckv_s[0, 4 * c:4 * c + 4] = r["o_ckv"][1024:1088].reshape(4, 16, 512)
        kr_p[0, 0, pos_own] = r["o_kr"][:1024]
        kr_s[0, 4 * c:4 * c + 4] = r["o_kr"][1024:1088].reshape(4, 16, 64)
        if c == 7:
            conv_p[0, 0] = r["o_conv"][0:2]
        conv_s[0, 4 * c:4 * c + 4] = r["o_conv"][2:10].reshape(4, 2, 1024)
    return (y_p, y_s, ckv_p, kr_p, conv_p, ckv_s, kr_s, conv_s)


def kernel(**inputs):
    maps = prepare(inputs)
    if "nc" not in _CACHE:
        _CACHE["nc"] = build()
    nc = _CACHE["nc"]
    maps = [{n: m[n] for n in nc._in_names} for m in maps]
    res = run_bass_kernel_spmd(nc, maps, core_ids=list(range(NCORES)))
    return assemble(res.results)
```
